# Optimizing a Trainium2 kernel written in Bass

```python
import math
import jax, jax.numpy as jnp
from jax import lax
import numpy as np

D_MODEL = 1024
BATCH = 4
SEQ = 4096
DEPTH = 4
DEC_BATCH = 128
DEC_SEQ = 4
PAST_LEN = 8192
PAGE_SIZE = 128

N_META = 16
N_A_LAYERS = DEPTH // 2
N_B_LAYERS = DEPTH - N_A_LAYERS
CONV_DIM = D_MODEL
CONV_WIDTH = 3
N_HEADS = 16
QK_NOPE = 64
QK_ROPE = 32
V_HEAD = 64
Q_LORA = D_MODEL // 2
KV_LORA = D_MODEL // 4
MLA_DIM = N_HEADS * V_HEAD
ROPE_THETA = 10000.0
RMS_EPS = 1e-6
Q_BLOCK = 128
SOFTMAX_SCALE = (QK_NOPE + QK_ROPE) ** -0.5

kernel_name = "yoco_shortconv_mla_decoder_step"


def rmsnorm(x, g):
    xf = x.astype(jnp.float32)
    y = xf * lax.rsqrt(jnp.mean(xf * xf, axis=-1, keepdims=True) + RMS_EPS)
    return (y * g.astype(jnp.float32)).astype(x.dtype)


def rope(x, pos):
    half = x.shape[-1] // 2
    inv = ROPE_THETA ** (-jnp.arange(half, dtype=jnp.float32) / half)
    ang = pos.astype(jnp.float32)[:, None] * inv[None, :]
    cos = jnp.cos(ang)[None, :, None, :]
    sin = jnp.sin(ang)[None, :, None, :]
    xf = x.astype(jnp.float32)
    x1, x2 = xf[..., :half], xf[..., half:]
    return jnp.concatenate([x1 * cos - x2 * sin, x1 * sin + x2 * cos], axis=-1).astype(x.dtype)


def conv_mixer(h, prev, w_in, conv_w, w_out):
    L = h.shape[1]
    b_gate, c_gate, u, z = jnp.split(h @ w_in, 4, axis=-1)
    v = c_gate * u
    vc = jnp.concatenate([prev.astype(v.dtype), v], axis=1)
    y = conv_w[0] * vc[:, 0:L]
    for k in range(1, CONV_WIDTH):
        y = y + conv_w[k] * vc[:, k:k + L]
    out = (b_gate * y * jax.nn.silu(z)) @ w_out
    return out, vc[:, L:]


def shared_latent(s, pos, kv_norm_g, w_dkv, kv_lat_g):
    ckr = rmsnorm(s, kv_norm_g) @ w_dkv
    c = rmsnorm(ckr[..., :KV_LORA], kv_lat_g)
    kr = rope(ckr[..., None, KV_LORA:], pos)[:, :, 0]
    return c, kr


def mla_query(h, pos, w_in, q_norm_g, w_uq):
    proj = h @ w_in
    q_lat, z = proj[..., :Q_LORA], proj[..., Q_LORA:]
    q = (rmsnorm(q_lat, q_norm_g) @ w_uq).reshape(h.shape[0], h.shape[1], N_HEADS, QK_NOPE + QK_ROPE)
    return q[..., :QK_NOPE], rope(q[..., QK_NOPE:], pos), z


def attend(q, k, v, q_pos, k_pos):
    Bq, Q = q.shape[0], q.shape[1]
    qb = min(Q_BLOCK, Q)
    nb = -(-Q // qb)
    pad = nb * qb - Q
    qp = jnp.pad(q, ((0, 0), (0, pad), (0, 0), (0, 0), (0, 0)))
    pp = jnp.pad(q_pos, (0, pad), mode='edge')
    q_blocks = jnp.swapaxes(qp.reshape((Bq, nb, qb) + q.shape[2:]), 0, 1)
    p_blocks = pp.reshape(nb, qb)
    neg = jnp.finfo(jnp.float32).min

    def one_block(args):
        qi, pi = args
        s = jnp.einsum('bqghd,bkgd->bghqk', qi, k, preferred_element_type=jnp.float32) * SOFTMAX_SCALE
        s = jnp.where(k_pos[None, :] <= pi[:, None], s, neg)
        p = jax.nn.softmax(s, axis=-1)
        return jnp.einsum('bghqk,bkgv->bqghv', p.astype(v.dtype), v)

    out = lax.map(one_block, (q_blocks, p_blocks))
    out = jnp.swapaxes(out, 0, 1).reshape((Bq, nb * qb) + out.shape[3:])
    return out[:, :Q]


def setup_inputs(seed: int = 0) -> dict:
    key = jax.random.key(seed)
    ks = jax.random.split(key, 24)
    n_pages = PAST_LEN // PAGE_SIZE
    n_used = DEC_BATCH * n_pages
    n_pool = n_used + n_used // 4
    f32 = jnp.float32
    nrm = lambda k, shape, s: jax.random.normal(k, shape, f32) * s
    gain = lambda k, shape: 1.0 + 0.05 * jax.random.normal(k, shape, f32)
    page_table = jax.random.permutation(ks[0], n_pool)[:n_used].reshape(DEC_BATCH, n_pages).astype(jnp.int32)
    return {
        "x_prompt": nrm(ks[1], (BATCH, SEQ, D_MODEL), 1.0),
        "x_sample": nrm(ks[2], (DEC_BATCH, DEC_SEQ, D_MODEL), 1.0),
        "cache_ckv": nrm(ks[3], (n_pool, PAGE_SIZE, KV_LORA), 1.0),
        "cache_krope": nrm(ks[4], (n_pool, PAGE_SIZE, QK_ROPE), 1.0),
        "state_conv": nrm(ks[5], (N_A_LAYERS, DEC_BATCH, CONV_WIDTH - 1, CONV_DIM), 1.0),
        "page_table": page_table,
        "meta_tokens": nrm(ks[6], (N_META, D_MODEL), 1.0),
        "pre_norm_g": gain(ks[7], (DEPTH, D_MODEL)),
        "post_norm_g": gain(ks[8], (DEPTH, D_MODEL)),
        "w_in_conv": nrm(ks[9], (N_A_LAYERS, D_MODEL, 4 * CONV_DIM), D_MODEL ** -0.5),
        "conv_w": nrm(ks[10], (N_A_LAYERS, CONV_WIDTH, CONV_DIM), CONV_WIDTH ** -0.5),
        "w_out_conv": nrm(ks[11], (N_A_LAYERS, CONV_DIM, D_MODEL), CONV_DIM ** -0.5),
        "kv_norm_g": gain(ks[12], (D_MODEL,)),
        "w_dkv": nrm(ks[13], (D_MODEL, KV_LORA + QK_ROPE), D_MODEL ** -0.5),
        "kv_lat_norm_g": gain(ks[14], (KV_LORA,)),
        "w_uk": nrm(ks[15], (KV_LORA, N_HEADS, QK_NOPE), KV_LORA ** -0.5),
        "w_uv": nrm(ks[16], (KV_LORA, N_HEADS, V_HEAD), KV_LORA ** -0.5),
        "w_in_mla": nrm(ks[17], (N_B_LAYERS, D_MODEL, Q_LORA + MLA_DIM), D_MODEL ** -0.5),
        "q_norm_g": gain(ks[18], (N_B_LAYERS, Q_LORA)),
        "w_uq": nrm(ks[19], (N_B_LAYERS, Q_LORA, N_HEADS * (QK_NOPE + QK_ROPE)), Q_LORA ** -0.5),
        "w_out_mla": nrm(ks[20], (N_B_LAYERS, MLA_DIM, D_MODEL), MLA_DIM ** -0.5),
    }


def reference(x_prompt, x_sample, cache_ckv, cache_krope, state_conv, page_table, meta_tokens,
              pre_norm_g, post_norm_g, w_in_conv, conv_w, w_out_conv, kv_norm_g, w_dkv,
              kv_lat_norm_g, w_uk, w_uv, w_in_mla, q_norm_g, w_uq, w_out_mla):

    def trunk(x, pos, conv_prev, past_c, past_kr, k_pos):
        absorbed = past_c is not None
        new_conv = []
        keys = vals = c_new = kr_new = None
        for l in range(DEPTH):
            h = rmsnorm(x, pre_norm_g[l])
            if l < N_A_LAYERS:
                m, st = conv_mixer(h, conv_prev[l], w_in_conv[l], conv_w[l], w_out_conv[l])
                new_conv.append(st)
            else:
                j = l - N_A_LAYERS
                q_nope, q_rope, z = mla_query(h, pos, w_in_mla[j], q_norm_g[j], w_uq[j])
                Bx, L = h.shape[0], h.shape[1]
                if absorbed:
                    q_lat = jnp.einsum('blhd,chd->blhc', q_nope, w_uk)
                    q = jnp.concatenate([q_lat, q_rope], axis=-1)[:, :, None]
                    o_lat = attend(q, keys, vals, pos, k_pos)[:, :, 0]
                    o = jnp.einsum('blhc,chv->blhv', o_lat, w_uv)
                else:
                    q = jnp.concatenate([q_nope, q_rope], axis=-1)[:, :, :, None]
                    o = attend(q, keys, vals, pos, k_pos)[:, :, :, 0]
                m = (o.reshape(Bx, L, MLA_DIM) * jax.nn.silu(z)) @ w_out_mla[j]
            x = x + rmsnorm(m, post_norm_g[l])
            if l == N_A_LAYERS - 1:
                c_new, kr_new = shared_latent(x, pos, kv_norm_g, w_dkv, kv_lat_norm_g)
                if absorbed:
                    c_all = jnp.concatenate([past_c.astype(c_new.dtype), c_new], axis=1)
                    kr_all = jnp.concatenate([past_kr.astype(kr_new.dtype), kr_new], axis=1)
                    keys = jnp.concatenate([c_all, kr_all], axis=-1)[:, :, None]
                    vals = c_all[:, :, None]
                else:
                    k_nope = jnp.einsum('btc,chd->bthd', c_new, w_uk)
                    k_rope_h = jnp.broadcast_to(kr_new[:, :, None], kr_new.shape[:2] + (N_HEADS, QK_ROPE))
                    keys = jnp.concatenate([k_nope, k_rope_h], axis=-1)
                    vals = jnp.einsum('btc,chv->bthv', c_new, w_uv)
        return x, c_new, kr_new, jnp.stack(new_conv)

    bp = x_prompt.shape[0]
    t_total = x_prompt.shape[1] + N_META
    meta = jnp.broadcast_to(meta_tokens[None].astype(x_prompt.dtype), (bp, N_META, D_MODEL))
    xp = jnp.concatenate([meta, x_prompt], axis=1)
    pos_p = jnp.arange(t_total, dtype=jnp.int32)
    conv0 = jnp.zeros((N_A_LAYERS, bp, CONV_WIDTH - 1, CONV_DIM), x_prompt.dtype)
    hp, ckv_prompt, krope_prompt, conv_prompt = trunk(xp, pos_p, conv0, None, None, pos_p)
    y_prompt = hp[:, N_META:]

    bs, ls = x_sample.shape[0], x_sample.shape[1]
    past_len = page_table.shape[1] * cache_ckv.shape[1]
    past_c = cache_ckv[page_table].reshape(bs, past_len, KV_LORA)
    past_kr = cache_krope[page_table].reshape(bs, past_len, QK_ROPE)
    pos_s = past_len + jnp.arange(ls, dtype=jnp.int32)
    k_pos_s = jnp.arange(past_len + ls, dtype=jnp.int32)
    y_sample, ckv_sample, krope_sample, conv_sample = trunk(x_sample, pos_s, state_conv, past_c, past_kr, k_pos_s)

    return (y_prompt, y_sample, ckv_prompt, krope_prompt, conv_prompt, ckv_sample, krope_sample, conv_sample)
```

```python
import contextlib
import numpy as np
import concourse.bass as bass
import concourse.mybir as mybir
from concourse.bass_utils import run_bass_kernel_spmd

F32 = mybir.dt.float32
BF16 = mybir.dt.bfloat16
I32 = mybir.dt.int32
ALU = mybir.AluOpType
AF = mybir.ActivationFunctionType
AX = mybir.AxisListType

D = 1024
KC = 8
SEQ = 4096
NMETA = 16
T = SEQ + NMETA
NCORES = 8
NB = 16
LS = 4
NS = NB * LS
EPS = 1e-6
SCALE = 96 ** -0.5
PAGE = 128
NEG = -30000.0

ENGS = ("pe", "act", "dve", "pool", "sp")
import os
XENG = os.environ.get("XENG", "act,dve").split(",")
PSUM_KEYS = ("ps", "pt", "ssq")


class Op:
    __slots__ = ("eng", "fn", "deps", "signal", "semval", "is_dma", "slot", "idx")

    def __init__(self, eng, fn):
        self.eng = eng
        self.fn = fn
        self.deps = ()
        self.signal = False
        self.semval = 0
        self.is_dma = False
        self.slot = None


class Graph:
    def __init__(self):
        self.ops = {e: [] for e in ENGS}
        self.last_w = {}
        self.readers = {}
        self.slot_count = {}
        self.n = 0
        self.barrier_ops = []
        self.pending_dma = []
        self.last_on = {}

    def add(self, eng, fn, reads=(), writes=(), slot=None):
        op = Op(eng, fn)
        op.idx = self.n
        self.n += 1
        deps = {}

        def dep(o):
            if o is not None and o is not op:
                deps[id(o)] = o

        for k in reads:
            dep(self.last_w.get(k))
            if isinstance(k, tuple) and k[0] in PSUM_KEYS:
                r = self.readers.get(k)
                if r:
                    for ek, o in r.items():
                        if ek != eng:
                            dep(o)
        for k in writes:
            dep(self.last_w.get(k))
            r = self.readers.get(k)
            if r:
                for o in r.values():
                    dep(o)
        for o in self.barrier_ops:
            dep(o)
        op.deps = list(deps.values())
        for o in op.deps:
            o.signal = True
        for k in reads:
            r = self.readers.setdefault(k, {})
            if slot is not None:
                r[("dma", op.idx)] = op
            else:
                r[eng] = op
        for k in writes:
            self.last_w[k] = op
            self.readers[k] = {}
        if slot is not None:
            op.is_dma = True
            op.slot = slot
            c = self.slot_count.get(slot, 0) + 1
            self.slot_count[slot] = c
            op.semval = 16 * c
            self.pending_dma.append(op)
        else:
            self.last_on[eng] = op
        self.ops[eng].append(op)
        return op

    def barrier(self):
        ops = list(self.last_on.values()) + self.pending_dma
        self.barrier_ops = ops
        self.pending_dma = []

    def emit(self, nc, stack, limit=None):
        if limit is not None:
            for e in ENGS:
                self.ops[e] = [o for o in self.ops[e] if o.idx < limit]
        esem = {e: stack.enter_context(nc.semaphore("s_" + e)) for e in ENGS if e != "sp"}
        ssem = {s: stack.enter_context(nc.semaphore("d_%d" % i)) for i, s in enumerate(self.slot_count)}
        for e in ENGS:
            c = 0
            for op in self.ops[e]:
                if not op.is_dma and op.signal:
                    c += 1
                    op.semval = c

        def sem_of(o):
            return ssem[o.slot] if o.is_dma else esem[o.eng]

        def run(ename, eng):
            known = {}
            for op in self.ops[ename]:
                need = {}
                for d in op.deps:
                    if ename == "pe" and d.eng == "pe" and not d.is_dma:
                        continue
                    s = sem_of(d)
                    v = d.semval
                    key = id(s)
                    if known.get(key, 0) >= v:
                        continue
                    if key not in need or need[key][1] < v:
                        need[key] = (s, v)
                for key, (s, v) in need.items():
                    eng.wait_ge(s, v)
                    known[key] = v
                inst = op.fn(eng)
                if inst is None:
                    continue
                if op.is_dma:
                    inst.then_inc(ssem[op.slot], 16)
                elif op.signal:
                    inst.then_inc(esem[ename], 1)

        block = stack.enter_context(nc.Block())

        @block.sync
        def _(e):
            run("sp", e)

        @block.scalar
        def _(e):
            run("act", e)

        @block.vector
        def _(e):
            run("dve", e)

        @block.gpsimd
        def _(e):
            run("pool", e)

        @block.tensor
        def _(e):
            run("pe", e)


class Arena:
    def __init__(self, big, nwords):
        self.big = big
        self.n = nwords
        self.top = 0

    def mark(self):
        return self.top

    def release(self, m):
        self.top = m

    def alloc(self, nelem, dtype=F32):
        words = (nelem * (2 if dtype == BF16 else 4) + 3) // 4
        words = (words + 7) // 8 * 8
        off = self.top
        self.top += words
        assert self.top <= self.n, "SBUF arena overflow: %d > %d words" % (self.top, self.n)
        ap = self.big[:, off:off + words]
        if dtype != F32:
            ap = ap.bitcast(dtype)
        return ap[:, 0:nelem]


class Ring:
    def __init__(self, name, aps):
        self.name = name
        self.aps = aps
        self.i = 0

    def next(self):
        i = self.i % len(self.aps)
        self.i += 1
        return (self.name, i), self.aps[i]


def chunks(n, c=512):
    return [(i, min(c, n - i)) for i in range(0, n, c)]


class Cfg:
    def __init__(self, n_pool=10240, n_pages=64, do_b=True, do_s=True):
        self.n_pool = n_pool
        self.n_pages = n_pages
        self.do_b = do_b
        self.do_s = do_s


V_PRE = 0
V_POST = 32
V_KVN = 64
V_LAT = 72
V_QN = 74
V_CW = 82
V_BLEND = 130
V_R16 = 138
NV = 139


def build(cfg):
    nc = bass.Bass("TRN2", target_bir_lowering=False)
    G = Graph()
    stack = contextlib.ExitStack()

    regcache = {}

    def I(eng, method, reads, writes, *args, slot=None, **kw):
        def fn(e):
            try:
                if "bounds_check" in kw and not isinstance(kw["bounds_check"], (type(None),)) and isinstance(kw["bounds_check"], int):
                    if "bcreg" not in regcache:
                        regcache["bcreg"] = e.to_reg(kw["bounds_check"])
                    kw2 = dict(kw)
                    kw2["bounds_check"] = regcache["bcreg"]
                    return getattr(e, method)(*args, **kw2)
                return getattr(e, method)(*args, **kw)
            except Exception:
                print("FAILED OP", eng, method, writes, [str(a)[:200] for a in args], {k: str(v)[:200] for k, v in kw.items()})
                raise
        return G.add(eng, fn, reads=reads, writes=writes, slot=slot)

    def din(name, shape, dt=F32):
        return nc.dram_tensor(name, list(shape), dt, kind="ExternalInput").ap()

    def dout(name, shape, dt=F32):
        return nc.dram_tensor(name, list(shape), dt, kind="ExternalOutput").ap()

    NPG = cfg.n_pages
    NGRP = NPG // 8
    xp = din("xp", [SEQ, D])
    meta = din("meta", [NMETA, D])
    xs = din("xs", [NS, D])
    sconv = din("sconv", [64, D])
    vecs = din("vecs", [128, NV])
    ident_d = din("ident", [128, 128])
    wic = din("wic", [2 * 32 * 128, 1024])
    woc = din("woc", [2 * 8 * 128, 1024])
    wdkv = din("wdkv", [3 * 128, 1024])
    wim = din("wim", [2 * 12 * 128, 1024])
    wuq = din("wuq", [2 * 16 * 128, 512])
    wom = din("wom", [2 * 8 * 128, 1024])
    wuk = din("wuk", [128, 2048])
    wuv = din("wuv", [128, 2048])
    wukT = din("wukT", [64, 16 * 256])
    ktab = din("ktab", [128, T])
    mtab = din("mtab", [128, 80])
    qtab = din("qtab", [128, 2048])
    masks = din("masks", [2 * 128, 4096])
    ptx = din("ptx", [128, NB * NGRP], I32)
    smask = din("smask", [64, NB * 64])
    cckv = din("cckv", [cfg.n_pool * 16, 2048])
    ckr = din("ckr", [cfg.n_pool * 16, 256])

    y_own = dout("y_own", [2048, D])
    ys = dout("ys", [NS, D])
    ckv_p = dout("ckv_p", [T, 256])
    kr_p = dout("kr_p", [T, 32])
    conv_p = dout("conv_p", [4, D])
    ckv_s = dout("ckv_s", [NS, 256])
    kr_s = dout("kr_s", [NS, 32])
    conv_s = dout("conv_s", [64, D])

    NW = 53000
    big = stack.enter_context(nc.sbuf_tensor("arena", [128, NW], F32))
    A = Arena(big, NW)
    banks = [stack.enter_context(nc.psum_tensor("bank%d" % i, [128, 512], F32)) for i in range(8)]

    def v3(ap, a):
        return ap.rearrange("p (a b) -> p a b", a=a)

    def bt(ap, t):
        return ap.rearrange("p (b t) -> p b t", t=t)

    ident = A.alloc(128)
    identb = A.alloc(128, BF16)
    onesb = A.alloc(128, BF16)
    neghalf = A.alloc(512)
    vec = A.alloc(NV)
    I("sp", "dma_start", [], ["ident"], out=ident, in_=ident_d, slot="c_ident")
    I("sp", "dma_start", [], ["vec"], out=vec, in_=vecs, slot="c_vec")
    I("pool", "dma_start", [], ["identb"], out=identb, in_=ident_d, slot="c_identb")
    I("dve", "memset", [], ["onesb"], onesb, 1.0)
    I("dve", "memset", [], ["neghalf"], neghalf, -0.5)

    def vcol(off):
        return vec[:, off:off + 1]

    xsT = v3(A.alloc(8 * NS), 8)
    cTs = v3(A.alloc(2 * NS, BF16), 2)
    krTs = A.alloc(NS, BF16)
    cns_tok = A.alloc(256, BF16)
    mSP = A.mark()
    x_own = v3(A.alloc(8 * 2048), 8)
    cT = v3(A.alloc(2 * T, BF16), 2)
    Kbuf = [A.alloc(T, BF16) for _ in range(2)]
    halo = A.alloc(2 * 8 * 2)
    cvp = A.alloc(8 * 4)
    cvs = A.alloc(8 * 64)
    stT = A.alloc(8 * 64)

    for i in range(2):
        I("pool", "memset", [], [("Kb", i, "kr")], Kbuf[i], 0.0)

    psring = Ring("ps", [banks[i] for i in range(4)])
    ssq_banks = [banks[4], banks[5]]
    trring = Ring("pt", [banks[6], banks[7]])

    def copy_any(eng, reads, writes, out, in_):
        if eng == "act":
            I("act", "activation", reads, writes, out=out, in_=in_, func=AF.Copy)
        else:
            I(eng, "tensor_copy", reads, writes, out=out, in_=in_)

    def transposes_to(dst_fn, src, src_key, nrows, ncols, idn, eng_alt):
        nblk = (ncols + 127) // 128
        per = max(1, 512 // nrows)
        for g0 in range(0, nblk, per):
            pk, pb = trring.next()
            nb = min(per, nblk - g0)
            for i in range(nb):
                cb = g0 + i
                cw = min(128, ncols - cb * 128)
                I("pe", "transpose", [src_key, "ident"], [pk],
                  pb[:cw, i * nrows:(i + 1) * nrows], src[:nrows, cb * 128:cb * 128 + cw], idn[:nrows, :nrows])
            for i in range(nb):
                cb = g0 + i
                cw = min(128, ncols - cb * 128)
                dst, dkey = dst_fn(cb)
                copy_any(eng_alt[cb % len(eng_alt)], [pk], [dkey], dst, pb[:cw, i * nrows:(i + 1) * nrows])

    mA = A.mark()
    wring = Ring("w", [v3(A.alloc(1024, BF16), 8) for _ in range(4)])
    sqr = Ring("sq", [A.alloc(512, BF16) for _ in range(2)])
    rsr = Ring("rs", [A.alloc(512) for _ in range(2)])

    def load_w(dram, row0):
        k, w = wring.next()
        I("pool", "dma_start", [], [k], out=w.rearrange("p a b -> p (a b)"), in_=dram[row0:row0 + 128, :], slot="w%d" % k[1])
        return k, w

    def rstd_from(sb, skey, n, nfeat):
        rk, r = rsr.next()
        I("dve", "tensor_scalar", [skey], [rk], out=r[:, :n], in0=sb[:, :n], scalar1=1.0 / nfeat, scalar2=EPS, op0=ALU.mult, op1=ALU.add)
        I("pool", "tensor_tensor", [rk, "neghalf"], [rk], out=r[:, :n], in0=r[:, :n], in1=neghalf[:, :n], op=ALU.pow)
        return rk, r

    def rms_stats(srcT, key_fn, nchunks, tcs, nfeat):
        res = []
        for ti, (c0, n) in enumerate(tcs):
            sb = ssq_banks[ti % 2]
            sk = ("ssq", ti % 2)
            for kc in range(nchunks):
                qk, q = sqr.next()
                I("act", "activation", [key_fn(kc)], [qk], out=q[:, :n], in_=srcT[:, kc, c0:c0 + n], func=AF.Square)
                I("pe", "matmul", [qk, "onesb"], [sk], sb[:, :n], onesb, q[:, :n], start=(kc == 0), stop=(kc == nchunks - 1))
            res.append(rstd_from(sb, sk, n, nfeat))
        return res

    def norm_to(dstT, dkey_fn, srcT, skey_fn, nchunks, tcs, rst, goff):
        for ti, (c0, n) in enumerate(tcs):
            rk, r = rst[ti]
            for kc in range(nchunks):
                I("dve", "scalar_tensor_tensor", [skey_fn(kc), rk, "vec"], [dkey_fn(kc)],
                  out=dstT[:, kc, c0:c0 + n], in0=srcT[:, kc, c0:c0 + n], scalar=vcol(goff + kc), in1=r[:, :n], op0=ALU.mult, op1=ALU.mult)

    def proj(wk, w, nk, srcT, skey_fn, c0, n, ring=None):
        pk, pb = (ring or psring).next()
        for kc in range(nk):
            I("pe", "matmul", [wk, skey_fn(kc)], [pk], pb[:, :n], w[:, kc, :], srcT[:, kc, c0:c0 + n], start=(kc == 0), stop=(kc == nk - 1))
        return pk, pb

    def out_proj_residual(wdram, wrow0, srcT, skey_fn, xT, xkey_fn, tcs, NT, goff):
        mT = v3(A.alloc(8 * NT), 8)
        assert len(tcs) <= 2
        for of in range(8):
            wk, w = load_w(wdram, wrow0 + of * 128)
            for ti, (c0, n) in enumerate(tcs):
                pk, pb = proj(wk, w, 8, srcT, skey_fn, c0, n)
                I("act", "activation", [pk], [("mT", of, ti)], out=mT[:, of, c0:c0 + n], in_=pb[:, :n], func=AF.Copy)
                qk, q = sqr.next()
                I("act", "activation", [pk], [qk], out=q[:, :n], in_=pb[:, :n], func=AF.Square)
                I("pe", "matmul", [qk, "onesb"], [("ssq", ti % 2)], ssq_banks[ti % 2][:, :n], onesb, q[:, :n], start=(of == 0), stop=(of == 7))
        for ti, (c0, n) in enumerate(tcs):
            rk, r = rstd_from(ssq_banks[ti % 2], ("ssq", ti % 2), n, D)
            for of in range(8):
                I("dve", "scalar_tensor_tensor", [("mT", of, ti), rk, "vec"], [("mT", of, ti)],
                  out=mT[:, of, c0:c0 + n], in0=mT[:, of, c0:c0 + n], scalar=vcol(goff + of), in1=r[:, :n], op0=ALU.mult, op1=ALU.mult)
                I("pool", "tensor_tensor", [("mT", of, ti), xkey_fn(of)], [xkey_fn(of)],
                  out=xT[:, of, c0:c0 + n], in0=xT[:, of, c0:c0 + n], in1=mT[:, of, c0:c0 + n], op=ALU.add)

    def run_group(gi, NT, segs, x_loads, tabs, dests, own_slot):
        tcs = chunks(NT)
        mG = A.mark()
        xT = v3(A.alloc(8 * NT), 8)
        gT = v3(A.alloc(8 * NT, BF16), 8)
        xkey = lambda kc: ("xT", kc)
        mS = A.mark()
        xr = Ring("xin", [A.alloc(1024) for _ in range(2)])
        for (src, ntok, col0) in x_loads:
            xk, xb = xr.next()
            I("sp", "dma_start", [], [xk], out=xb[:ntok, :], in_=src, slot="xin%d" % xk[1])
            transposes_to(lambda cb, col0=col0, ntok=ntok: (xT[:, cb, col0:col0 + ntok], ("xT", cb)), xb, xk, ntok, 1024, ident, XENG)
        G.barrier()
        A.release(mS)
        for l in range(2):
            mL = A.mark()
            hT = v3(A.alloc(8 * NT, BF16), 8)
            hkey = lambda kc: ("hT", kc)
            cbuf = A.alloc(NT)
            ybuf = A.alloc(NT)
            tbuf = A.alloc(NT)
            szb = A.alloc(NT)
            vexts = [A.alloc(sg["nseq"] * (sg["L"] + 2)) for sg in segs]
            rst = rms_stats(xT, xkey, 8, tcs, D)
            norm_to(hT, hkey, xT, xkey, 8, tcs, rst, V_PRE + l * 8)
            for j in range(8):
                hoff = (l * 8 + j) * 2
                cwb = V_CW + l * 24 + j
                for si, sg in enumerate(segs):
                    ve = vexts[si]
                    L = sg["L"]
                    if sg["kind"] == "meta":
                        I("dve", "memset", [], [("vext", si)], ve[:, 0:2], 0.0)
                    elif sg["kind"] == "prompt":
                        I("dve", "tensor_copy", [("halo", l, j)], [("vext", si)], out=ve[:, 0:2], in_=halo[:, hoff:hoff + 2])
                    else:
                        I("dve", "tensor_copy", ["stT"], [("vext", si)], out=bt(ve, L + 2)[:, :, 0:2],
                          in_=stT[:, j * 64 + l * 32:j * 64 + l * 32 + 32].rearrange("p (b k) -> p b k", k=2))
                for part, pname in ((1, "c"), (2, "u"), (0, "b"), (3, "z")):
                    if pname == "b":
                        for si, sg in enumerate(segs):
                            ve = vexts[si]
                            L = sg["L"]
                            s0 = sg["col0"]
                            if sg["nseq"] == 1:
                                src = [ve[:, k:k + L] for k in range(3)]
                                yv = ybuf[:, s0:s0 + L]
                                tail = ve[:, L:L + 2]
                            else:
                                ve3 = bt(ve, L + 2)
                                src = [ve3[:, :, k:k + L] for k in range(3)]
                                yv = bt(ybuf[:, s0:s0 + sg["nseq"] * L], L)
                                tail = ve3[:, :, L:L + 2]
                            vk, yk = ("vext", si), ("ybuf", si)
                            I("pool", "tensor_scalar", [vk, "vec"], [yk], out=yv, in0=src[0], scalar1=vcol(cwb), scalar2=None, op0=ALU.mult)
                            I("dve", "scalar_tensor_tensor", [vk, "vec", yk], [yk], out=yv, in0=src[1], scalar=vcol(cwb + 8), in1=yv, op0=ALU.mult, op1=ALU.add)
                            I("dve", "scalar_tensor_tensor", [vk, "vec", yk], [yk], out=yv, in0=src[2], scalar=vcol(cwb + 16), in1=yv, op0=ALU.mult, op1=ALU.add)
                            if sg["kind"] in ("meta", "prompt"):
                                I("act", "activation", [vk], [("halo", l, j)], out=halo[:, hoff:hoff + 2], in_=tail, func=AF.Copy)
                                if sg.get("last"):
                                    I("act", "activation", [vk], ["cvp"], out=cvp[:, j * 4 + l * 2:j * 4 + l * 2 + 2], in_=tail, func=AF.Copy)
                            else:
                                I("act", "activation", [vk], ["cvs"], out=cvs[:, j * 64 + l * 32:j * 64 + l * 32 + 32].rearrange("p (b k) -> p b k", k=2),
                                  in_=tail, func=AF.Copy)
                    wk, w = load_w(wic, (l * 32 + part * 8 + j) * 128)
                    for ti, (c0, n) in enumerate(tcs):
                        pk, pb = proj(wk, w, 8, hT, hkey, c0, n)
                        if pname == "c":
                            I("act", "activation", [pk], [("cbuf", ti)], out=cbuf[:, c0:c0 + n], in_=pb[:, :n], func=AF.Copy)
                        elif pname == "u":
                            for si, sg in enumerate(segs):
                                L = sg["L"]
                                s0 = sg["col0"]
                                lo = max(c0, s0)
                                hi = min(c0 + n, s0 + sg["nseq"] * L)
                                if hi <= lo:
                                    continue
                                ve = vexts[si]
                                if sg["nseq"] == 1:
                                    o = ve[:, 2 + lo - s0:2 + hi - s0]
                                    a = pb[:, lo - c0:hi - c0]
                                    b_ = cbuf[:, lo:hi]
                                else:
                                    assert lo == s0 and hi == s0 + sg["nseq"] * L
                                    o = bt(ve, L + 2)[:, :, 2:2 + L]
                                    a = bt(pb[:, lo - c0:hi - c0], L)
                                    b_ = bt(cbuf[:, lo:hi], L)
                                I("dve", "tensor_tensor", [pk, ("cbuf", ti)], [("vext", si)], out=o, in0=a, in1=b_, op=ALU.mult)
                        elif pname == "b":
                            I("dve", "tensor_tensor", [pk] + [("ybuf", si) for si in range(len(segs))], [("tbuf", ti)],
                              out=tbuf[:, c0:c0 + n], in0=pb[:, :n], in1=ybuf[:, c0:c0 + n], op=ALU.mult)
                        else:
                            I("act", "activation", [pk], [("szb", ti)], out=szb[:, c0:c0 + n], in_=pb[:, :n], func=AF.Silu)
                            I("pool", "tensor_tensor", [("tbuf", ti), ("szb", ti)], [("gT", j)],
                              out=gT[:, j, c0:c0 + n], in0=tbuf[:, c0:c0 + n], in1=szb[:, c0:c0 + n], op=ALU.mult)
            G.barrier()
            A.release(mL)
            out_proj_residual(woc, l * 8 * 128, gT, lambda kc: ("gT", kc), xT, xkey, tcs, NT, V_POST + l * 8)
            G.barrier()
            A.release(mL)
        mL = A.mark()
        hT = gT
        hkey = lambda kc: ("gT", kc)
        craw = v3(A.alloc(2 * NT), 2)
        tabb = A.alloc(NT)
        krf = A.alloc(NT)
        krt = A.alloc(NT)
        ctr = Ring("ctok", [A.alloc(256) for _ in range(2)])
        ktr = Ring("ktok", [A.alloc(32) for _ in range(2)])
        for (tsrc, c0, n) in tabs:
            I("sp", "dma_start", [], ["tabb"], out=tabb[:, c0:c0 + n], in_=tsrc, slot="tabb")
        rst = rms_stats(xT, xkey, 8, tcs, D)
        norm_to(hT, hkey, xT, xkey, 8, tcs, rst, V_KVN)
        for ocl in range(3):
            wk, w = load_w(wdkv, ocl * 128)
            for ti, (c0, n) in enumerate(tcs):
                pk, pb = proj(wk, w, 8, hT, hkey, c0, n)
                if ocl < 2:
                    I("act", "activation", [pk], [("craw", ocl, ti)], out=craw[:, ocl, c0:c0 + n], in_=pb[:, :n], func=AF.Copy)
                    qk, q = sqr.next()
                    I("act", "activation", [pk], [qk], out=q[:, :n], in_=pb[:, :n], func=AF.Square)
                    I("pe", "matmul", [qk, "onesb"], [("ssq", ti % 2)], ssq_banks[ti % 2][:, :n], onesb, q[:, :n], start=(ocl == 0), stop=(ocl == 1))
                else:
                    I("dve", "tensor_tensor", [pk, "tabb"], [("krf", ti)], out=krf[64:96, c0:c0 + n], in0=pb[64:96, :n], in1=tabb[64:96, c0:c0 + n], op=ALU.mult)
                    I("dve", "tensor_tensor", [pk, "tabb"], [("krt", ti)], out=krt[64:96, c0:c0 + n], in0=pb[96:128, :n], in1=tabb[96:128, c0:c0 + n], op=ALU.mult)
                    I("pool", "tensor_tensor", [("krf", ti), ("krt", ti)], [("krf", ti)], out=krf[64:96, c0:c0 + n], in0=krf[64:96, c0:c0 + n], in1=krt[64:96, c0:c0 + n], op=ALU.add)
        for ti, (c0, n) in enumerate(tcs):
            rk, r = rstd_from(ssq_banks[ti % 2], ("ssq", ti % 2), n, 256)
            for ocl in range(2):
                I("dve", "scalar_tensor_tensor", [("craw", ocl, ti), rk, "vec"], [("craw", ocl, ti)],
                  out=craw[:, ocl, c0:c0 + n], in0=craw[:, ocl, c0:c0 + n], scalar=vcol(V_LAT + ocl), in1=r[:, :n], op0=ALU.mult, op1=ALU.mult)
        for dd in dests:
            c0, n = dd["c0"], dd["n"]
            tis = sorted(set(ti for ti, (a, m) in enumerate(tcs) if a < c0 + n and a + m > c0))
            rkeys = [("craw", ocl, ti) for ocl in range(2) for ti in tis]
            kkeys = [("krf", ti) for ti in tis]
            for ocl in range(2):
                I("act", "activation", rkeys, [dd["cT_key"]], out=dd["cT"][:, ocl, :], in_=craw[:, ocl, c0:c0 + n], func=AF.Copy)
            for (kap, kkey) in dd["krT"]:
                I("act", "activation", kkeys, [kkey], out=kap, in_=krf[64:96, c0:c0 + n], func=AF.Copy)
            for t0 in range(0, n, 128):
                tw = min(128, n - t0)
                ck, cb_ = ctr.next()
                kk, kb_ = ktr.next()
                pk, pb = trring.next()
                for ocl in range(2):
                    I("pe", "transpose", rkeys + ["ident"], [pk], pb[:tw, ocl * 128:(ocl + 1) * 128], craw[:, ocl, c0 + t0:c0 + t0 + tw], ident)
                I("pe", "transpose", kkeys + ["ident"], [pk], pb[:tw, 256:288], krf[64:96, c0 + t0:c0 + t0 + tw], ident[64:96, 64:96])
                I("act", "activation", [pk], [ck], out=cb_[:tw, :], in_=pb[:tw, 0:256], func=AF.Copy)
                I("dve", "tensor_copy", [pk], [kk], out=kb_[:tw, :], in_=pb[:tw, 256:288])
                r0 = dd["row0"] + t0
                I("sp", "dma_start", [ck], [("out", dd["name"], "c", r0)], out=dd["ckv_out"][r0:r0 + tw, :], in_=cb_[:tw, :], slot="octok%d" % ck[1])
                I("sp", "dma_start", [kk], [("out", dd["name"], "k", r0)], out=dd["kr_out"][r0:r0 + tw, :], in_=kb_[:tw, :], slot="oktok%d" % kk[1])
                if dd.get("tok_bf") is not None:
                    I("dve", "tensor_copy", [pk], ["cns_tok"], out=dd["tok_bf"][:tw, :], in_=pb[:tw, 0:256])
        if own_slot is not None:
            s = own_slot
            for kc in range(8):
                xo = x_own[:, kc, s * 512:(s + 1) * 512]
                I("dve", "tensor_scalar", [("xT", kc), "vec"], [("x_own", s, kc)], out=xo, in0=xT[:, kc, 0:512], scalar1=vcol(V_BLEND + 2 * s), scalar2=None, op0=ALU.mult)
                I("dve", "scalar_tensor_tensor", [("xT", kc), "vec", ("x_own", s, kc)], [("x_own", s, kc)],
                  out=xo, in0=xT[:, kc, 512:1024], scalar=vcol(V_BLEND + 2 * s + 1), in1=xo, op0=ALU.mult, op1=ALU.add)
        else:
            for kc in range(8):
                I("act", "activation", [("xT", kc)], [("xs", kc)], out=xsT[:, kc, :], in_=xT[:, kc, 16:80], func=AF.Copy)
        G.barrier()
        A.release(mG)

    mS0 = A.mark()
    scb = A.alloc(1024)
    I("sp", "dma_start", [], ["scb"], out=scb[:64, :], in_=sconv, slot="scb")
    transposes_to(lambda cb: (stT[:, cb * 64:(cb + 1) * 64], "stT"), scb, "scb", 64, 1024, ident, ["act", "dve"])
    G.barrier()
    A.release(mS0)

    run_group(
        0, 80,
        [dict(col0=0, nseq=1, L=16, kind="meta"), dict(col0=16, nseq=NB, L=LS, kind="sample")],
        [(meta, 16, 0), (xs, 64, 16)],
        [(mtab, 0, 80)],
        [dict(c0=0, n=16, cT=cT[:, :, SEQ:SEQ + 16], cT_key=("cT", "m"), krT=[(Kbuf[i][64:96, SEQ:SEQ + 16], ("Kb", i, "kr")) for i in range(2)],
              ckv_out=ckv_p, kr_out=kr_p, row0=0, name="p"),
         dict(c0=16, n=64, cT=cTs, cT_key="cTs", krT=[(krTs[0:32, :], "krTs")],
              ckv_out=ckv_s, kr_out=kr_s, row0=0, name="s", tok_bf=cns_tok)],
        None)
    for g in range(4):
        run_group(
            1 + g, 1024,
            [dict(col0=0, nseq=1, L=1024, kind="prompt", last=(g == 3))],
            [(xp[g * 1024 + i * 128:g * 1024 + (i + 1) * 128, :], 128, i * 128) for i in range(8)],
            [(ktab[:, g * 1024:(g + 1) * 1024], 0, 1024)],
            [dict(c0=0, n=1024, cT=cT[:, :, g * 1024:(g + 1) * 1024], cT_key=("cT", g), krT=[(Kbuf[i][64:96, g * 1024:(g + 1) * 1024], ("Kb", i, "kr")) for i in range(2)],
                  ckv_out=ckv_p, kr_out=kr_p, row0=16 + g * 1024, name="p")],
            g)

    mO = A.mark()
    cvo = A.alloc(1024)
    cso = A.alloc(1024)
    for j in range(8):
        pk, pb = trring.next()
        I("pe", "transpose", ["cvp", "ident"], [pk], pb[:4, 0:128], cvp[:, j * 4:(j + 1) * 4], ident)
        I("pe", "transpose", ["cvs", "ident"], [pk], pb[:64, 128:256], cvs[:, j * 64:(j + 1) * 64], ident)
        I("dve", "tensor_copy", [pk], ["cvo"], out=cvo[:4, j * 128:(j + 1) * 128], in_=pb[:4, 0:128])
        I("act", "activation", [pk], ["cso"], out=cso[:64, j * 128:(j + 1) * 128], in_=pb[:64, 128:256], func=AF.Copy)
    I("sp", "dma_start", ["cvo"], [("out", "conv_p")], out=conv_p, in_=cvo[:4, :], slot="o_cvo")
    I("sp", "dma_start", ["cso"], [("out", "conv_s")], out=conv_s, in_=cso[:64, :], slot="o_cso")
    G.barrier()
    A.release(mO)
    A.release(mA)


    mB = A.mark()
    wring = Ring("w", [v3(A.alloc(1024, BF16), 8) for _ in range(4)])
    sqr = Ring("sq", [A.alloc(512, BF16) for _ in range(2)])
    rsr = Ring("rs", [A.alloc(512) for _ in range(2)])
    Vh = v3(A.alloc(33 * 128, BF16), 33)
    maskb = [A.alloc(4096, BF16) for _ in range(2)]
    kmax2 = A.alloc(16)
    half = A.alloc(1)
    wqr = Ring("wq", [v3(A.alloc(512, BF16), 4) for _ in range(2)])
    wkr = Ring("wkh", [v3(A.alloc(128, BF16), 2) for _ in range(2)])
    wvr = Ring("wvh", [v3(A.alloc(128, BF16), 2) for _ in range(2)])
    poring = Ring("ssq", [banks[4], banks[5]])
    I("dve", "memset", [], ["Vh"], Vh.rearrange("p a b -> p (a b)"), 1.0)
    I("dve", "memset", [], ["half"], half, 0.5)
    for par in range(2):
        I("pool", "dma_start", [], ["maskb"], out=maskb[par], in_=masks[par * 128:(par + 1) * 128, :], slot="maskb%d" % par)
    wuk3 = wuk.rearrange("p (kc f) -> p kc f", kc=2)
    wuv3 = wuv.rearrange("p (kc f) -> p kc f", kc=2)
    print("arena words used before phase-B halves:", A.top, "of", A.n)

    def mla_prompt(j, hf):
        NT = 1024
        tcs = chunks(NT)
        xT = x_own[:, :, hf * 1024:(hf + 1) * 1024]
        xkey = lambda kc: ("xo", hf, kc)
        first = (j == 0 and hf == 0)
        mH = A.mark()
        ogT = v3(A.alloc(8 * NT, BF16), 8)
        ogkey = lambda kc: ("og", kc)
        qln = v3(A.alloc(4 * NT, BF16), 4)
        qlnkey = lambda kc: ("qln", kc)
        mX = A.mark()
        hT = v3(A.alloc(8 * NT, BF16), 8)
        hkey = lambda kc: ("hT", kc)
        qraw = v3(A.alloc(4 * NT), 4)
        rst = rms_stats(xT, xkey, 8, tcs, D)
        norm_to(hT, hkey, xT, xkey, 8, tcs, rst, V_PRE + (2 + j) * 8)
        for oc in range(12):
            wk, w = load_w(wim, (j * 12 + oc) * 128)
            for ti, (c0, n) in enumerate(tcs):
                pk, pb = proj(wk, w, 8, hT, hkey, c0, n)
                if oc < 4:
                    I("act", "activation", [pk], [("qraw", oc, ti)], out=qraw[:, oc, c0:c0 + n], in_=pb[:, :n], func=AF.Copy)
                    qk, q = sqr.next()
                    I("act", "activation", [pk], [qk], out=q[:, :n], in_=pb[:, :n], func=AF.Square)
                    I("pe", "matmul", [qk, "onesb"], [("ssq", ti % 2)], ssq_banks[ti % 2][:, :n], onesb, q[:, :n], start=(oc == 0), stop=(oc == 3))
                else:
                    I("act", "activation", [pk], [("og", oc - 4)], out=ogT[:, oc - 4, c0:c0 + n], in_=pb[:, :n], func=AF.Silu)
            if oc == 3:
                for ti, (c0, n) in enumerate(tcs):
                    rk, r = rstd_from(ssq_banks[ti % 2], ("ssq", ti % 2), n, 512)
                    for kc in range(4):
                        I("dve", "scalar_tensor_tensor", [("qraw", kc, ti), rk, "vec"], [("qln", kc)],
                          out=qln[:, kc, c0:c0 + n], in0=qraw[:, kc, c0:c0 + n], scalar=vcol(V_QN + j * 4 + kc), in1=r[:, :n], op0=ALU.mult, op1=ALU.mult)
        G.barrier()
        A.release(mX)
        Qr = Ring("Qh", [A.alloc(NT, BF16) for _ in range(2)])
        Ptr = Ring("Pt", [A.alloc(512, BF16) for _ in range(3)])
        qtabb = A.alloc(NT)
        rsum = Ring("rsum", [A.alloc(512) for _ in range(2)])
        tmpf = Ring("tmpf", [A.alloc(512) for _ in range(2)])
        t1r = Ring("t1", [A.alloc(512) for _ in range(2)])
        t2r = Ring("t2", [A.alloc(512) for _ in range(2)])
        negr = Ring("negm", [A.alloc(1) for _ in range(2)])
        qmx = A.alloc(4)
        kmx = A.alloc(16)
        I("sp", "dma_start", [], ["qtabb"], out=qtabb, in_=qtab[:, hf * 1024:(hf + 1) * 1024], slot="qtabb")
        for (qk_, qb_) in [Qr.next(), Qr.next()]:
            I("pool", "memset", [], [qk_], qb_, 0.0)
        for h in range(16):
            hb = (h % 2) * 64
            pr = h // 2
            Kk = ("Kb", h % 2)
            Kh = Kbuf[h % 2]
            wqk, wq = wqr.next()
            I("pool", "dma_start", [], [wqk], out=wq.rearrange("p a b -> p (a b)"), in_=wuq[(j * 16 + h) * 128:(j * 16 + h + 1) * 128, :], slot="wq%d" % wqk[1])
            wkk, wkh = wkr.next()
            I("pool", "dma_start", [], [wkk], out=wkh, in_=wuk3[:, :, h * 64:(h + 1) * 64], slot="wkh%d" % wkk[1])
            wvk, wvh = wvr.next()
            I("pool", "dma_start", [], [wvk], out=wvh, in_=wuv3[:, :, h * 64:(h + 1) * 64], slot="wvh%d" % wvk[1])
            Qk, Qh = Qr.next()
            for ti, (c0, n) in enumerate(tcs):
                pk, pb = proj(wqk, wq, 4, qln, qlnkey, c0, n, ring=trring)
                I("act", "activation", [pk], [Qk], out=Qh[0:64, c0:c0 + n], in_=pb[0:64, :n], func=AF.Copy)
                k1, t1 = t1r.next()
                k2, t2 = t2r.next()
                I("dve", "tensor_tensor", [pk, "qtabb"], [k1], out=t1[64:96, :n], in0=pb[64:96, :n], in1=qtabb[64:96, c0:c0 + n], op=ALU.mult)
                I("dve", "tensor_tensor", [pk, "qtabb"], [k2], out=t2[64:96, :n], in0=pb[96:128, :n], in1=qtabb[96:128, c0:c0 + n], op=ALU.mult)
                I("pool", "tensor_tensor", [k1, k2], [Qk], out=Qh[64:96, c0:c0 + n], in0=t1[64:96, :n], in1=t2[64:96, :n], op=ALU.add)
            for ci, (c0, n) in enumerate(chunks(T)):
                pk, pb = trring.next()
                for kc in range(2):
                    I("pe", "matmul", [wkk, "cT"], [pk], pb[0:64, :n], wkh[:, kc, :], cT[:, kc, c0:c0 + n], start=(kc == 0), stop=(kc == 1))
                I("dve", "tensor_copy", [pk], [Kk], out=Kh[0:64, c0:c0 + n], in_=pb[0:64, :n])
            for g0 in range(0, 33, 8):
                pk, pb = trring.next()
                nb = min(8, 33 - g0)
                for i in range(nb):
                    kb = g0 + i
                    kk = 128 if kb < 32 else 16
                    for kc in range(2):
                        I("pe", "matmul", [wvk, "cT"], [pk], pb[:kk, i * 64:(i + 1) * 64], cT[:, kc, kb * 128:kb * 128 + kk], wvh[:, kc, :], start=(kc == 0), stop=(kc == 1))
                if nb == 8:
                    I("dve", "tensor_copy", [pk], ["Vh"], out=Vh[:, g0:g0 + 8, 0:64], in_=pb[:, :512].rearrange("p (a b) -> p a b", a=8))
                else:
                    I("dve", "tensor_copy", [pk], ["Vh"], out=Vh[:16, 32, 0:64], in_=pb[:16, 0:64])
            if first:
                for ci, (c0, n) in enumerate(chunks(T)):
                    qk, q = sqr.next()
                    I("act", "activation", [Kk, ("Kb", h % 2, "kr")], [qk], out=q[0:96, :n], in_=Kh[0:96, c0:c0 + n], func=AF.Square)
                    pk, pb = trring.next()
                    I("pe", "matmul", [qk, "onesb"], [pk], pb[:, :n], onesb[0:96, :], q[0:96, :n], start=True, stop=True)
                    I("dve", "tensor_reduce", [pk], [("kmx", ci)], out=kmx[:, ci:ci + 1], in_=pb[:, :n], axis=AX.X, op=ALU.max)
                I("dve", "tensor_reduce", [("kmx", ci) for ci in range(9)], ["kmax2"], out=kmax2[:, h:h + 1], in_=kmx[:, 0:9], axis=AX.X, op=ALU.max)
            for ti, (c0, n) in enumerate(tcs):
                qk, q = sqr.next()
                I("act", "activation", [Qk], [qk], out=q[0:96, :n], in_=Qh[0:96, c0:c0 + n], func=AF.Square)
                pk, pb = trring.next()
                I("pe", "matmul", [qk, "onesb"], [pk], pb[:, :n], onesb[0:96, :], q[0:96, :n], start=True, stop=True)
                I("dve", "tensor_reduce", [pk], [("qmx", ti)], out=qmx[:, ti:ti + 1], in_=pb[:, :n], axis=AX.X, op=ALU.max)
            nk, negm = negr.next()
            I("dve", "tensor_tensor", [("qmx", 0), ("qmx", 1)], [nk], out=negm, in0=qmx[:, 0:1], in1=qmx[:, 1:2], op=ALU.max)
            I("dve", "tensor_tensor", [nk, "kmax2"], [nk], out=negm, in0=negm, in1=kmax2[:, h:h + 1], op=ALU.mult)
            I("pool", "tensor_tensor", [nk, "half"], [nk], out=negm, in0=negm, in1=half, op=ALU.pow)
            I("dve", "tensor_scalar", [nk], [nk], out=negm, in0=negm, scalar1=-SCALE * 1.02, scalar2=None, op0=ALU.mult)
            for sl in range(2):
                sg = 2 * hf + sl
                nkb = 8 * sg + 8
                q0 = sl * 512
                pok, po = poring.next()
                seq = list(range(nkb)) + [32]
                for idx, kb in enumerate(seq):
                    kk = 128 if kb < 32 else 16
                    pk, pb = psring.next()
                    I("pe", "matmul", [Kk, ("Kb", h % 2, "kr"), Qk], [pk], pb[:kk, :512], Kh[:, kb * 128:kb * 128 + kk], Qh[:, q0:q0 + 512], start=True, stop=True)
                    ptk, pt = Ptr.next()
                    I("act", "activation", [pk, nk], [ptk], out=pt[:kk, :], in_=pb[:kk, :512], func=AF.Exp, scale=SCALE, bias=negm[:kk, :])
                    if kb < 32 and kb >= 8 * sg:
                        mi = kb - 8 * sg
                        I("pool", "tensor_tensor", [ptk, "maskb"], [ptk], out=pt, in0=pt, in1=maskb[sg % 2][:, mi * 512:(mi + 1) * 512], op=ALU.mult)
                    I("pe", "matmul", [ptk, "Vh"], [pok], po[:, :512], Vh[:kk, kb, :], pt[:kk, :], start=(idx == 0), stop=(idx == len(seq) - 1))
                rsk, rs = rsum.next()
                tmk, tm = tmpf.next()
                I("dve", "tensor_copy", [pok], [rsk], out=rs[hb:hb + 64, :], in_=po[64:128, :512])
                I("pool", "tensor_tensor", [rsk, "neghalf"], [rsk], out=rs[hb:hb + 64, :], in0=rs[hb:hb + 64, :], in1=neghalf[hb:hb + 64, :], op=ALU.pow)
                I("dve", "tensor_tensor", [pok, rsk], [tmk], out=tm[hb:hb + 64, :], in0=po[0:64, :512], in1=rs[hb:hb + 64, :], op=ALU.mult)
                I("dve", "tensor_tensor", [tmk, rsk], [tmk], out=tm[hb:hb + 64, :], in0=tm[hb:hb + 64, :], in1=rs[hb:hb + 64, :], op=ALU.mult)
                I("pool", "tensor_tensor", [tmk, ("og", pr)], [("og", pr)], out=ogT[hb:hb + 64, pr, q0:q0 + 512], in0=tm[hb:hb + 64, :], in1=ogT[hb:hb + 64, pr, q0:q0 + 512], op=ALU.mult)
        G.barrier()
        A.release(mX)
        out_proj_residual(wom, j * 8 * 128, ogT, ogkey, xT, xkey, tcs, NT, V_POST + (2 + j) * 8)
        G.barrier()
        A.release(mH)

    if cfg.do_b:
        for j in range(2):
            for hf in range(2):
                mla_prompt(j, hf)
        mY = A.mark()
        ytr = Ring("ytok", [A.alloc(1024) for _ in range(2)])
        for tt in range(16):
            yk, yb = ytr.next()
            for g0 in range(0, 8, 4):
                pk, pb = trring.next()
                for i in range(4):
                    kc = g0 + i
                    I("pe", "transpose", [("xo", tt // 8, kc), "ident"], [pk], pb[:, i * 128:(i + 1) * 128], x_own[:, kc, tt * 128:(tt + 1) * 128], ident)
                copy_any("act" if g0 == 0 else "dve", [pk], [yk], yb[:, g0 * 128:(g0 + 4) * 128], pb[:, :512])
            I("sp", "dma_start", [yk], [("out", "y", tt)], out=y_own[tt * 128:(tt + 1) * 128, :], in_=yb, slot="oy%d" % yk[1])
        G.barrier()
        A.release(mY)
    A.release(mB)
    G.barrier()
    A.release(mSP)

    def bfv(bank):
        return bank[:, :].bitcast(BF16)

    if cfg.do_s:
        wring = Ring("w", [v3(A.alloc(1024, BF16), 8) for _ in range(4)])
        sqr = Ring("sq", [A.alloc(512, BF16) for _ in range(2)])
        rsr = Ring("rs", [A.alloc(512) for _ in range(2)])
        wqr = Ring("wq", [v3(A.alloc(512, BF16), 4) for _ in range(2)])
        wvr = Ring("wvh", [v3(A.alloc(128, BF16), 2) for _ in range(2)])
        wukT_sb = v3(A.alloc(16 * 256, BF16), 16)
        stab = A.alloc(64)
        smask_sb = A.alloc(NB * 64)
        ptxi = A.alloc(NB * NGRP, I32)
        ptxf = A.alloc(NB * NGRP)
        idxi = A.alloc(NB * NGRP, I32)
        I("pool", "dma_start", [], ["wukT"], out=wukT_sb[0:64].rearrange("p a b -> p (a b)"), in_=wukT, slot="wukT")
        I("sp", "dma_start", [], ["stab"], out=stab, in_=mtab[:, 16:80], slot="stab")
        I("sp", "dma_start", [], ["smask"], out=smask_sb[0:64, :], in_=smask, slot="smask")
        I("sp", "dma_start", [], ["ptxi"], out=ptxi, in_=ptx, slot="ptx")
        I("dve", "tensor_copy", ["ptxi"], ["ptxf"], out=ptxf, in_=ptxi)
        I("dve", "tensor_scalar", ["ptxf", "vec"], ["ptxf"], out=ptxf, in0=ptxf, scalar1=16.0, scalar2=vcol(V_R16), op0=ALU.mult, op1=ALU.add)
        I("dve", "tensor_copy", ["ptxf"], ["idx"], out=idxi, in_=ptxf)
        wuv3 = wuv.rearrange("p (kc f) -> p kc f", kc=2)
        NT = NS
        tcs = [(0, NS)]
        xkey = lambda kc: ("xs", kc)
        ogT = v3(A.alloc(8 * NT, BF16), 8)
        ogkey = lambda kc: ("ogs", kc)
        qln = v3(A.alloc(4 * NT, BF16), 4)
        qlnkey = lambda kc: ("qlns", kc)
        hT = v3(A.alloc(8 * NT, BF16), 8)
        hkey = lambda kc: ("hTs", kc)
        qraw = v3(A.alloc(4 * NT), 4)
        QA = [v3(A.alloc(16 * 64, BF16), 16) for _ in range(3)]
        OL = [v3(A.alloc(16 * 64, BF16), 16) for _ in range(2)]
        qnr = Ring("qn", [A.alloc(64, BF16) for _ in range(2)])
        t1r = Ring("t1s", [A.alloc(64) for _ in range(2)])
        t2r = Ring("t2s", [A.alloc(64) for _ in range(2)])
        gtr = Ring("gt", [A.alloc(2048, BF16) for _ in range(3)])
        gkr = Ring("gk", [A.alloc(256, BF16) for _ in range(3)])
        KT = [A.alloc(1024, BF16) for _ in range(3)]
        Pr = Ring("P", [A.alloc(1024, BF16) for _ in range(2)])
        PTr = Ring("PT", [A.alloc(512, BF16) for _ in range(2)])
        Qbr = Ring("Qb", [v3(A.alloc(3 * 64, BF16), 3) for _ in range(2)])
        oacc = A.alloc(256)
        obf = A.alloc(256, BF16)
        Sn = A.alloc(64)
        sm = A.alloc(16)
        print("arena words used in sample phase:", A.top, "of", A.n)
        ktb = [bfv(banks[0]), bfv(banks[1]), bfv(banks[2])]
        ptb = bfv(banks[3])
        Sring = Ring("ssq", [banks[4], banks[5]])
        pvb = banks[6]
        msb = banks[7]

        def col(i):
            return sm[0:64, i:i + 1]

        for j in range(2):
            rst = rms_stats(xsT, xkey, 8, tcs, D)
            norm_to(hT, hkey, xsT, xkey, 8, tcs, rst, V_PRE + (2 + j) * 8)
            for oc in range(12):
                wk, w = load_w(wim, (j * 12 + oc) * 128)
                pk, pb = proj(wk, w, 8, hT, hkey, 0, NT)
                if oc < 4:
                    I("act", "activation", [pk], [("qraws", oc)], out=qraw[:, oc, :], in_=pb[:, :NT], func=AF.Copy)
                    qk, q = sqr.next()
                    I("act", "activation", [pk], [qk], out=q[:, :NT], in_=pb[:, :NT], func=AF.Square)
                    I("pe", "matmul", [qk, "onesb"], [("ssq", 0)], ssq_banks[0][:, :NT], onesb, q[:, :NT], start=(oc == 0), stop=(oc == 3))
                else:
                    I("act", "activation", [pk], [("ogs", oc - 4)], out=ogT[:, oc - 4, :], in_=pb[:, :NT], func=AF.Silu)
                if oc == 3:
                    rk, r = rstd_from(ssq_banks[0], ("ssq", 0), NT, 512)
                    for kc in range(4):
                        I("dve", "scalar_tensor_tensor", [("qraws", kc), rk, "vec"], [("qlns", kc)],
                          out=qln[:, kc, :], in0=qraw[:, kc, :], scalar=vcol(V_QN + j * 4 + kc), in1=r[:, :NT], op0=ALU.mult, op1=ALU.mult)
            for h in range(16):
                wqk, wq = wqr.next()
                I("pool", "dma_start", [], [wqk], out=wq.rearrange("p a b -> p (a b)"), in_=wuq[(j * 16 + h) * 128:(j * 16 + h + 1) * 128, :], slot="wq%d" % wqk[1])
                pk, pb = proj(wqk, wq, 4, qln, qlnkey, 0, NT, ring=trring)
                qnk, qn = qnr.next()
                I("act", "activation", [pk], [qnk], out=qn[0:64, :], in_=pb[0:64, :NT], func=AF.Copy)
                k1, t1 = t1r.next()
                k2, t2 = t2r.next()
                I("dve", "tensor_tensor", [pk, "stab"], [k1], out=t1[64:96, :], in0=pb[64:96, :NT], in1=stab[64:96, :], op=ALU.mult)
                I("dve", "tensor_tensor", [pk, "stab"], [k2], out=t2[64:96, :], in0=pb[96:128, :NT], in1=stab[96:128, :], op=ALU.mult)
                I("pool", "tensor_tensor", [k1, k2], [("QA", 2)], out=QA[2][0:32, h, :], in0=t1[64:96, :], in1=t2[64:96, :], op=ALU.add)
                for c2 in range(2):
                    pk2, pb2 = psring.next()
                    I("pe", "matmul", [qnk, "wukT"], [pk2], pb2[:, :NT], wukT_sb[0:64, h, c2 * 128:(c2 + 1) * 128], qn[0:64, :], start=True, stop=True)
                    copy_any("act" if c2 == 0 else "dve", [pk2], [("QA", c2)], QA[c2][:, h, :], pb2[:, :NT])
            G.barrier()
            for b in range(NB):
                Qbk, Qb = Qbr.next()
                for c in range(3):
                    np_ = 128 if c < 2 else 32
                    I("pool", "tensor_copy", [("QA", c)], [Qbk], out=Qb[:np_, c, :].rearrange("p (h t) -> p h t", t=LS), in_=QA[c][:np_, :, b * LS:(b + 1) * LS])
                I("dve", "memset", [], ["m_old"], col(0), NEG)
                I("dve", "memset", [], ["l"], col(4), 0.0)
                I("dve", "memset", [], ["oacc"], oacc[0:64, :], 0.0)
                for g in range(NGRP + 1):
                    newk = (g == NGRP)
                    if not newk:
                        gk_, gt = gtr.next()
                        kk_, gkk = gkr.next()
                        icol = idxi[:, b * NGRP + g:b * NGRP + g + 1]
                        I("pool", "indirect_dma_start", ["idx"], [gk_], out=gt, out_offset=None, in_=cckv,
                          in_offset=bass.IndirectOffsetOnAxis(ap=icol, axis=0), bounds_check=cfg.n_pool * 16 - 1, oob_is_err=False, slot="gt%d" % gk_[1])
                        I("pool", "indirect_dma_start", ["idx"], [kk_], out=gkk, out_offset=None, in_=ckr,
                          in_offset=bass.IndirectOffsetOnAxis(ap=icol, axis=0), bounds_check=cfg.n_pool * 16 - 1, oob_is_err=False, slot="gk%d" % kk_[1])
                        for jj in range(8):
                            I("pe", "transpose", [gk_, "identb"], [("ps", 0)], ktb[0][:, jj * 128:(jj + 1) * 128], gt[:, jj * 256:jj * 256 + 128], identb)
                            I("pe", "transpose", [gk_, "identb"], [("ps", 1)], ktb[1][:, jj * 128:(jj + 1) * 128], gt[:, jj * 256 + 128:jj * 256 + 256], identb)
                            I("pe", "transpose", [kk_, "identb"], [("ps", 2)], ktb[2][0:32, jj * 128:(jj + 1) * 128], gkk[:, jj * 32:(jj + 1) * 32], identb)
                        I("dve", "tensor_copy", [("ps", 0)], [("KT", 0)], out=KT[0], in_=ktb[0])
                        I("act", "activation", [("ps", 1)], [("KT", 1)], out=KT[1], in_=ktb[1], func=AF.Copy)
                        I("dve", "tensor_copy", [("ps", 2)], [("KT", 2)], out=KT[2][0:32, :], in_=ktb[2][0:32, :])
                        nh = 2
                        Ssrc = []
                        for hh in range(2):
                            sk, sb = Sring.next()
                            I("pe", "matmul", [Qbk, ("KT", 0)], [sk], sb[0:64, :512], Qb[:, 0, :], KT[0][:, hh * 512:(hh + 1) * 512], start=True, stop=False)
                            I("pe", "matmul", [Qbk, ("KT", 1)], [sk], sb[0:64, :512], Qb[:, 1, :], KT[1][:, hh * 512:(hh + 1) * 512], start=False, stop=False)
                            I("pe", "matmul", [Qbk, ("KT", 2)], [sk], sb[0:64, :512], Qb[0:32, 2, :], KT[2][0:32, hh * 512:(hh + 1) * 512], start=False, stop=True)
                            I("dve", "tensor_reduce", [sk], [("gm", hh)], out=col(5 + hh), in_=sb[0:64, :512], axis=AX.X, op=ALU.max)
                            Ssrc.append((sk, sb[0:64, :512], 512))
                    else:
                        sk = ("ps", 3)
                        sb = msb
                        I("pe", "matmul", [Qbk, "cTs"], [("pt", 1)], sb[0:64, :64], Qb[:, 0, :], cTs[:, 0, :], start=True, stop=False)
                        I("pe", "matmul", [Qbk, "cTs"], [("pt", 1)], sb[0:64, :64], Qb[:, 1, :], cTs[:, 1, :], start=False, stop=False)
                        I("pe", "matmul", [Qbk, "krTs"], [("pt", 1)], sb[0:64, :64], Qb[0:32, 2, :], krTs[0:32, :], start=False, stop=True)
                        I("dve", "tensor_tensor", [("pt", 1), "smask"], ["Sn"], out=Sn[0:64, :], in0=sb[0:64, :64], in1=smask_sb[0:64, b * 64:(b + 1) * 64], op=ALU.add)
                        I("dve", "tensor_reduce", ["Sn"], [("gm", 0)], out=col(5), in_=Sn[0:64, :], axis=AX.X, op=ALU.max)
                        I("dve", "tensor_copy", [("gm", 0)], [("gm", 1)], out=col(6), in_=col(5))
                        nh = 1
                        Ssrc = [("Sn", Sn[0:64, :], 64)]
                    I("dve", "tensor_tensor", [("gm", 0), ("gm", 1)], ["m_new"], out=col(1), in0=col(5), in1=col(6), op=ALU.max)
                    I("dve", "tensor_tensor", ["m_new", "m_old"], ["m_new"], out=col(1), in0=col(1), in1=col(0), op=ALU.max)
                    I("dve", "tensor_scalar", ["m_new"], ["negb"], out=col(2), in0=col(1), scalar1=-SCALE, scalar2=None, op0=ALU.mult)
                    I("act", "activation", ["m_old", "negb"], ["alpha"], out=col(3), in_=col(0), func=AF.Exp, scale=SCALE, bias=col(2))
                    I("dve", "memset", [], ["rs"], sm[0:64, 7:9], 0.0)
                    Pk, P = Pr.next()
                    off = 0
                    for hi, (sk, sap, w_) in enumerate(Ssrc):
                        I("act", "activation", [sk, "negb", "rs"], [Pk, "rs"] if hi == len(Ssrc) - 1 else [Pk, "rs"], out=P[0:64, off:off + w_], in_=sap, func=AF.Exp, scale=SCALE, bias=col(2), accum_out=col(7 + hi))
                        off += w_
                    nblk = 8 if not newk else 1
                    bw = 128 if not newk else 64
                    PTk, PT = PTr.next()
                    for jj in range(nblk):
                        I("pe", "transpose", [Pk, "identb"], [("pt", 0)], ptb[:bw, jj * 64:(jj + 1) * 64], P[0:64, jj * bw:(jj + 1) * bw], identb[0:64, 0:64])
                    I("dve", "tensor_copy", [("pt", 0)], [PTk], out=PT[:bw, :nblk * 64], in_=ptb[:bw, :nblk * 64])
                    for jj in range(nblk):
                        if not newk:
                            I("pe", "matmul", [PTk, gk_], [("pv",)], pvb[0:64, :256], PT[:, jj * 64:(jj + 1) * 64], gt[:, jj * 256:(jj + 1) * 256], start=(jj == 0), stop=(jj == nblk - 1))
                        else:
                            I("pe", "matmul", [PTk, "cns_tok"], [("pv",)], pvb[0:64, :256], PT[0:64, 0:64], cns_tok[0:64, :], start=True, stop=True)
                    I("dve", "scalar_tensor_tensor", [("pv",), "alpha", "oacc"], ["oacc"], out=oacc[0:64, :], in0=oacc[0:64, :], scalar=col(3), in1=pvb[0:64, :256], op0=ALU.mult, op1=ALU.add)
                    I("dve", "scalar_tensor_tensor", ["l", "alpha", "rs"], ["l"], out=col(4), in0=col(4), scalar=col(3), in1=col(7), op0=ALU.mult, op1=ALU.add)
                    if nh == 2:
                        I("dve", "tensor_tensor", ["l", "rs"], ["l"], out=col(4), in0=col(4), in1=col(8), op=ALU.add)
                    I("dve", "tensor_copy", ["m_new"], ["m_old"], out=col(0), in_=col(1))
                I("pool", "tensor_tensor", ["l", "neghalf"], ["rl"], out=col(9), in0=col(4), in1=neghalf[0:64, 0:1], op=ALU.pow)
                I("dve", "tensor_scalar", ["oacc", "rl"], ["obf"], out=obf[0:64, :], in0=oacc[0:64, :], scalar1=col(9), scalar2=col(9), op0=ALU.mult, op1=ALU.mult)
                for c2 in range(2):
                    I("pe", "transpose", ["obf", "identb"], [("pt", 0)], ptb[:, c2 * 64:(c2 + 1) * 64], obf[0:64, c2 * 128:(c2 + 1) * 128], identb[0:64, 0:64])
                for c2 in range(2):
                    I("dve", "tensor_copy", [("pt", 0)], [("OL", c2)], out=OL[c2][:, :, b * LS:(b + 1) * LS], in_=ptb[:, c2 * 64:(c2 + 1) * 64].rearrange("p (h t) -> p h t", t=LS))
            G.barrier()
            for h in range(16):
                hb = (h % 2) * 64
                pr = h // 2
                wvk, wvh = wvr.next()
                I("pool", "dma_start", [], [wvk], out=wvh, in_=wuv3[:, :, h * 64:(h + 1) * 64], slot="wvh%d" % wvk[1])
                pk, pb = trring.next()
                for c2 in range(2):
                    I("pe", "matmul", [wvk, ("OL", c2)], [pk], pb[0:64, :NT], wvh[:, c2, :], OL[c2][:, h, :], start=(c2 == 0), stop=(c2 == 1))
                I("dve", "tensor_tensor", [pk, ("ogs", pr)], [("ogs", pr)], out=ogT[hb:hb + 64, pr, :], in0=pb[0:64, :NT], in1=ogT[hb:hb + 64, pr, :], op=ALU.mult)
            G.barrier()
            mO2 = A.mark()
            out_proj_residual(wom, j * 8 * 128, ogT, ogkey, xsT, xkey, tcs, NT, V_POST + (2 + j) * 8)
            G.barrier()
            A.release(mO2)
        ysb = A.alloc(1024)
        for g0 in range(0, 8, 4):
            pk, pb = trring.next()
            for i in range(4):
                I("pe", "transpose", [("xs", g0 + i), "ident"], [pk], pb[:NS, i * 128:(i + 1) * 128], xsT[:, g0 + i, :], ident)
            copy_any("act" if g0 == 0 else "dve", [pk], ["ysb"], ysb[:NS, g0 * 128:(g0 + 4) * 128], pb[:NS, :512])
        I("sp", "dma_start", ["ysb"], [("out", "ys")], out=ys, in_=ysb[:NS, :], slot="oys")

    G.barrier()
    G.add("sp", lambda e: None)
    print('total ops', G.n, {e: len(v) for e, v in G.ops.items()})
    G.emit(nc, stack, getattr(cfg, 'limit', None))
    stack.close()
    return nc


OWN_CHUNKS = {0: (0, 3, 4, 7), 1: (1, 2, 5, 6)}


def _chunked_w(w, ncols_chunk=128):
    K, N = w.shape
    a = w.reshape(K // 128, 128, N // 128, 128)
    a = a.transpose(2, 1, 0, 3)
    return np.ascontiguousarray(a).reshape(N // 128 * 128, K // 128 * 128)


def _rope_tab(pos):
    half = 16
    inv = (10000.0 ** (-np.arange(half, dtype=np.float32) / half)).astype(np.float32)
    ang = pos.astype(np.float32)[None, :] * inv[:, None]
    cos = np.cos(ang).astype(np.float32)
    sin = np.sin(ang).astype(np.float32)
    tab = np.zeros((128, len(pos)), np.float32)
    tab[64:80] = cos
    tab[80:96] = cos
    tab[96:112] = -sin
    tab[112:128] = sin
    return tab


def prep_inputs(cfg, x_prompt, x_sample, cache_ckv, cache_krope, state_conv, page_table, meta_tokens,
                pre_norm_g, post_norm_g, w_in_conv, conv_w, w_out_conv, kv_norm_g, w_dkv,
                kv_lat_norm_g, w_uk, w_uv, w_in_mla, q_norm_g, w_uq, w_out_mla):
    f = np.float32
    shared = {}
    shared["meta"] = np.ascontiguousarray(meta_tokens, f)
    shared["ident"] = np.eye(128, dtype=f)
    shared["wic"] = np.concatenate([_chunked_w(np.asarray(w_in_conv[l])) for l in range(2)], 0)
    shared["woc"] = np.concatenate([_chunked_w(np.asarray(w_out_conv[l])) for l in range(2)], 0)
    wd = np.asarray(w_dkv)
    wkr = wd[:, 256:288]
    wd2 = np.concatenate([np.zeros((1024, 64), f), wkr, wkr[:, 16:32], wkr[:, 0:16]], 1)
    shared["wdkv"] = np.concatenate([_chunked_w(wd[:, 0:128]), _chunked_w(wd[:, 128:256]), _chunked_w(wd2)], 0)
    shared["wim"] = np.concatenate([_chunked_w(np.asarray(w_in_mla[j])) for j in range(2)], 0)
    uq = []
    for j in range(2):
        w = np.asarray(w_uq[j]).reshape(512, 16, 96)
        wh = np.concatenate([w[:, :, 0:64], w[:, :, 64:96], w[:, :, 80:96], w[:, :, 64:80]], 2)
        for h in range(16):
            uq.append(_chunked_w(np.ascontiguousarray(wh[:, h, :])))
    shared["wuq"] = np.concatenate(uq, 0)
    shared["wom"] = np.concatenate([_chunked_w(np.asarray(w_out_mla[j])) for j in range(2)], 0)
    shared["wuk"] = np.ascontiguousarray(np.asarray(w_uk).reshape(2, 128, 1024).transpose(1, 0, 2)).reshape(128, 2048)
    shared["wuv"] = np.ascontiguousarray(np.asarray(w_uv).reshape(2, 128, 1024).transpose(1, 0, 2)).reshape(128, 2048)
    shared["wukT"] = np.ascontiguousarray(np.asarray(w_uk).transpose(2, 1, 0)).reshape(64, 16 * 256)
    kpos = np.concatenate([np.arange(16, T), np.arange(0, 16)])
    shared["ktab"] = _rope_tab(kpos)
    npg = cfg.n_pages
    past = npg * PAGE
    spos = np.tile(past + np.arange(LS), NB)
    shared["mtab"] = np.concatenate([_rope_tab(np.arange(16)), _rope_tab(spos)], 1)
    shared["cckv"] = np.asarray(cache_ckv).reshape(cfg.n_pool * 16, 2048)
    shared["ckr"] = np.asarray(cache_krope).reshape(cfg.n_pool * 16, 256)
    sm = np.full((64, NB, NB, LS), NEG, f)
    for b in range(NB):
        for t in range(LS):
            sm[np.arange(16) * 4 + t, b, b, 0:t + 1] = 0.0
    shared["smask"] = sm.reshape(64, NB * 64)

    def fm(v):
        return np.asarray(v, f).reshape(-1, 128).T

    vec = np.zeros((128, NV), f)
    for l in range(4):
        vec[:, V_PRE + l * 8:V_PRE + l * 8 + 8] = fm(pre_norm_g[l])
        vec[:, V_POST + l * 8:V_POST + l * 8 + 8] = fm(post_norm_g[l])
    vec[:, V_KVN:V_KVN + 8] = fm(kv_norm_g)
    vec[:, V_LAT:V_LAT + 2] = fm(kv_lat_norm_g)
    for j in range(2):
        vec[:, V_QN + j * 4:V_QN + j * 4 + 4] = fm(q_norm_g[j])
    vec[:, V_R16] = np.arange(128) % 16
    for l in range(2):
        for k in range(3):
            vec[:, V_CW + l * 24 + k * 8:V_CW + l * 24 + k * 8 + 8] = fm(conv_w[l, k])

    ki = np.arange(128)[:, None]
    qi = np.arange(512)[None, :]
    pat = np.zeros((2, 128, 8, 512), f)
    for d in range(4):
        tri = ((d * 128 + ki) <= qi).astype(f)
        pat[0, :, d, :] = tri
        pat[1, :, d, :] = 1.0
        pat[1, :, 4 + d, :] = tri

    in_maps = []
    xp_all = np.asarray(x_prompt)
    xs_all = np.asarray(x_sample)
    sc_all = np.asarray(state_conv)
    pt_all = np.asarray(page_table)
    for c in range(NCORES):
        b, r = c // 2, c % 2
        m = dict(shared)
        m["xp"] = np.ascontiguousarray(xp_all[b])
        m["xs"] = np.ascontiguousarray(xs_all[c * NB:(c + 1) * NB].reshape(NS, D))
        m["sconv"] = np.ascontiguousarray(sc_all[:, c * NB:(c + 1) * NB].reshape(64, D))
        v = vec.copy()
        own = OWN_CHUNKS[r]
        for s in range(4):
            v[:, V_BLEND + 2 * s] = 1.0 if own[s] == 2 * s else 0.0
            v[:, V_BLEND + 2 * s + 1] = 1.0 if own[s] == 2 * s + 1 else 0.0
        m["vecs"] = v
        qpos = np.concatenate([16 + ch * 512 + np.arange(512) for ch in own])
        m["qtab"] = _rope_tab(qpos)
        mk = np.stack([pat[0 if own[par] == 2 * par else 1] for par in range(2)], 0)
        m["masks"] = np.ascontiguousarray(mk).reshape(2 * 128, 4096)
        pt = pt_all[c * NB:(c + 1) * NB]
        e = pt.reshape(NB, npg // 8, 8)
        e = np.repeat(e[:, :, :, None], 16, 3)
        m["ptx"] = np.ascontiguousarray(e.transpose(2, 3, 0, 1)).reshape(128, NB * (npg // 8)).astype(np.int32)
        in_maps.append(m)
    return in_maps


def assemble(res, cfg):
    y_prompt = np.zeros((4, SEQ, D), np.float32)
    y_sample = np.zeros((128, LS, D), np.float32)
    ckv_p = np.zeros((4, T, 256), np.float32)
    kr_p = np.zeros((4, T, 32), np.float32)
    conv_p = np.zeros((2, 4, 2, D), np.float32)
    ckv_s = np.zeros((128, LS, 256), np.float32)
    kr_s = np.zeros((128, LS, 32), np.float32)
    conv_s = np.zeros((2, 128, 2, D), np.float32)
    for c in range(NCORES):
        b, r = c // 2, c % 2
        o = res[c]
        for s, ch in enumerate(OWN_CHUNKS[r]):
            y_prompt[b, ch * 512:(ch + 1) * 512] = o["y_own"][s * 512:(s + 1) * 512]
        y_sample[c * NB:(c + 1) * NB] = o["ys"].reshape(NB, LS, D)
        if r == 0:
            ckv_p[b] = o["ckv_p"]
            kr_p[b] = o["kr_p"]
            conv_p[:, b] = o["conv_p"].reshape(2, 2, D)
        ckv_s[c * NB:(c + 1) * NB] = o["ckv_s"].reshape(NB, LS, 256)
        kr_s[c * NB:(c + 1) * NB] = o["kr_s"].reshape(NB, LS, 32)
        conv_s[:, c * NB:(c + 1) * NB] = o["conv_s"].reshape(2, NB, 2, D)
    return (y_prompt, y_sample, ckv_p, kr_p, conv_p, ckv_s, kr_s, conv_s)


_NC_CACHE = {}


def kernel(**inputs):
    cfg = Cfg(n_pool=inputs["cache_ckv"].shape[0], n_pages=inputs["page_table"].shape[1])
    key = (cfg.n_pool, cfg.n_pages)
    if key not in _NC_CACHE:
        _NC_CACHE[key] = build(cfg)
    nc = _NC_CACHE[key]
    in_maps = prep_inputs(cfg, **inputs)
    res = run_bass_kernel_spmd(nc, in_maps, core_ids=list(range(NCORES)))
    return assemble(res.results, cfg)
```

```python
import contextlib
import numpy as np
import concourse.bass as bass
import concourse.mybir as mybir
from concourse.bass_utils import run_bass_kernel_spmd

F32 = mybir.dt.float32
BF16 = mybir.dt.bfloat16
I32 = mybir.dt.int32
ALU = mybir.AluOpType
AF = mybir.ActivationFunctionType
AX = mybir.AxisListType

D = 1024
KC = 8
SEQ = 4096
NMETA = 16
T = SEQ + NMETA
NCORES = 8
NB = 16
LS = 4
NS = NB * LS
EPS = 1e-6
SCALE = 96 ** -0.5
PAGE = 128
NEG = -30000.0

ENGS = ("pe", "act", "dve", "pool", "sp")
import os
XENG = os.environ.get("XENG", "act,dve").split(",")
PSUM_KEYS = ("ps", "pt", "ssq")


class Op:
    __slots__ = ("eng", "fn", "deps", "signal", "semval", "is_dma", "slot", "idx")

    def __init__(self, eng, fn):
        self.eng = eng
        self.fn = fn
        self.deps = ()
        self.signal = False
        self.semval = 0
        self.is_dma = False
        self.slot = None


class Graph:
    def __init__(self):
        self.ops = {e: [] for e in ENGS}
        self.last_w = {}
        self.readers = {}
        self.slot_count = {}
        self.n = 0
        self.barrier_ops = []
        self.pending_dma = []
        self.last_on = {}

    def add(self, eng, fn, reads=(), writes=(), slot=None):
        op = Op(eng, fn)
        op.idx = self.n
        self.n += 1
        deps = {}

        def dep(o):
            if o is not None and o is not op:
                deps[id(o)] = o

        for k in reads:
            dep(self.last_w.get(k))
            if isinstance(k, tuple) and k[0] in PSUM_KEYS:
                r = self.readers.get(k)
                if r:
                    for ek, o in r.items():
                        if ek != eng:
                            dep(o)
        for k in writes:
            dep(self.last_w.get(k))
            r = self.readers.get(k)
            if r:
                for o in r.values():
                    dep(o)
        for o in self.barrier_ops:
            dep(o)
        op.deps = list(deps.values())
        for o in op.deps:
            o.signal = True
        for k in reads:
            r = self.readers.setdefault(k, {})
            if slot is not None:
                r[("dma", op.idx)] = op
            else:
                r[eng] = op
        for k in writes:
            self.last_w[k] = op
            self.readers[k] = {}
        if slot is not None:
            op.is_dma = True
            op.slot = slot
            c = self.slot_count.get(slot, 0) + 1
            self.slot_count[slot] = c
            op.semval = 16 * c
            self.pending_dma.append(op)
        else:
            self.last_on[eng] = op
        self.ops[eng].append(op)
        return op

    def barrier(self):
        ops = list(self.last_on.values()) + self.pending_dma
        self.barrier_ops = ops
        self.pending_dma = []

    def emit(self, nc, stack, limit=None):
        if limit is not None:
            for e in ENGS:
                self.ops[e] = [o for o in self.ops[e] if o.idx < limit]
        esem = {e: stack.enter_context(nc.semaphore("s_" + e)) for e in ENGS if e != "sp"}
        ssem = {s: stack.enter_context(nc.semaphore("d_%d" % i)) for i, s in enumerate(self.slot_count)}
        for e in ENGS:
            c = 0
            for op in self.ops[e]:
                if not op.is_dma and op.signal:
                    c += 1
                    op.semval = c

        def sem_of(o):
            return ssem[o.slot] if o.is_dma else esem[o.eng]

        def run(ename, eng):
            known = {}
            for op in self.ops[ename]:
                need = {}
                for d in op.deps:
                    if ename == "pe" and d.eng == "pe" and not d.is_dma:
                        continue
                    s = sem_of(d)
                    v = d.semval
                    key = id(s)
                    if known.get(key, 0) >= v:
                        continue
                    if key not in need or need[key][1] < v:
                        need[key] = (s, v)
                for key, (s, v) in need.items():
                    eng.wait_ge(s, v)
                    known[key] = v
                inst = op.fn(eng)
                if inst is None:
                    continue
                if op.is_dma:
                    inst.then_inc(ssem[op.slot], 16)
                elif op.signal:
                    inst.then_inc(esem[ename], 1)

        block = stack.enter_context(nc.Block())

        @block.sync
        def _(e):
            run("sp", e)

        @block.scalar
        def _(e):
            run("act", e)

        @block.vector
        def _(e):
            run("dve", e)

        @block.gpsimd
        def _(e):
            run("pool", e)

        @block.tensor
        def _(e):
            run("pe", e)


class Arena:
    def __init__(self, big, nwords):
        self.big = big
        self.n = nwords
        self.top = 0

    def mark(self):
        return self.top

    def release(self, m):
        self.top = m

    def alloc(self, nelem, dtype=F32):
        words = (nelem * (2 if dtype == BF16 else 4) + 3) // 4
        words = (words + 7) // 8 * 8
        off = self.top
        self.top += words
        assert self.top <= self.n, "SBUF arena overflow: %d > %d words" % (self.top, self.n)
        ap = self.big[:, off:off + words]
        if dtype != F32:
            ap = ap.bitcast(dtype)
        return ap[:, 0:nelem]


class Ring:
    def __init__(self, name, aps):
        self.name = name
        self.aps = aps
        self.i = 0

    def next(self):
        i = self.i % len(self.aps)
        self.i += 1
        return (self.name, i), self.aps[i]


def chunks(n, c=512):
    return [(i, min(c, n - i)) for i in range(0, n, c)]


class Cfg:
    def __init__(self, n_pool=10240, n_pages=64, do_b=True, do_s=True):
        self.n_pool = n_pool
        self.n_pages = n_pages
        self.do_b = do_b
        self.do_s = do_s


V_PRE = 0
V_POST = 32
V_KVN = 64
V_LAT = 72
V_QN = 74
V_CW = 82
V_BLEND = 130
V_R16 = 138
V_EPS = 139
NV = 140


def build(cfg):
    nc = bass.Bass("TRN2", target_bir_lowering=False)
    G = Graph()
    stack = contextlib.ExitStack()

    regcache = {}

    def I(eng, method, reads, writes, *args, slot=None, **kw):
        def fn(e):
            try:
                if "bounds_check" in kw and not isinstance(kw["bounds_check"], (type(None),)) and isinstance(kw["bounds_check"], int):
                    if "bcreg" not in regcache:
                        regcache["bcreg"] = e.to_reg(kw["bounds_check"])
                    kw2 = dict(kw)
                    kw2["bounds_check"] = regcache["bcreg"]
                    return getattr(e, method)(*args, **kw2)
                return getattr(e, method)(*args, **kw)
            except Exception:
                print("FAILED OP", eng, method, writes, [str(a)[:200] for a in args], {k: str(v)[:200] for k, v in kw.items()})
                raise
        return G.add(eng, fn, reads=reads, writes=writes, slot=slot)

    def din(name, shape, dt=F32):
        return nc.dram_tensor(name, list(shape), dt, kind="ExternalInput").ap()

    def dout(name, shape, dt=F32):
        return nc.dram_tensor(name, list(shape), dt, kind="ExternalOutput").ap()

    NPG = cfg.n_pages
    NGRP = NPG // 8
    xp = din("xp", [SEQ, D])
    meta = din("meta", [NMETA, D])
    xs = din("xs", [NS, D])
    sconv = din("sconv", [64, D])
    vecs = din("vecs", [128, NV])
    ident_d = din("ident", [128, 128])
    wic = din("wic", [2 * 32 * 128, 1024])
    woc = din("woc", [2 * 8 * 128, 1024])
    wdkv = din("wdkv", [3 * 128, 1024])
    wim = din("wim", [2 * 12 * 128, 1024])
    wuq = din("wuq", [2 * 16 * 128, 512])
    wom = din("wom", [2 * 8 * 128, 1024])
    wuk = din("wuk", [128, 2048])
    wuv = din("wuv", [128, 2048])
    wukT = din("wukT", [64, 16 * 256])
    ktab = din("ktab", [128, T])
    mtab = din("mtab", [128, 80])
    qtab = din("qtab", [128, 2048])
    masks = din("masks", [2 * 128, 4096])
    ptx = din("ptx", [128, NB * NGRP], I32)
    smask = din("smask", [64, NB * 64])
    cckv = din("cckv", [cfg.n_pool * 16, 2048])
    ckr = din("ckr", [cfg.n_pool * 16, 256])

    y_own = dout("y_own", [2048, D])
    ys = dout("ys", [NS, D])
    ckv_p = dout("ckv_p", [T, 256])
    kr_p = dout("kr_p", [T, 32])
    conv_p = dout("conv_p", [4, D])
    ckv_s = dout("ckv_s", [NS, 256])
    kr_s = dout("kr_s", [NS, 32])
    conv_s = dout("conv_s", [64, D])

    NW = 53000
    big = stack.enter_context(nc.sbuf_tensor("arena", [128, NW], F32))
    A = Arena(big, NW)
    banks = [stack.enter_context(nc.psum_tensor("bank%d" % i, [128, 512], F32)) for i in range(8)]

    def v3(ap, a):
        return ap.rearrange("p (a b) -> p a b", a=a)

    def bt(ap, t):
        return ap.rearrange("p (b t) -> p b t", t=t)

    ident = A.alloc(128)
    identb = A.alloc(128, BF16)
    onesb = A.alloc(128, BF16)
    neghalf = A.alloc(512)
    vec = A.alloc(NV)
    I("sp", "dma_start", [], ["ident"], out=ident, in_=ident_d, slot="c_ident")
    I("sp", "dma_start", [], ["vec"], out=vec, in_=vecs, slot="c_vec")
    I("pool", "dma_start", [], ["identb"], out=identb, in_=ident_d, slot="c_identb")
    I("dve", "memset", [], ["onesb"], onesb, 1.0)
    I("dve", "memset", [], ["neghalf"], neghalf, -0.5)

    def vcol(off):
        return vec[:, off:off + 1]


    xsT = v3(A.alloc(8 * NS), 8)
    cTs = v3(A.alloc(2 * NS, BF16), 2)
    krTs = A.alloc(NS, BF16)
    cns_tok = A.alloc(256, BF16)
    mSP = A.mark()
    x_own = v3(A.alloc(8 * 2048), 8)
    cT = v3(A.alloc(2 * T, BF16), 2)
    Kbuf = [A.alloc(T, BF16) for _ in range(2)]
    halo = A.alloc(2 * 8 * 2)
    cvp = A.alloc(8 * 4)
    cvs = A.alloc(8 * 64)
    stT = A.alloc(8 * 64)

    for i in range(2):
        I("pool", "memset", [], [("Kb", i, "kr")], Kbuf[i], 0.0)

    psring = Ring("ps", [banks[i] for i in range(4)])
    ssq_banks = [banks[4], banks[5]]
    trring = Ring("pt", [banks[6], banks[7]])

    def copy_any(eng, reads, writes, out, in_):
        if eng == "act":
            I("act", "activation", reads, writes, out=out, in_=in_, func=AF.Copy)
        else:
            I(eng, "tensor_copy", reads, writes, out=out, in_=in_)

    def transposes_to(dst_fn, src, src_key, nrows, ncols, idn, eng_alt):
        nblk = (ncols + 127) // 128
        per = max(1, 512 // nrows)
        for g0 in range(0, nblk, per):
            pk, pb = trring.next()
            nb = min(per, nblk - g0)
            for i in range(nb):
                cb = g0 + i
                cw = min(128, ncols - cb * 128)
                I("pe", "transpose", [src_key, "ident"], [pk],
                  pb[:cw, i * nrows:(i + 1) * nrows], src[:nrows, cb * 128:cb * 128 + cw], idn[:nrows, :nrows])
            for i in range(nb):
                cb = g0 + i
                cw = min(128, ncols - cb * 128)
                dst, dkey = dst_fn(cb)
                copy_any(eng_alt[cb % len(eng_alt)], [pk], [dkey], dst, pb[:cw, i * nrows:(i + 1) * nrows])

    mA = A.mark()
    wring = Ring("w", [v3(A.alloc(1024, BF16), 8) for _ in range(4)])
    sqr = Ring("sq", [A.alloc(512, BF16) for _ in range(2)])
    rsr = Ring("rs", [A.alloc(512) for _ in range(2)])

    def load_w(dram, row0):
        k, w = wring.next()
        I("pool", "dma_start", [], [k], out=w.rearrange("p a b -> p (a b)"), in_=dram[row0:row0 + 128, :], slot="w%d" % k[1])
        return k, w

    def rstd_from(sb, skey, n, nfeat):
        rk, r = rsr.next()
        I("act", "activation", [skey, "vec"], [rk], out=r[:, :n], in_=sb[:, :n], func=AF.Ln, scale=1.0 / nfeat, bias=vcol(V_EPS))
        I("act", "activation", [rk], [rk], out=r[:, :n], in_=r[:, :n], func=AF.Exp, scale=-0.5)
        return rk, r

    def rms_stats(srcT, key_fn, nchunks, tcs, nfeat):
        res = []
        for ti, (c0, n) in enumerate(tcs):
            sb = ssq_banks[ti % 2]
            sk = ("ssq", ti % 2)
            for kc in range(nchunks):
                qk, q = sqr.next()
                I("act", "activation", [key_fn(kc)], [qk], out=q[:, :n], in_=srcT[:, kc, c0:c0 + n], func=AF.Square)
                I("pe", "matmul", [qk, "onesb"], [sk], sb[:, :n], onesb, q[:, :n], start=(kc == 0), stop=(kc == nchunks - 1))
            res.append(rstd_from(sb, sk, n, nfeat))
        return res

    def norm_to(dstT, dkey_fn, srcT, skey_fn, nchunks, tcs, rst, goff):
        for ti, (c0, n) in enumerate(tcs):
            rk, r = rst[ti]
            for kc in range(nchunks):
                I("dve", "scalar_tensor_tensor", [skey_fn(kc), rk, "vec"], [dkey_fn(kc)],
                  out=dstT[:, kc, c0:c0 + n], in0=srcT[:, kc, c0:c0 + n], scalar=vcol(goff + kc), in1=r[:, :n], op0=ALU.mult, op1=ALU.mult)

    def proj(wk, w, nk, srcT, skey_fn, c0, n, ring=None):
        pk, pb = (ring or psring).next()
        for kc in range(nk):
            I("pe", "matmul", [wk, skey_fn(kc)], [pk], pb[:, :n], w[:, kc, :], srcT[:, kc, c0:c0 + n], start=(kc == 0), stop=(kc == nk - 1))
        return pk, pb

    def out_proj_residual(wdram, wrow0, srcT, skey_fn, xT, xkey_fn, tcs, NT, goff):
        mT = v3(A.alloc(8 * NT), 8)
        assert len(tcs) <= 2
        for of in range(8):
            wk, w = load_w(wdram, wrow0 + of * 128)
            for ti, (c0, n) in enumerate(tcs):
                pk, pb = proj(wk, w, 8, srcT, skey_fn, c0, n)
                I("act", "activation", [pk], [("mT", of, ti)], out=mT[:, of, c0:c0 + n], in_=pb[:, :n], func=AF.Copy)
                qk, q = sqr.next()
                I("act", "activation", [pk], [qk], out=q[:, :n], in_=pb[:, :n], func=AF.Square)
                I("pe", "matmul", [qk, "onesb"], [("ssq", ti % 2)], ssq_banks[ti % 2][:, :n], onesb, q[:, :n], start=(of == 0), stop=(of == 7))
        for ti, (c0, n) in enumerate(tcs):
            rk, r = rstd_from(ssq_banks[ti % 2], ("ssq", ti % 2), n, D)
            for of in range(8):
                I("dve", "scalar_tensor_tensor", [("mT", of, ti), rk, "vec"], [("mT", of, ti)],
                  out=mT[:, of, c0:c0 + n], in0=mT[:, of, c0:c0 + n], scalar=vcol(goff + of), in1=r[:, :n], op0=ALU.mult, op1=ALU.mult)
                I("pool", "tensor_tensor", [("mT", of, ti), xkey_fn(of)], [xkey_fn(of)],
                  out=xT[:, of, c0:c0 + n], in0=xT[:, of, c0:c0 + n], in1=mT[:, of, c0:c0 + n], op=ALU.add)

    def run_group(gi, NT, segs, x_loads, tabs, dests, own_slot):
        tcs = chunks(NT)
        mG = A.mark()
        xT = v3(A.alloc(8 * NT), 8)
        gT = v3(A.alloc(8 * NT, BF16), 8)
        xkey = lambda kc: ("xT", kc)
        mS = A.mark()
        xr = Ring("xin", [A.alloc(1024) for _ in range(2)])
        for (src, ntok, col0) in x_loads:
            xk, xb = xr.next()
            I("sp", "dma_start", [], [xk], out=xb[:ntok, :], in_=src, slot="xin%d" % xk[1])
            transposes_to(lambda cb, col0=col0, ntok=ntok: (xT[:, cb, col0:col0 + ntok], ("xT", cb)), xb, xk, ntok, 1024, ident, XENG)
        G.barrier()
        A.release(mS)
        for l in range(2):
            mL = A.mark()
            hT = v3(A.alloc(8 * NT, BF16), 8)
            hkey = lambda kc: ("hT", kc)
            cbuf = A.alloc(NT)
            ybuf = A.alloc(NT)
            tbuf = A.alloc(NT)
            szb = A.alloc(NT)
            vexts = [A.alloc(sg["nseq"] * (sg["L"] + 2)) for sg in segs]
            rst = rms_stats(xT, xkey, 8, tcs, D)
            norm_to(hT, hkey, xT, xkey, 8, tcs, rst, V_PRE + l * 8)
            for j in range(8):
                hoff = (l * 8 + j) * 2
                cwb = V_CW + l * 24 + j
                for si, sg in enumerate(segs):
                    ve = vexts[si]
                    L = sg["L"]
                    if sg["kind"] == "meta":
                        I("dve", "memset", [], [("vext", si)], ve[:, 0:2], 0.0)
                    elif sg["kind"] == "prompt":
                        I("dve", "tensor_copy", [("halo", l, j)], [("vext", si)], out=ve[:, 0:2], in_=halo[:, hoff:hoff + 2])
                    else:
                        I("dve", "tensor_copy", ["stT"], [("vext", si)], out=bt(ve, L + 2)[:, :, 0:2],
                          in_=stT[:, j * 64 + l * 32:j * 64 + l * 32 + 32].rearrange("p (b k) -> p b k", k=2))
                for part, pname in ((1, "c"), (2, "u"), (0, "b"), (3, "z")):
                    if pname == "b":
                        for si, sg in enumerate(segs):
                            ve = vexts[si]
                            L = sg["L"]
                            s0 = sg["col0"]
                            if sg["nseq"] == 1:
                                src = [ve[:, k:k + L] for k in range(3)]
                                yv = ybuf[:, s0:s0 + L]
                                tail = ve[:, L:L + 2]
                            else:
                                ve3 = bt(ve, L + 2)
                                src = [ve3[:, :, k:k + L] for k in range(3)]
                                yv = bt(ybuf[:, s0:s0 + sg["nseq"] * L], L)
                                tail = ve3[:, :, L:L + 2]
                            vk, yk = ("vext", si), ("ybuf", si)
                            I("dve", "tensor_scalar", [vk, "vec"], [yk], out=yv, in0=src[0], scalar1=vcol(cwb), scalar2=None, op0=ALU.mult)
                            I("dve", "scalar_tensor_tensor", [vk, "vec", yk], [yk], out=yv, in0=src[1], scalar=vcol(cwb + 8), in1=yv, op0=ALU.mult, op1=ALU.add)
                            I("dve", "scalar_tensor_tensor", [vk, "vec", yk], [yk], out=yv, in0=src[2], scalar=vcol(cwb + 16), in1=yv, op0=ALU.mult, op1=ALU.add)
                            if sg["kind"] in ("meta", "prompt"):
                                I("act", "activation", [vk], [("halo", l, j)], out=halo[:, hoff:hoff + 2], in_=tail, func=AF.Copy)
                                if sg.get("last"):
                                    I("act", "activation", [vk], ["cvp"], out=cvp[:, j * 4 + l * 2:j * 4 + l * 2 + 2], in_=tail, func=AF.Copy)
                            else:
                                I("act", "activation", [vk], ["cvs"], out=cvs[:, j * 64 + l * 32:j * 64 + l * 32 + 32].rearrange("p (b k) -> p b k", k=2),
                                  in_=tail, func=AF.Copy)
                    wk, w = load_w(wic, (l * 32 + part * 8 + j) * 128)
                    for ti, (c0, n) in enumerate(tcs):
                        pk, pb = proj(wk, w, 8, hT, hkey, c0, n)
                        if pname == "c":
                            I("act", "activation", [pk], [("cbuf", ti)], out=cbuf[:, c0:c0 + n], in_=pb[:, :n], func=AF.Copy)
                        elif pname == "u":
                            for si, sg in enumerate(segs):
                                L = sg["L"]
                                s0 = sg["col0"]
                                lo = max(c0, s0)
                                hi = min(c0 + n, s0 + sg["nseq"] * L)
                                if hi <= lo:
                                    continue
                                ve = vexts[si]
                                if sg["nseq"] == 1:
                                    o = ve[:, 2 + lo - s0:2 + hi - s0]
                                    a = pb[:, lo - c0:hi - c0]
                                    b_ = cbuf[:, lo:hi]
                                else:
                                    assert lo == s0 and hi == s0 + sg["nseq"] * L
                                    o = bt(ve, L + 2)[:, :, 2:2 + L]
                                    a = bt(pb[:, lo - c0:hi - c0], L)
                                    b_ = bt(cbuf[:, lo:hi], L)
                                I("dve", "tensor_tensor", [pk, ("cbuf", ti)], [("vext", si)], out=o, in0=a, in1=b_, op=ALU.mult)
                        elif pname == "b":
                            I("dve", "tensor_tensor", [pk] + [("ybuf", si) for si in range(len(segs))], [("tbuf", ti)],
                              out=tbuf[:, c0:c0 + n], in0=pb[:, :n], in1=ybuf[:, c0:c0 + n], op=ALU.mult)
                        else:
                            I("act", "activation", [pk], [("szb", ti)], out=szb[:, c0:c0 + n], in_=pb[:, :n], func=AF.Silu)
                            I("pool", "tensor_tensor", [("tbuf", ti), ("szb", ti)], [("gT", j)],
                              out=gT[:, j, c0:c0 + n], in0=tbuf[:, c0:c0 + n], in1=szb[:, c0:c0 + n], op=ALU.mult)
            G.barrier()
            A.release(mL)
            out_proj_residual(woc, l * 8 * 128, gT, lambda kc: ("gT", kc), xT, xkey, tcs, NT, V_POST + l * 8)
            G.barrier()
            A.release(mL)
        mL = A.mark()
        hT = gT
        hkey = lambda kc: ("gT", kc)
        craw = v3(A.alloc(2 * NT), 2)
        tabb = A.alloc(NT)
        krf = A.alloc(NT)
        krt = A.alloc(NT)
        ctr = Ring("ctok", [A.alloc(256) for _ in range(2)])
        ktr = Ring("ktok", [A.alloc(32) for _ in range(2)])
        for (tsrc, c0, n) in tabs:
            I("sp", "dma_start", [], ["tabb"], out=tabb[:, c0:c0 + n], in_=tsrc, slot="tabb")
        rst = rms_stats(xT, xkey, 8, tcs, D)
        norm_to(hT, hkey, xT, xkey, 8, tcs, rst, V_KVN)
        for ocl in range(3):
            wk, w = load_w(wdkv, ocl * 128)
            for ti, (c0, n) in enumerate(tcs):
                pk, pb = proj(wk, w, 8, hT, hkey, c0, n)
                if ocl < 2:
                    I("act", "activation", [pk], [("craw", ocl, ti)], out=craw[:, ocl, c0:c0 + n], in_=pb[:, :n], func=AF.Copy)
                    qk, q = sqr.next()
                    I("act", "activation", [pk], [qk], out=q[:, :n], in_=pb[:, :n], func=AF.Square)
                    I("pe", "matmul", [qk, "onesb"], [("ssq", ti % 2)], ssq_banks[ti % 2][:, :n], onesb, q[:, :n], start=(ocl == 0), stop=(ocl == 1))
                else:
                    I("dve", "tensor_tensor", [pk, "tabb"], [("krf", ti)], out=krf[64:96, c0:c0 + n], in0=pb[64:96, :n], in1=tabb[64:96, c0:c0 + n], op=ALU.mult)
                    I("dve", "tensor_tensor", [pk, "tabb"], [("krt", ti)], out=krt[64:96, c0:c0 + n], in0=pb[96:128, :n], in1=tabb[96:128, c0:c0 + n], op=ALU.mult)
                    I("pool", "tensor_tensor", [("krf", ti), ("krt", ti)], [("krf", ti)], out=krf[64:96, c0:c0 + n], in0=krf[64:96, c0:c0 + n], in1=krt[64:96, c0:c0 + n], op=ALU.add)
        for ti, (c0, n) in enumerate(tcs):
            rk, r = rstd_from(ssq_banks[ti % 2], ("ssq", ti % 2), n, 256)
            for ocl in range(2):
                I("dve", "scalar_tensor_tensor", [("craw", ocl, ti), rk, "vec"], [("craw", ocl, ti)],
                  out=craw[:, ocl, c0:c0 + n], in0=craw[:, ocl, c0:c0 + n], scalar=vcol(V_LAT + ocl), in1=r[:, :n], op0=ALU.mult, op1=ALU.mult)
        for dd in dests:
            c0, n = dd["c0"], dd["n"]
            tis = sorted(set(ti for ti, (a, m) in enumerate(tcs) if a < c0 + n and a + m > c0))
            rkeys = [("craw", ocl, ti) for ocl in range(2) for ti in tis]
            kkeys = [("krf", ti) for ti in tis]
            for ocl in range(2):
                I("act", "activation", rkeys, [dd["cT_key"]], out=dd["cT"][:, ocl, :], in_=craw[:, ocl, c0:c0 + n], func=AF.Copy)
            for (kap, kkey) in dd["krT"]:
                I("act", "activation", kkeys, [kkey], out=kap, in_=krf[64:96, c0:c0 + n], func=AF.Copy)
            for t0 in range(0, n, 128):
                tw = min(128, n - t0)
                ck, cb_ = ctr.next()
                kk, kb_ = ktr.next()
                pk, pb = trring.next()
                for ocl in range(2):
                    I("pe", "transpose", rkeys + ["ident"], [pk], pb[:tw, ocl * 128:(ocl + 1) * 128], craw[:, ocl, c0 + t0:c0 + t0 + tw], ident)
                I("pe", "transpose", kkeys + ["ident"], [pk], pb[:tw, 256:288], krf[64:96, c0 + t0:c0 + t0 + tw], ident[64:96, 64:96])
                I("act", "activation", [pk], [ck], out=cb_[:tw, :], in_=pb[:tw, 0:256], func=AF.Copy)
                I("dve", "tensor_copy", [pk], [kk], out=kb_[:tw, :], in_=pb[:tw, 256:288])
                r0 = dd["row0"] + t0
                I("sp", "dma_start", [ck], [("out", dd["name"], "c", r0)], out=dd["ckv_out"][r0:r0 + tw, :], in_=cb_[:tw, :], slot="octok%d" % ck[1])
                I("sp", "dma_start", [kk], [("out", dd["name"], "k", r0)], out=dd["kr_out"][r0:r0 + tw, :], in_=kb_[:tw, :], slot="oktok%d" % kk[1])
                if dd.get("tok_bf") is not None:
                    I("dve", "tensor_copy", [pk], ["cns_tok"], out=dd["tok_bf"][:tw, :], in_=pb[:tw, 0:256])
        if own_slot is not None:
            s = own_slot
            for kc in range(8):
                xo = x_own[:, kc, s * 512:(s + 1) * 512]
                I("dve", "tensor_scalar", [("xT", kc), "vec"], [("x_own", s, kc)], out=xo, in0=xT[:, kc, 0:512], scalar1=vcol(V_BLEND + 2 * s), scalar2=None, op0=ALU.mult)
                I("dve", "scalar_tensor_tensor", [("xT", kc), "vec", ("x_own", s, kc)], [("x_own", s, kc)],
                  out=xo, in0=xT[:, kc, 512:1024], scalar=vcol(V_BLEND + 2 * s + 1), in1=xo, op0=ALU.mult, op1=ALU.add)
        else:
            for kc in range(8):
                I("act", "activation", [("xT", kc)], [("xs", kc)], out=xsT[:, kc, :], in_=xT[:, kc, 16:80], func=AF.Copy)
        G.barrier()
        A.release(mG)

    mS0 = A.mark()
    scb = A.alloc(1024)
    I("sp", "dma_start", [], ["scb"], out=scb[:64, :], in_=sconv, slot="scb")
    transposes_to(lambda cb: (stT[:, cb * 64:(cb + 1) * 64], "stT"), scb, "scb", 64, 1024, ident, ["act", "dve"])
    G.barrier()
    A.release(mS0)

    run_group(
        0, 80,
        [dict(col0=0, nseq=1, L=16, kind="meta"), dict(col0=16, nseq=NB, L=LS, kind="sample")],
        [(meta, 16, 0), (xs, 64, 16)],
        [(mtab, 0, 80)],
        [dict(c0=0, n=16, cT=cT[:, :, SEQ:SEQ + 16], cT_key=("cT", "m"), krT=[(Kbuf[i][64:96, SEQ:SEQ + 16], ("Kb", i, "kr")) for i in range(2)],
              ckv_out=ckv_p, kr_out=kr_p, row0=0, name="p"),
         dict(c0=16, n=64, cT=cTs, cT_key="cTs", krT=[(krTs[0:32, :], "krTs")],
              ckv_out=ckv_s, kr_out=kr_s, row0=0, name="s", tok_bf=cns_tok)],
        None)
    for g in range(4):
        run_group(
            1 + g, 1024,
            [dict(col0=0, nseq=1, L=1024, kind="prompt", last=(g == 3))],
            [(xp[g * 1024 + i * 128:g * 1024 + (i + 1) * 128, :], 128, i * 128) for i in range(8)],
            [(ktab[:, g * 1024:(g + 1) * 1024], 0, 1024)],
            [dict(c0=0, n=1024, cT=cT[:, :, g * 1024:(g + 1) * 1024], cT_key=("cT", g), krT=[(Kbuf[i][64:96, g * 1024:(g + 1) * 1024], ("Kb", i, "kr")) for i in range(2)],
                  ckv_out=ckv_p, kr_out=kr_p, row0=16 + g * 1024, name="p")],
            g)

    mO = A.mark()
    cvo = A.alloc(1024)
    cso = A.alloc(1024)
    for j in range(8):
        pk, pb = trring.next()
        I("pe", "transpose", ["cvp", "ident"], [pk], pb[:4, 0:128], cvp[:, j * 4:(j + 1) * 4], ident)
        I("pe", "transpose", ["cvs", "ident"], [pk], pb[:64, 128:256], cvs[:, j * 64:(j + 1) * 64], ident)
        I("dve", "tensor_copy", [pk], ["cvo"], out=cvo[:4, j * 128:(j + 1) * 128], in_=pb[:4, 0:128])
        I("act", "activation", [pk], ["cso"], out=cso[:64, j * 128:(j + 1) * 128], in_=pb[:64, 128:256], func=AF.Copy)
    I("sp", "dma_start", ["cvo"], [("out", "conv_p")], out=conv_p, in_=cvo[:4, :], slot="o_cvo")
    I("sp", "dma_start", ["cso"], [("out", "conv_s")], out=conv_s, in_=cso[:64, :], slot="o_cso")
    G.barrier()
    A.release(mO)
    A.release(mA)


    mB = A.mark()
    wring = Ring("w", [v3(A.alloc(1024, BF16), 8) for _ in range(4)])
    sqr = Ring("sq", [A.alloc(512, BF16) for _ in range(2)])
    rsr = Ring("rs", [A.alloc(512) for _ in range(2)])
    Vh = v3(A.alloc(33 * 128, BF16), 33)
    maskb = [A.alloc(4096, BF16) for _ in range(2)]
    kmax2 = A.alloc(16)
    half = A.alloc(1)
    wqr = Ring("wq", [v3(A.alloc(512, BF16), 4) for _ in range(2)])
    wkr = Ring("wkh", [v3(A.alloc(128, BF16), 2) for _ in range(2)])
    wvr = Ring("wvh", [v3(A.alloc(128, BF16), 2) for _ in range(2)])
    poring = Ring("ssq", [banks[4], banks[5]])
    I("dve", "memset", [], ["Vh"], Vh.rearrange("p a b -> p (a b)"), 1.0)
    I("dve", "memset", [], ["half"], half, 0.5)
    for par in range(2):
        I("pool", "dma_start", [], ["maskb"], out=maskb[par], in_=masks[par * 128:(par + 1) * 128, :], slot="maskb%d" % par)
    wuk3 = wuk.rearrange("p (kc f) -> p kc f", kc=2)
    wuv3 = wuv.rearrange("p (kc f) -> p kc f", kc=2)
    print("arena words used before phase-B halves:", A.top, "of", A.n)

    def mla_prompt(j, hf):
        NT = 1024
        tcs = chunks(NT)
        xT = x_own[:, :, hf * 1024:(hf + 1) * 1024]
        xkey = lambda kc: ("xo", hf, kc)
        first = (j == 0 and hf == 0)
        mH = A.mark()
        ogT = v3(A.alloc(8 * NT, BF16), 8)
        ogkey = lambda kc: ("og", kc)
        qln = v3(A.alloc(4 * NT, BF16), 4)
        qlnkey = lambda kc: ("qln", kc)
        mX = A.mark()
        hT = v3(A.alloc(8 * NT, BF16), 8)
        hkey = lambda kc: ("hT", kc)
        qraw = v3(A.alloc(4 * NT), 4)
        rst = rms_stats(xT, xkey, 8, tcs, D)
        norm_to(hT, hkey, xT, xkey, 8, tcs, rst, V_PRE + (2 + j) * 8)
        for oc in range(12):
            wk, w = load_w(wim, (j * 12 + oc) * 128)
            for ti, (c0, n) in enumerate(tcs):
                pk, pb = proj(wk, w, 8, hT, hkey, c0, n)
                if oc < 4:
                    I("act", "activation", [pk], [("qraw", oc, ti)], out=qraw[:, oc, c0:c0 + n], in_=pb[:, :n], func=AF.Copy)
                    qk, q = sqr.next()
                    I("act", "activation", [pk], [qk], out=q[:, :n], in_=pb[:, :n], func=AF.Square)
                    I("pe", "matmul", [qk, "onesb"], [("ssq", ti % 2)], ssq_banks[ti % 2][:, :n], onesb, q[:, :n], start=(oc == 0), stop=(oc == 3))
                else:
                    I("act", "activation", [pk], [("og", oc - 4)], out=ogT[:, oc - 4, c0:c0 + n], in_=pb[:, :n], func=AF.Silu)
            if oc == 3:
                for ti, (c0, n) in enumerate(tcs):
                    rk, r = rstd_from(ssq_banks[ti % 2], ("ssq", ti % 2), n, 512)
                    for kc in range(4):
                        I("dve", "scalar_tensor_tensor", [("qraw", kc, ti), rk, "vec"], [("qln", kc)],
                          out=qln[:, kc, c0:c0 + n], in0=qraw[:, kc, c0:c0 + n], scalar=vcol(V_QN + j * 4 + kc), in1=r[:, :n], op0=ALU.mult, op1=ALU.mult)
        G.barrier()
        A.release(mX)
        Qr = Ring("Qh", [A.alloc(NT, BF16) for _ in range(2)])
        Ptr = Ring("Pt", [A.alloc(512, BF16) for _ in range(4)])
        qtabb = A.alloc(NT)
        rsum = Ring("rsum", [A.alloc(512) for _ in range(2)])
        tmpf = Ring("tmpf", [A.alloc(512) for _ in range(2)])
        t1r = Ring("t1", [A.alloc(512) for _ in range(2)])
        t2r = Ring("t2", [A.alloc(512) for _ in range(2)])
        negr = Ring("negm", [A.alloc(1) for _ in range(2)])
        qmx = A.alloc(4)
        kmx = A.alloc(16)
        I("sp", "dma_start", [], ["qtabb"], out=qtabb, in_=qtab[:, hf * 1024:(hf + 1) * 1024], slot="qtabb")
        for (qk_, qb_) in [Qr.next(), Qr.next()]:
            I("pool", "memset", [], [qk_], qb_, 0.0)
        for h in range(16):
            hb = (h % 2) * 64
            pr = h // 2
            Kk = ("Kb", h % 2)
            Kh = Kbuf[h % 2]
            wqk, wq = wqr.next()
            I("pool", "dma_start", [], [wqk], out=wq.rearrange("p a b -> p (a b)"), in_=wuq[(j * 16 + h) * 128:(j * 16 + h + 1) * 128, :], slot="wq%d" % wqk[1])
            wkk, wkh = wkr.next()
            I("pool", "dma_start", [], [wkk], out=wkh, in_=wuk3[:, :, h * 64:(h + 1) * 64], slot="wkh%d" % wkk[1])
            wvk, wvh = wvr.next()
            I("pool", "dma_start", [], [wvk], out=wvh, in_=wuv3[:, :, h * 64:(h + 1) * 64], slot="wvh%d" % wvk[1])
            Qk, Qh = Qr.next()
            for ti, (c0, n) in enumerate(tcs):
                pk, pb = proj(wqk, wq, 4, qln, qlnkey, c0, n, ring=trring)
                I("act", "activation", [pk], [Qk], out=Qh[0:64, c0:c0 + n], in_=pb[0:64, :n], func=AF.Copy)
                k1, t1 = t1r.next()
                k2, t2 = t2r.next()
                I("dve", "tensor_tensor", [pk, "qtabb"], [k1], out=t1[64:96, :n], in0=pb[64:96, :n], in1=qtabb[64:96, c0:c0 + n], op=ALU.mult)
                I("dve", "tensor_tensor", [pk, "qtabb"], [k2], out=t2[64:96, :n], in0=pb[96:128, :n], in1=qtabb[96:128, c0:c0 + n], op=ALU.mult)
                I("pool", "tensor_tensor", [k1, k2], [Qk], out=Qh[64:96, c0:c0 + n], in0=t1[64:96, :n], in1=t2[64:96, :n], op=ALU.add)
            for ci, (c0, n) in enumerate(chunks(T)):
                pk, pb = trring.next()
                for kc in range(2):
                    I("pe", "matmul", [wkk, "cT"], [pk], pb[0:64, :n], wkh[:, kc, :], cT[:, kc, c0:c0 + n], start=(kc == 0), stop=(kc == 1))
                I("dve", "tensor_copy", [pk], [Kk], out=Kh[0:64, c0:c0 + n], in_=pb[0:64, :n])
            for g0 in range(0, 33, 8):
                pk, pb = trring.next()
                nb = min(8, 33 - g0)
                for i in range(nb):
                    kb = g0 + i
                    kk = 128 if kb < 32 else 16
                    for kc in range(2):
                        I("pe", "matmul", [wvk, "cT"], [pk], pb[:kk, i * 64:(i + 1) * 64], cT[:, kc, kb * 128:kb * 128 + kk], wvh[:, kc, :], start=(kc == 0), stop=(kc == 1))
                if nb == 8:
                    I("dve", "tensor_copy", [pk], ["Vh"], out=Vh[:, g0:g0 + 8, 0:64], in_=pb[:, :512].rearrange("p (a b) -> p a b", a=8))
                else:
                    I("dve", "tensor_copy", [pk], ["Vh"], out=Vh[:16, 32, 0:64], in_=pb[:16, 0:64])
            if first:
                for ci, (c0, n) in enumerate(chunks(T)):
                    qk, q = sqr.next()
                    I("act", "activation", [Kk, ("Kb", h % 2, "kr")], [qk], out=q[0:96, :n], in_=Kh[0:96, c0:c0 + n], func=AF.Square)
                    pk, pb = trring.next()
                    I("pe", "matmul", [qk, "onesb"], [pk], pb[:, :n], onesb[0:96, :], q[0:96, :n], start=True, stop=True)
                    I("dve", "tensor_reduce", [pk], [("kmx", ci)], out=kmx[:, ci:ci + 1], in_=pb[:, :n], axis=AX.X, op=ALU.max)
                I("dve", "tensor_reduce", [("kmx", ci) for ci in range(9)], ["kmax2"], out=kmax2[:, h:h + 1], in_=kmx[:, 0:9], axis=AX.X, op=ALU.max)
            for ti, (c0, n) in enumerate(tcs):
                qk, q = sqr.next()
                I("act", "activation", [Qk], [qk], out=q[0:96, :n], in_=Qh[0:96, c0:c0 + n], func=AF.Square)
                pk, pb = trring.next()
                I("pe", "matmul", [qk, "onesb"], [pk], pb[:, :n], onesb[0:96, :], q[0:96, :n], start=True, stop=True)
                I("dve", "tensor_reduce", [pk], [("qmx", ti)], out=qmx[:, ti:ti + 1], in_=pb[:, :n], axis=AX.X, op=ALU.max)
            nk, negm = negr.next()
            I("dve", "tensor_tensor", [("qmx", 0), ("qmx", 1)], [nk], out=negm, in0=qmx[:, 0:1], in1=qmx[:, 1:2], op=ALU.max)
            I("dve", "tensor_tensor", [nk, "kmax2"], [nk], out=negm, in0=negm, in1=kmax2[:, h:h + 1], op=ALU.mult)
            I("act", "activation", [nk], [nk], out=negm, in_=negm, func=AF.Ln)
            I("act", "activation", [nk], [nk], out=negm, in_=negm, func=AF.Exp, scale=0.5)
            I("dve", "tensor_scalar", [nk], [nk], out=negm, in0=negm, scalar1=-SCALE * 1.02, scalar2=None, op0=ALU.mult)
            LAG = 2
            items = []
            for sl in range(2):
                sg = 2 * hf + sl
                seq = list(range(8 * sg + 8)) + [32]
                for idx, kb in enumerate(seq):
                    items.append((sl, sg, idx, kb, len(seq)))
            pos = {}
            info = {}

            def qk_stage(it):
                sl, sg, idx, kb, nseq = it
                if idx == 0:
                    pos[sl] = poring.next()
                kk = 128 if kb < 32 else 16
                q0 = sl * 512
                pk, pb = psring.next()
                I("pe", "matmul", [Kk, ("Kb", h % 2, "kr"), Qk], [pk], pb[:kk, :512], Kh[:, kb * 128:kb * 128 + kk], Qh[:, q0:q0 + 512], start=True, stop=True)
                ptk, pt = Ptr.next()
                I("act", "activation", [pk, nk], [ptk], out=pt[:kk, :], in_=pb[:kk, :512], func=AF.Exp, scale=SCALE, bias=negm[:kk, :])
                if kb < 32 and kb >= 8 * sg:
                    mi = kb - 8 * sg
                    I("pool", "tensor_tensor", [ptk, "maskb"], [ptk], out=pt, in0=pt, in1=maskb[sg % 2][:, mi * 512:(mi + 1) * 512], op=ALU.mult)
                info[it] = (ptk, pt, kk)

            def pv_stage(it):
                sl, sg, idx, kb, nseq = it
                ptk, pt, kk = info.pop(it)
                pok, po = pos[sl]
                q0 = sl * 512
                I("pe", "matmul", [ptk, "Vh"], [pok], po[:, :512], Vh[:kk, kb, :], pt[:kk, :], start=(idx == 0), stop=(idx == nseq - 1))
                if idx == nseq - 1:
                    rsk, rs = rsum.next()
                    tmk, tm = tmpf.next()
                    I("act", "activation", [pok], [rsk], out=rs[hb:hb + 64, :], in_=po[64:128, :512], func=AF.Ln)
                    I("act", "activation", [rsk], [rsk], out=rs[hb:hb + 64, :], in_=rs[hb:hb + 64, :], func=AF.Exp, scale=-1.0)
                    I("dve", "tensor_tensor", [pok, rsk], [tmk], out=tm[hb:hb + 64, :], in0=po[0:64, :512], in1=rs[hb:hb + 64, :], op=ALU.mult)
                    I("pool", "tensor_tensor", [tmk, ("og", pr)], [("og", pr)], out=ogT[hb:hb + 64, pr, q0:q0 + 512], in0=tm[hb:hb + 64, :], in1=ogT[hb:hb + 64, pr, q0:q0 + 512], op=ALU.mult)

            for i in range(len(items) + LAG):
                if i < len(items):
                    qk_stage(items[i])
                if i - LAG >= 0:
                    pv_stage(items[i - LAG])
        G.barrier()
        A.release(mX)
        out_proj_residual(wom, j * 8 * 128, ogT, ogkey, xT, xkey, tcs, NT, V_POST + (2 + j) * 8)
        G.barrier()
        A.release(mH)

    if cfg.do_b:
        for j in range(2):
            for hf in range(2):
                mla_prompt(j, hf)
        mY = A.mark()
        ytr = Ring("ytok", [A.alloc(1024) for _ in range(2)])
        for tt in range(16):
            yk, yb = ytr.next()
            for g0 in range(0, 8, 4):
                pk, pb = trring.next()
                for i in range(4):
                    kc = g0 + i
                    I("pe", "transpose", [("xo", tt // 8, kc), "ident"], [pk], pb[:, i * 128:(i + 1) * 128], x_own[:, kc, tt * 128:(tt + 1) * 128], ident)
                copy_any("act" if g0 == 0 else "dve", [pk], [yk], yb[:, g0 * 128:(g0 + 4) * 128], pb[:, :512])
            I("sp", "dma_start", [yk], [("out", "y", tt)], out=y_own[tt * 128:(tt + 1) * 128, :], in_=yb, slot="oy%d" % yk[1])
        G.barrier()
        A.release(mY)
    A.release(mB)
    G.barrier()
    A.release(mSP)

    def bfv(bank):
        return bank[:, :].bitcast(BF16)

    if cfg.do_s:
        wring = Ring("w", [v3(A.alloc(1024, BF16), 8) for _ in range(4)])
        sqr = Ring("sq", [A.alloc(512, BF16) for _ in range(2)])
        rsr = Ring("rs", [A.alloc(512) for _ in range(2)])
        wqr = Ring("wq", [v3(A.alloc(512, BF16), 4) for _ in range(2)])
        wvr = Ring("wvh", [v3(A.alloc(128, BF16), 2) for _ in range(2)])
        wukT_sb = v3(A.alloc(16 * 256, BF16), 16)
        stab = A.alloc(64)
        smask_sb = A.alloc(NB * 64)
        ptxi = A.alloc(NB * NGRP, I32)
        ptxf = A.alloc(NB * NGRP)
        idxi = A.alloc(NB * NGRP, I32)
        I("pool", "dma_start", [], ["wukT"], out=wukT_sb[0:64].rearrange("p a b -> p (a b)"), in_=wukT, slot="wukT")
        I("sp", "dma_start", [], ["stab"], out=stab, in_=mtab[:, 16:80], slot="stab")
        I("sp", "dma_start", [], ["smask"], out=smask_sb[0:64, :], in_=smask, slot="smask")
        I("sp", "dma_start", [], ["ptxi"], out=ptxi, in_=ptx, slot="ptx")
        I("dve", "tensor_copy", ["ptxi"], ["ptxf"], out=ptxf, in_=ptxi)
        I("dve", "tensor_scalar", ["ptxf", "vec"], ["ptxf"], out=ptxf, in0=ptxf, scalar1=16.0, scalar2=vcol(V_R16), op0=ALU.mult, op1=ALU.add)
        I("dve", "tensor_copy", ["ptxf"], ["idx"], out=idxi, in_=ptxf)
        wuv3 = wuv.rearrange("p (kc f) -> p kc f", kc=2)
        NT = NS
        tcs = [(0, NS)]
        xkey = lambda kc: ("xs", kc)
        ogT = v3(A.alloc(8 * NT, BF16), 8)
        ogkey = lambda kc: ("ogs", kc)
        qln = v3(A.alloc(4 * NT, BF16), 4)
        qlnkey = lambda kc: ("qlns", kc)
        hT = v3(A.alloc(8 * NT, BF16), 8)
        hkey = lambda kc: ("hTs", kc)
        qraw = v3(A.alloc(4 * NT), 4)
        QA = [v3(A.alloc(16 * 64, BF16), 16) for _ in range(3)]
        OL = [v3(A.alloc(16 * 64, BF16), 16) for _ in range(2)]
        qnr = Ring("qn", [A.alloc(64, BF16) for _ in range(2)])
        t1r = Ring("t1s", [A.alloc(64) for _ in range(2)])
        t2r = Ring("t2s", [A.alloc(64) for _ in range(2)])
        gtr = Ring("gt", [A.alloc(2048, BF16) for _ in range(3)])
        gkr = Ring("gk", [A.alloc(256, BF16) for _ in range(3)])
        KT = [A.alloc(1024, BF16) for _ in range(3)]
        Pr = Ring("P", [A.alloc(1024, BF16) for _ in range(2)])
        PTr = Ring("PT", [A.alloc(512, BF16) for _ in range(2)])
        Qbr = Ring("Qb", [v3(A.alloc(3 * 64, BF16), 3) for _ in range(2)])
        oacc = A.alloc(256)
        obf = A.alloc(256, BF16)
        Sn = A.alloc(64)
        sm = A.alloc(16)
        print("arena words used in sample phase:", A.top, "of", A.n)
        ktb = [bfv(banks[0]), bfv(banks[1]), bfv(banks[2])]
        ptb = bfv(banks[3])
        Sring = Ring("ssq", [banks[4], banks[5]])
        pvb = banks[6]
        msb = banks[7]

        def col(i):
            return sm[0:64, i:i + 1]

        for j in range(2):
            rst = rms_stats(xsT, xkey, 8, tcs, D)
            norm_to(hT, hkey, xsT, xkey, 8, tcs, rst, V_PRE + (2 + j) * 8)
            for oc in range(12):
                wk, w = load_w(wim, (j * 12 + oc) * 128)
                pk, pb = proj(wk, w, 8, hT, hkey, 0, NT)
                if oc < 4:
                    I("act", "activation", [pk], [("qraws", oc)], out=qraw[:, oc, :], in_=pb[:, :NT], func=AF.Copy)
                    qk, q = sqr.next()
                    I("act", "activation", [pk], [qk], out=q[:, :NT], in_=pb[:, :NT], func=AF.Square)
                    I("pe", "matmul", [qk, "onesb"], [("ssq", 0)], ssq_banks[0][:, :NT], onesb, q[:, :NT], start=(oc == 0), stop=(oc == 3))
                else:
                    I("act", "activation", [pk], [("ogs", oc - 4)], out=ogT[:, oc - 4, :], in_=pb[:, :NT], func=AF.Silu)
                if oc == 3:
                    rk, r = rstd_from(ssq_banks[0], ("ssq", 0), NT, 512)
                    for kc in range(4):
                        I("dve", "scalar_tensor_tensor", [("qraws", kc), rk, "vec"], [("qlns", kc)],
                          out=qln[:, kc, :], in0=qraw[:, kc, :], scalar=vcol(V_QN + j * 4 + kc), in1=r[:, :NT], op0=ALU.mult, op1=ALU.mult)
            for h in range(16):
                wqk, wq = wqr.next()
                I("pool", "dma_start", [], [wqk], out=wq.rearrange("p a b -> p (a b)"), in_=wuq[(j * 16 + h) * 128:(j * 16 + h + 1) * 128, :], slot="wq%d" % wqk[1])
                pk, pb = proj(wqk, wq, 4, qln, qlnkey, 0, NT, ring=trring)
                qnk, qn = qnr.next()
                I("act", "activation", [pk], [qnk], out=qn[0:64, :], in_=pb[0:64, :NT], func=AF.Copy)
                k1, t1 = t1r.next()
                k2, t2 = t2r.next()
                I("dve", "tensor_tensor", [pk, "stab"], [k1], out=t1[64:96, :], in0=pb[64:96, :NT], in1=stab[64:96, :], op=ALU.mult)
                I("dve", "tensor_tensor", [pk, "stab"], [k2], out=t2[64:96, :], in0=pb[96:128, :NT], in1=stab[96:128, :], op=ALU.mult)
                I("pool", "tensor_tensor", [k1, k2], [("QA", 2)], out=QA[2][0:32, h, :], in0=t1[64:96, :], in1=t2[64:96, :], op=ALU.add)
                for c2 in range(2):
                    pk2, pb2 = psring.next()
                    I("pe", "matmul", [qnk, "wukT"], [pk2], pb2[:, :NT], wukT_sb[0:64, h, c2 * 128:(c2 + 1) * 128], qn[0:64, :], start=True, stop=True)
                    copy_any("act" if c2 == 0 else "dve", [pk2], [("QA", c2)], QA[c2][:, h, :], pb2[:, :NT])
            G.barrier()
            for b in range(NB):
                Qbk, Qb = Qbr.next()
                for c in range(3):
                    np_ = 128 if c < 2 else 32
                    I("pool", "tensor_copy", [("QA", c)], [Qbk], out=Qb[:np_, c, :].rearrange("p (h t) -> p h t", t=LS), in_=QA[c][:np_, :, b * LS:(b + 1) * LS])
                I("dve", "memset", [], ["m_old"], col(0), NEG)
                I("dve", "memset", [], ["l"], col(4), 0.0)
                I("dve", "memset", [], ["oacc"], oacc[0:64, :], 0.0)
                for g in range(NGRP + 1):
                    newk = (g == NGRP)
                    if not newk:
                        gk_, gt = gtr.next()
                        kk_, gkk = gkr.next()
                        icol = idxi[:, b * NGRP + g:b * NGRP + g + 1]
                        I("pool", "indirect_dma_start", ["idx"], [gk_], out=gt, out_offset=None, in_=cckv,
                          in_offset=bass.IndirectOffsetOnAxis(ap=icol, axis=0), bounds_check=cfg.n_pool * 16 - 1, oob_is_err=False, slot="gt%d" % gk_[1])
                        I("pool", "indirect_dma_start", ["idx"], [kk_], out=gkk, out_offset=None, in_=ckr,
                          in_offset=bass.IndirectOffsetOnAxis(ap=icol, axis=0), bounds_check=cfg.n_pool * 16 - 1, oob_is_err=False, slot="gk%d" % kk_[1])
                        for jj in range(8):
                            I("pe", "transpose", [gk_, "identb"], [("ps", 0)], ktb[0][:, jj * 128:(jj + 1) * 128], gt[:, jj * 256:jj * 256 + 128], identb)
                            I("pe", "transpose", [gk_, "identb"], [("ps", 1)], ktb[1][:, jj * 128:(jj + 1) * 128], gt[:, jj * 256 + 128:jj * 256 + 256], identb)
                            I("pe", "transpose", [kk_, "identb"], [("ps", 2)], ktb[2][0:32, jj * 128:(jj + 1) * 128], gkk[:, jj * 32:(jj + 1) * 32], identb)
                        I("dve", "tensor_copy", [("ps", 0)], [("KT", 0)], out=KT[0], in_=ktb[0])
                        I("act", "activation", [("ps", 1)], [("KT", 1)], out=KT[1], in_=ktb[1], func=AF.Copy)
                        I("dve", "tensor_copy", [("ps", 2)], [("KT", 2)], out=KT[2][0:32, :], in_=ktb[2][0:32, :])
                        nh = 2
                        Ssrc = []
                        for hh in range(2):
                            sk, sb = Sring.next()
                            I("pe", "matmul", [Qbk, ("KT", 0)], [sk], sb[0:64, :512], Qb[:, 0, :], KT[0][:, hh * 512:(hh + 1) * 512], start=True, stop=False)
                            I("pe", "matmul", [Qbk, ("KT", 1)], [sk], sb[0:64, :512], Qb[:, 1, :], KT[1][:, hh * 512:(hh + 1) * 512], start=False, stop=False)
                            I("pe", "matmul", [Qbk, ("KT", 2)], [sk], sb[0:64, :512], Qb[0:32, 2, :], KT[2][0:32, hh * 512:(hh + 1) * 512], start=False, stop=True)
                            I("dve", "tensor_reduce", [sk], [("gm", hh)], out=col(5 + hh), in_=sb[0:64, :512], axis=AX.X, op=ALU.max)
                            Ssrc.append((sk, sb[0:64, :512], 512))
                    else:
                        sk = ("ps", 3)
                        sb = msb
                        I("pe", "matmul", [Qbk, "cTs"], [("pt", 1)], sb[0:64, :64], Qb[:, 0, :], cTs[:, 0, :], start=True, stop=False)
                        I("pe", "matmul", [Qbk, "cTs"], [("pt", 1)], sb[0:64, :64], Qb[:, 1, :], cTs[:, 1, :], start=False, stop=False)
                        I("pe", "matmul", [Qbk, "krTs"], [("pt", 1)], sb[0:64, :64], Qb[0:32, 2, :], krTs[0:32, :], start=False, stop=True)
                        I("dve", "tensor_tensor", [("pt", 1), "smask"], ["Sn"], out=Sn[0:64, :], in0=sb[0:64, :64], in1=smask_sb[0:64, b * 64:(b + 1) * 64], op=ALU.add)
                        I("dve", "tensor_reduce", ["Sn"], [("gm", 0)], out=col(5), in_=Sn[0:64, :], axis=AX.X, op=ALU.max)
                        I("dve", "tensor_copy", [("gm", 0)], [("gm", 1)], out=col(6), in_=col(5))
                        nh = 1
                        Ssrc = [("Sn", Sn[0:64, :], 64)]
                    I("dve", "tensor_tensor", [("gm", 0), ("gm", 1)], ["m_new"], out=col(1), in0=col(5), in1=col(6), op=ALU.max)
                    I("dve", "tensor_tensor", ["m_new", "m_old"], ["m_new"], out=col(1), in0=col(1), in1=col(0), op=ALU.max)
                    I("dve", "tensor_scalar", ["m_new"], ["negb"], out=col(2), in0=col(1), scalar1=-SCALE, scalar2=None, op0=ALU.mult)
                    I("act", "activation", ["m_old", "negb"], ["alpha"], out=col(3), in_=col(0), func=AF.Exp, scale=SCALE, bias=col(2))
                    I("dve", "memset", [], ["rs"], sm[0:64, 7:9], 0.0)
                    Pk, P = Pr.next()
                    off = 0
                    for hi, (sk, sap, w_) in enumerate(Ssrc):
                        I("act", "activation", [sk, "negb", "rs"], [Pk, "rs"] if hi == len(Ssrc) - 1 else [Pk, "rs"], out=P[0:64, off:off + w_], in_=sap, func=AF.Exp, scale=SCALE, bias=col(2), accum_out=col(7 + hi))
                        off += w_
                    nblk = 8 if not newk else 1
                    bw = 128 if not newk else 64
                    PTk, PT = PTr.next()
                    for jj in range(nblk):
                        I("pe", "transpose", [Pk, "identb"], [("pt", 0)], ptb[:bw, jj * 64:(jj + 1) * 64], P[0:64, jj * bw:(jj + 1) * bw], identb[0:64, 0:64])
                    I("dve", "tensor_copy", [("pt", 0)], [PTk], out=PT[:bw, :nblk * 64], in_=ptb[:bw, :nblk * 64])
                    for jj in range(nblk):
                        if not newk:
                            I("pe", "matmul", [PTk, gk_], [("pv",)], pvb[0:64, :256], PT[:, jj * 64:(jj + 1) * 64], gt[:, jj * 256:(jj + 1) * 256], start=(jj == 0), stop=(jj == nblk - 1))
                        else:
                            I("pe", "matmul", [PTk, "cns_tok"], [("pv",)], pvb[0:64, :256], PT[0:64, 0:64], cns_tok[0:64, :], start=True, stop=True)
                    I("dve", "scalar_tensor_tensor", [("pv",), "alpha", "oacc"], ["oacc"], out=oacc[0:64, :], in0=oacc[0:64, :], scalar=col(3), in1=pvb[0:64, :256], op0=ALU.mult, op1=ALU.add)
                    I("dve", "scalar_tensor_tensor", ["l", "alpha", "rs"], ["l"], out=col(4), in0=col(4), scalar=col(3), in1=col(7), op0=ALU.mult, op1=ALU.add)
                    if nh == 2:
                        I("dve", "tensor_tensor", ["l", "rs"], ["l"], out=col(4), in0=col(4), in1=col(8), op=ALU.add)
                    I("dve", "tensor_copy", ["m_new"], ["m_old"], out=col(0), in_=col(1))
                I("act", "activation", ["l"], ["rl"], out=col(9), in_=col(4), func=AF.Ln)
                I("act", "activation", ["rl"], ["rl"], out=col(9), in_=col(9), func=AF.Exp, scale=-1.0)
                I("dve", "tensor_scalar", ["oacc", "rl"], ["obf"], out=obf[0:64, :], in0=oacc[0:64, :], scalar1=col(9), scalar2=None, op0=ALU.mult)
                for c2 in range(2):
                    I("pe", "transpose", ["obf", "identb"], [("pt", 0)], ptb[:, c2 * 64:(c2 + 1) * 64], obf[0:64, c2 * 128:(c2 + 1) * 128], identb[0:64, 0:64])
                for c2 in range(2):
                    I("dve", "tensor_copy", [("pt", 0)], [("OL", c2)], out=OL[c2][:, :, b * LS:(b + 1) * LS], in_=ptb[:, c2 * 64:(c2 + 1) * 64].rearrange("p (h t) -> p h t", t=LS))
            G.barrier()
            for h in range(16):
                hb = (h % 2) * 64
                pr = h // 2
                wvk, wvh = wvr.next()
                I("pool", "dma_start", [], [wvk], out=wvh, in_=wuv3[:, :, h * 64:(h + 1) * 64], slot="wvh%d" % wvk[1])
                pk, pb = trring.next()
                for c2 in range(2):
                    I("pe", "matmul", [wvk, ("OL", c2)], [pk], pb[0:64, :NT], wvh[:, c2, :], OL[c2][:, h, :], start=(c2 == 0), stop=(c2 == 1))
                I("dve", "tensor_tensor", [pk, ("ogs", pr)], [("ogs", pr)], out=ogT[hb:hb + 64, pr, :], in0=pb[0:64, :NT], in1=ogT[hb:hb + 64, pr, :], op=ALU.mult)
            G.barrier()
            mO2 = A.mark()
            out_proj_residual(wom, j * 8 * 128, ogT, ogkey, xsT, xkey, tcs, NT, V_POST + (2 + j) * 8)
            G.barrier()
            A.release(mO2)
        ysb = A.alloc(1024)
        for g0 in range(0, 8, 4):
            pk, pb = trring.next()
            for i in range(4):
                I("pe", "transpose", [("xs", g0 + i), "ident"], [pk], pb[:NS, i * 128:(i + 1) * 128], xsT[:, g0 + i, :], ident)
            copy_any("act" if g0 == 0 else "dve", [pk], ["ysb"], ysb[:NS, g0 * 128:(g0 + 4) * 128], pb[:NS, :512])
        I("sp", "dma_start", ["ysb"], [("out", "ys")], out=ys, in_=ysb[:NS, :], slot="oys")

    G.barrier()
    G.add("sp", lambda e: None)
    print('total ops', G.n, {e: len(v) for e, v in G.ops.items()})
    G.emit(nc, stack, getattr(cfg, 'limit', None))
    stack.close()
    return nc


OWN_CHUNKS = {0: (0, 3, 4, 7), 1: (1, 2, 5, 6)}


def _chunked_w(w, ncols_chunk=128):
    K, N = w.shape
    a = w.reshape(K // 128, 128, N // 128, 128)
    a = a.transpose(2, 1, 0, 3)
    return np.ascontiguousarray(a).reshape(N // 128 * 128, K // 128 * 128)


def _rope_tab(pos):
    half = 16
    inv = (10000.0 ** (-np.arange(half, dtype=np.float32) / half)).astype(np.float32)
    ang = pos.astype(np.float32)[None, :] * inv[:, None]
    cos = np.cos(ang).astype(np.float32)
    sin = np.sin(ang).astype(np.float32)
    tab = np.zeros((128, len(pos)), np.float32)
    tab[64:80] = cos
    tab[80:96] = cos
    tab[96:112] = -sin
    tab[112:128] = sin
    return tab


def prep_inputs(cfg, x_prompt, x_sample, cache_ckv, cache_krope, state_conv, page_table, meta_tokens,
                pre_norm_g, post_norm_g, w_in_conv, conv_w, w_out_conv, kv_norm_g, w_dkv,
                kv_lat_norm_g, w_uk, w_uv, w_in_mla, q_norm_g, w_uq, w_out_mla):
    f = np.float32
    shared = {}
    shared["meta"] = np.ascontiguousarray(meta_tokens, f)
    shared["ident"] = np.eye(128, dtype=f)
    shared["wic"] = np.concatenate([_chunked_w(np.asarray(w_in_conv[l])) for l in range(2)], 0)
    shared["woc"] = np.concatenate([_chunked_w(np.asarray(w_out_conv[l])) for l in range(2)], 0)
    wd = np.asarray(w_dkv)
    wkr = wd[:, 256:288]
    wd2 = np.concatenate([np.zeros((1024, 64), f), wkr, wkr[:, 16:32], wkr[:, 0:16]], 1)
    shared["wdkv"] = np.concatenate([_chunked_w(wd[:, 0:128]), _chunked_w(wd[:, 128:256]), _chunked_w(wd2)], 0)
    shared["wim"] = np.concatenate([_chunked_w(np.asarray(w_in_mla[j])) for j in range(2)], 0)
    uq = []
    for j in range(2):
        w = np.asarray(w_uq[j]).reshape(512, 16, 96)
        wh = np.concatenate([w[:, :, 0:64], w[:, :, 64:96], w[:, :, 80:96], w[:, :, 64:80]], 2)
        for h in range(16):
            uq.append(_chunked_w(np.ascontiguousarray(wh[:, h, :])))
    shared["wuq"] = np.concatenate(uq, 0)
    shared["wom"] = np.concatenate([_chunked_w(np.asarray(w_out_mla[j])) for j in range(2)], 0)
    shared["wuk"] = np.ascontiguousarray(np.asarray(w_uk).reshape(2, 128, 1024).transpose(1, 0, 2)).reshape(128, 2048)
    shared["wuv"] = np.ascontiguousarray(np.asarray(w_uv).reshape(2, 128, 1024).transpose(1, 0, 2)).reshape(128, 2048)
    shared["wukT"] = np.ascontiguousarray(np.asarray(w_uk).transpose(2, 1, 0)).reshape(64, 16 * 256)
    kpos = np.concatenate([np.arange(16, T), np.arange(0, 16)])
    shared["ktab"] = _rope_tab(kpos)
    npg = cfg.n_pages
    past = npg * PAGE
    spos = np.tile(past + np.arange(LS), NB)
    shared["mtab"] = np.concatenate([_rope_tab(np.arange(16)), _rope_tab(spos)], 1)
    shared["cckv"] = np.asarray(cache_ckv).reshape(cfg.n_pool * 16, 2048)
    shared["ckr"] = np.asarray(cache_krope).reshape(cfg.n_pool * 16, 256)
    sm = np.full((64, NB, NB, LS), NEG, f)
    for b in range(NB):
        for t in range(LS):
            sm[np.arange(16) * 4 + t, b, b, 0:t + 1] = 0.0
    shared["smask"] = sm.reshape(64, NB * 64)

    def fm(v):
        return np.asarray(v, f).reshape(-1, 128).T

    vec = np.zeros((128, NV), f)
    for l in range(4):
        vec[:, V_PRE + l * 8:V_PRE + l * 8 + 8] = fm(pre_norm_g[l])
        vec[:, V_POST + l * 8:V_POST + l * 8 + 8] = fm(post_norm_g[l])
    vec[:, V_KVN:V_KVN + 8] = fm(kv_norm_g)
    vec[:, V_LAT:V_LAT + 2] = fm(kv_lat_norm_g)
    for j in range(2):
        vec[:, V_QN + j * 4:V_QN + j * 4 + 4] = fm(q_norm_g[j])
    vec[:, V_R16] = np.arange(128) % 16
    vec[:, V_EPS] = EPS
    for l in range(2):
        for k in range(3):
            vec[:, V_CW + l * 24 + k * 8:V_CW + l * 24 + k * 8 + 8] = fm(conv_w[l, k])

    ki = np.arange(128)[:, None]
    qi = np.arange(512)[None, :]
    pat = np.zeros((2, 128, 8, 512), f)
    for d in range(4):
        tri = ((d * 128 + ki) <= qi).astype(f)
        pat[0, :, d, :] = tri
        pat[1, :, d, :] = 1.0
        pat[1, :, 4 + d, :] = tri

    in_maps = []
    xp_all = np.asarray(x_prompt)
    xs_all = np.asarray(x_sample)
    sc_all = np.asarray(state_conv)
    pt_all = np.asarray(page_table)
    for c in range(NCORES):
        b, r = c // 2, c % 2
        m = dict(shared)
        m["xp"] = np.ascontiguousarray(xp_all[b])
        m["xs"] = np.ascontiguousarray(xs_all[c * NB:(c + 1) * NB].reshape(NS, D))
        m["sconv"] = np.ascontiguousarray(sc_all[:, c * NB:(c + 1) * NB].reshape(64, D))
        v = vec.copy()
        own = OWN_CHUNKS[r]
        for s in range(4):
            v[:, V_BLEND + 2 * s] = 1.0 if own[s] == 2 * s else 0.0
            v[:, V_BLEND + 2 * s + 1] = 1.0 if own[s] == 2 * s + 1 else 0.0
        m["vecs"] = v
        qpos = np.concatenate([16 + ch * 512 + np.arange(512) for ch in own])
        m["qtab"] = _rope_tab(qpos)
        mk = np.stack([pat[0 if own[par] == 2 * par else 1] for par in range(2)], 0)
        m["masks"] = np.ascontiguousarray(mk).reshape(2 * 128, 4096)
        pt = pt_all[c * NB:(c + 1) * NB]
        e = pt.reshape(NB, npg // 8, 8)
        e = np.repeat(e[:, :, :, None], 16, 3)
        m["ptx"] = np.ascontiguousarray(e.transpose(2, 3, 0, 1)).reshape(128, NB * (npg // 8)).astype(np.int32)
        in_maps.append(m)
    return in_maps


def assemble(res, cfg):
    y_prompt = np.zeros((4, SEQ, D), np.float32)
    y_sample = np.zeros((128, LS, D), np.float32)
    ckv_p = np.zeros((4, T, 256), np.float32)
    kr_p = np.zeros((4, T, 32), np.float32)
    conv_p = np.zeros((2, 4, 2, D), np.float32)
    ckv_s = np.zeros((128, LS, 256), np.float32)
    kr_s = np.zeros((128, LS, 32), np.float32)
    conv_s = np.zeros((2, 128, 2, D), np.float32)
    for c in range(NCORES):
        b, r = c // 2, c % 2
        o = res[c]
        for s, ch in enumerate(OWN_CHUNKS[r]):
            y_prompt[b, ch * 512:(ch + 1) * 512] = o["y_own"][s * 512:(s + 1) * 512]
        y_sample[c * NB:(c + 1) * NB] = o["ys"].reshape(NB, LS, D)
        if r == 0:
            ckv_p[b] = o["ckv_p"]
            kr_p[b] = o["kr_p"]
            conv_p[:, b] = o["conv_p"].reshape(2, 2, D)
        ckv_s[c * NB:(c + 1) * NB] = o["ckv_s"].reshape(NB, LS, 256)
        kr_s[c * NB:(c + 1) * NB] = o["kr_s"].reshape(NB, LS, 32)
        conv_s[:, c * NB:(c + 1) * NB] = o["conv_s"].reshape(2, NB, 2, D)
    return (y_prompt, y_sample, ckv_p, kr_p, conv_p, ckv_s, kr_s, conv_s)


_NC_CACHE = {}


def kernel(**inputs):
    cfg = Cfg(n_pool=inputs["cache_ckv"].shape[0], n_pages=inputs["page_table"].shape[1])
    key = (cfg.n_pool, cfg.n_pages)
    if key not in _NC_CACHE:
        _NC_CACHE[key] = build(cfg)
    nc = _NC_CACHE[key]
    in_maps = prep_inputs(cfg, **inputs)
    res = run_bass_kernel_spmd(nc, in_maps, core_ids=list(range(NCORES)))
    return assemble(res.results, cfg)
```

```python
import contextlib
import numpy as np
import concourse.bass as bass
import concourse.mybir as mybir
from concourse.bass_utils import run_bass_kernel_spmd

F32 = mybir.dt.float32
BF16 = mybir.dt.bfloat16
I32 = mybir.dt.int32
ALU = mybir.AluOpType
AF = mybir.ActivationFunctionType
AX = mybir.AxisListType

D = 1024
KC = 8
SEQ = 4096
NMETA = 16
T = SEQ + NMETA
NCORES = 8
NB = 16
LS = 4
NS = NB * LS
EPS = 1e-6
SCALE = 96 ** -0.5
PAGE = 128
NEG = -30000.0

ENGS = ("pe", "act", "dve", "pool", "sp")
import os
XENG = os.environ.get("XENG", "act,dve").split(",")
PSUM_KEYS = ("ps", "pt", "ssq", "sS")


class Op:
    __slots__ = ("eng", "fn", "deps", "signal", "semval", "is_dma", "slot", "idx")

    def __init__(self, eng, fn):
        self.eng = eng
        self.fn = fn
        self.deps = ()
        self.signal = False
        self.semval = 0
        self.is_dma = False
        self.slot = None


class Graph:
    def __init__(self):
        self.ops = {e: [] for e in ENGS}
        self.last_w = {}
        self.readers = {}
        self.slot_count = {}
        self.n = 0
        self.barrier_ops = []
        self.pending_dma = []
        self.last_on = {}

    def add(self, eng, fn, reads=(), writes=(), slot=None):
        op = Op(eng, fn)
        op.idx = self.n
        self.n += 1
        deps = {}

        def dep(o):
            if o is not None and o is not op:
                deps[id(o)] = o

        for k in reads:
            dep(self.last_w.get(k))
            if isinstance(k, tuple) and k[0] in PSUM_KEYS:
                r = self.readers.get(k)
                if r:
                    for ek, o in r.items():
                        if ek != eng:
                            dep(o)
        for k in writes:
            dep(self.last_w.get(k))
            r = self.readers.get(k)
            if r:
                for o in r.values():
                    dep(o)
        for o in self.barrier_ops:
            dep(o)
        op.deps = list(deps.values())
        for o in op.deps:
            o.signal = True
        for k in reads:
            r = self.readers.setdefault(k, {})
            if slot is not None:
                r[("dma", op.idx)] = op
            else:
                r[eng] = op
        for k in writes:
            self.last_w[k] = op
            self.readers[k] = {}
        if slot is not None:
            op.is_dma = True
            op.slot = slot
            c = self.slot_count.get(slot, 0) + 1
            self.slot_count[slot] = c
            op.semval = 16 * c
            self.pending_dma.append(op)
        else:
            self.last_on[eng] = op
        self.ops[eng].append(op)
        return op

    def barrier(self):
        ops = list(self.last_on.values()) + self.pending_dma
        self.barrier_ops = ops
        self.pending_dma = []

    def emit(self, nc, stack, limit=None):
        if limit is not None:
            for e in ENGS:
                self.ops[e] = [o for o in self.ops[e] if o.idx < limit]
        esem = {e: stack.enter_context(nc.semaphore("s_" + e)) for e in ENGS if e != "sp"}
        ssem = {s: stack.enter_context(nc.semaphore("d_%d" % i)) for i, s in enumerate(self.slot_count)}
        for e in ENGS:
            c = 0
            for op in self.ops[e]:
                if not op.is_dma and op.signal:
                    c += 1
                    op.semval = c

        def sem_of(o):
            return ssem[o.slot] if o.is_dma else esem[o.eng]

        def run(ename, eng):
            known = {}
            for op in self.ops[ename]:
                need = {}
                for d in op.deps:
                    if ename == "pe" and d.eng == "pe" and not d.is_dma:
                        continue
                    s = sem_of(d)
                    v = d.semval
                    key = id(s)
                    if known.get(key, 0) >= v:
                        continue
                    if key not in need or need[key][1] < v:
                        need[key] = (s, v)
                for key, (s, v) in need.items():
                    eng.wait_ge(s, v)
                    known[key] = v
                inst = op.fn(eng)
                if inst is None:
                    continue
                if op.is_dma:
                    inst.then_inc(ssem[op.slot], 16)
                elif op.signal:
                    inst.then_inc(esem[ename], 1)

        block = stack.enter_context(nc.Block())

        @block.sync
        def _(e):
            run("sp", e)

        @block.scalar
        def _(e):
            run("act", e)

        @block.vector
        def _(e):
            run("dve", e)

        @block.gpsimd
        def _(e):
            run("pool", e)

        @block.tensor
        def _(e):
            run("pe", e)


class Arena:
    def __init__(self, big, nwords):
        self.big = big
        self.n = nwords
        self.top = 0

    def mark(self):
        return self.top

    def release(self, m):
        self.top = m

    def alloc(self, nelem, dtype=F32):
        words = (nelem * (2 if dtype == BF16 else 4) + 3) // 4
        words = (words + 7) // 8 * 8
        off = self.top
        self.top += words
        assert self.top <= self.n, "SBUF arena overflow: %d > %d words" % (self.top, self.n)
        ap = self.big[:, off:off + words]
        if dtype != F32:
            ap = ap.bitcast(dtype)
        return ap[:, 0:nelem]


class Ring:
    def __init__(self, name, aps):
        self.name = name
        self.aps = aps
        self.i = 0

    def next(self):
        i = self.i % len(self.aps)
        self.i += 1
        return (self.name, i), self.aps[i]


def chunks(n, c=512):
    return [(i, min(c, n - i)) for i in range(0, n, c)]


class Cfg:
    def __init__(self, n_pool=10240, n_pages=64, do_b=True, do_s=True):
        self.n_pool = n_pool
        self.n_pages = n_pages
        self.do_b = do_b
        self.do_s = do_s


V_PRE = 0
V_POST = 32
V_KVN = 64
V_LAT = 72
V_QN = 74
V_CW = 82
V_BLEND = 130
V_R16 = 138
V_EPS = 139
NV = 140


def build(cfg):
    nc = bass.Bass("TRN2", target_bir_lowering=False)
    G = Graph()
    stack = contextlib.ExitStack()

    regcache = {}

    def I(eng, method, reads, writes, *args, slot=None, **kw):
        def fn(e):
            try:
                if "bounds_check" in kw and not isinstance(kw["bounds_check"], (type(None),)) and isinstance(kw["bounds_check"], int):
                    if "bcreg" not in regcache:
                        regcache["bcreg"] = e.to_reg(kw["bounds_check"])
                    kw2 = dict(kw)
                    kw2["bounds_check"] = regcache["bcreg"]
                    return getattr(e, method)(*args, **kw2)
                return getattr(e, method)(*args, **kw)
            except Exception:
                print("FAILED OP", eng, method, writes, [str(a)[:200] for a in args], {k: str(v)[:200] for k, v in kw.items()})
                raise
        return G.add(eng, fn, reads=reads, writes=writes, slot=slot)

    def din(name, shape, dt=F32):
        return nc.dram_tensor(name, list(shape), dt, kind="ExternalInput").ap()

    def dout(name, shape, dt=F32):
        return nc.dram_tensor(name, list(shape), dt, kind="ExternalOutput").ap()

    NPG = cfg.n_pages
    NGRP = NPG // 8
    xp = din("xp", [SEQ, D])
    meta = din("meta", [NMETA, D])
    xs = din("xs", [NS, D])
    sconv = din("sconv", [64, D])
    vecs = din("vecs", [128, NV])
    ident_d = din("ident", [128, 128])
    wic = din("wic", [2 * 32 * 128, 1024])
    woc = din("woc", [2 * 8 * 128, 1024])
    wdkv = din("wdkv", [3 * 128, 1024])
    wim = din("wim", [2 * 12 * 128, 1024])
    wuq = din("wuq", [2 * 16 * 128, 512])
    wom = din("wom", [2 * 8 * 128, 1024])
    wuk = din("wuk", [128, 2048])
    wuv = din("wuv", [128, 2048])
    wukT = din("wukT", [64, 16 * 256])
    ktab = din("ktab", [128, T])
    mtab = din("mtab", [128, 80])
    qtab = din("qtab", [128, 2048])
    masks = din("masks", [2 * 128, 4096])
    ptx = din("ptx", [128, NB * NGRP], I32)
    smask = din("smask", [64, NB * 64])
    cckv = din("cckv", [cfg.n_pool * 16, 2048])
    ckr = din("ckr", [cfg.n_pool * 16, 256])

    y_own = dout("y_own", [2048, D])
    ys = dout("ys", [NS, D])
    ckv_p = dout("ckv_p", [T, 256])
    kr_p = dout("kr_p", [T, 32])
    conv_p = dout("conv_p", [4, D])
    ckv_s = dout("ckv_s", [NS, 256])
    kr_s = dout("kr_s", [NS, 32])
    conv_s = dout("conv_s", [64, D])

    NW = 53000
    big = stack.enter_context(nc.sbuf_tensor("arena", [128, NW], F32))
    A = Arena(big, NW)
    banks = [stack.enter_context(nc.psum_tensor("bank%d" % i, [128, 512], F32)) for i in range(8)]

    def v3(ap, a):
        return ap.rearrange("p (a b) -> p a b", a=a)

    def bt(ap, t):
        return ap.rearrange("p (b t) -> p b t", t=t)

    ident = A.alloc(128)
    identb = A.alloc(128, BF16)
    onesb = A.alloc(128, BF16)
    neghalf = A.alloc(512)
    vec = A.alloc(NV)
    I("sp", "dma_start", [], ["ident"], out=ident, in_=ident_d, slot="c_ident")
    I("sp", "dma_start", [], ["vec"], out=vec, in_=vecs, slot="c_vec")
    I("pool", "dma_start", [], ["identb"], out=identb, in_=ident_d, slot="c_identb")
    I("dve", "memset", [], ["onesb"], onesb, 1.0)
    I("dve", "memset", [], ["neghalf"], neghalf, -0.5)

    def vcol(off):
        return vec[:, off:off + 1]


    xsT = v3(A.alloc(8 * NS), 8)
    cTs = v3(A.alloc(2 * NS, BF16), 2)
    krTs = A.alloc(NS, BF16)
    cns_tok = A.alloc(256, BF16)
    mSP = A.mark()
    x_own = v3(A.alloc(8 * 2048), 8)
    cT = v3(A.alloc(2 * T, BF16), 2)
    Kbuf = [A.alloc(T, BF16) for _ in range(2)]
    halo = A.alloc(2 * 8 * 2)
    cvp = A.alloc(8 * 4)
    cvs = A.alloc(8 * 64)
    stT = A.alloc(8 * 64)

    for i in range(2):
        I("pool", "memset", [], [("Kb", i, "kr")], Kbuf[i], 0.0)

    psring = Ring("ps", [banks[i] for i in range(4)])
    ssq_banks = [banks[4], banks[5]]
    trring = Ring("pt", [banks[6], banks[7]])

    def copy_any(eng, reads, writes, out, in_):
        if eng == "act":
            I("act", "activation", reads, writes, out=out, in_=in_, func=AF.Copy)
        else:
            I(eng, "tensor_copy", reads, writes, out=out, in_=in_)

    def transposes_to(dst_fn, src, src_key, nrows, ncols, idn, eng_alt):
        nblk = (ncols + 127) // 128
        per = max(1, 512 // nrows)
        for g0 in range(0, nblk, per):
            pk, pb = trring.next()
            nb = min(per, nblk - g0)
            for i in range(nb):
                cb = g0 + i
                cw = min(128, ncols - cb * 128)
                I("pe", "transpose", [src_key, "ident"], [pk],
                  pb[:cw, i * nrows:(i + 1) * nrows], src[:nrows, cb * 128:cb * 128 + cw], idn[:nrows, :nrows])
            for i in range(nb):
                cb = g0 + i
                cw = min(128, ncols - cb * 128)
                dst, dkey = dst_fn(cb)
                copy_any(eng_alt[cb % len(eng_alt)], [pk], [dkey], dst, pb[:cw, i * nrows:(i + 1) * nrows])

    mA = A.mark()
    wring = Ring("w", [v3(A.alloc(1024, BF16), 8) for _ in range(4)])
    sqr = Ring("sq", [A.alloc(512, BF16) for _ in range(2)])
    rsr = Ring("rs", [A.alloc(512) for _ in range(2)])

    def load_w(dram, row0):
        k, w = wring.next()
        I("pool", "dma_start", [], [k], out=w.rearrange("p a b -> p (a b)"), in_=dram[row0:row0 + 128, :], slot="w%d" % k[1])
        return k, w

    def rstd_from(sb, skey, n, nfeat):
        rk, r = rsr.next()
        I("act", "activation", [skey, "vec"], [rk], out=r[:, :n], in_=sb[:, :n], func=AF.Ln, scale=1.0 / nfeat, bias=vcol(V_EPS))
        I("act", "activation", [rk], [rk], out=r[:, :n], in_=r[:, :n], func=AF.Exp, scale=-0.5)
        return rk, r

    def rms_stats(srcT, key_fn, nchunks, tcs, nfeat):
        res = []
        for ti, (c0, n) in enumerate(tcs):
            sb = ssq_banks[ti % 2]
            sk = ("ssq", ti % 2)
            for kc in range(nchunks):
                qk, q = sqr.next()
                I("act", "activation", [key_fn(kc)], [qk], out=q[:, :n], in_=srcT[:, kc, c0:c0 + n], func=AF.Square)
                I("pe", "matmul", [qk, "onesb"], [sk], sb[:, :n], onesb, q[:, :n], start=(kc == 0), stop=(kc == nchunks - 1))
            res.append(rstd_from(sb, sk, n, nfeat))
        return res

    def norm_to(dstT, dkey_fn, srcT, skey_fn, nchunks, tcs, rst, goff):
        for ti, (c0, n) in enumerate(tcs):
            rk, r = rst[ti]
            for kc in range(nchunks):
                I("dve", "scalar_tensor_tensor", [skey_fn(kc), rk, "vec"], [dkey_fn(kc)],
                  out=dstT[:, kc, c0:c0 + n], in0=srcT[:, kc, c0:c0 + n], scalar=vcol(goff + kc), in1=r[:, :n], op0=ALU.mult, op1=ALU.mult)

    def proj(wk, w, nk, srcT, skey_fn, c0, n, ring=None):
        pk, pb = (ring or psring).next()
        for kc in range(nk):
            I("pe", "matmul", [wk, skey_fn(kc)], [pk], pb[:, :n], w[:, kc, :], srcT[:, kc, c0:c0 + n], start=(kc == 0), stop=(kc == nk - 1))
        return pk, pb

    def out_proj_residual(wdram, wrow0, srcT, skey_fn, xT, xkey_fn, tcs, NT, goff):
        mT = v3(A.alloc(8 * NT), 8)
        assert len(tcs) <= 2
        for of in range(8):
            wk, w = load_w(wdram, wrow0 + of * 128)
            for ti, (c0, n) in enumerate(tcs):
                pk, pb = proj(wk, w, 8, srcT, skey_fn, c0, n)
                I("act", "activation", [pk], [("mT", of, ti)], out=mT[:, of, c0:c0 + n], in_=pb[:, :n], func=AF.Copy)
                qk, q = sqr.next()
                I("act", "activation", [pk], [qk], out=q[:, :n], in_=pb[:, :n], func=AF.Square)
                I("pe", "matmul", [qk, "onesb"], [("ssq", ti % 2)], ssq_banks[ti % 2][:, :n], onesb, q[:, :n], start=(of == 0), stop=(of == 7))
        for ti, (c0, n) in enumerate(tcs):
            rk, r = rstd_from(ssq_banks[ti % 2], ("ssq", ti % 2), n, D)
            for of in range(8):
                I("dve", "scalar_tensor_tensor", [("mT", of, ti), rk, "vec"], [("mT", of, ti)],
                  out=mT[:, of, c0:c0 + n], in0=mT[:, of, c0:c0 + n], scalar=vcol(goff + of), in1=r[:, :n], op0=ALU.mult, op1=ALU.mult)
                I("dve", "tensor_tensor", [("mT", of, ti), xkey_fn(of)], [xkey_fn(of)],
                  out=xT[:, of, c0:c0 + n], in0=xT[:, of, c0:c0 + n], in1=mT[:, of, c0:c0 + n], op=ALU.add)

    def run_group(gi, NT, segs, x_loads, tabs, dests, own_slot):
        tcs = chunks(NT)
        mG = A.mark()
        xT = v3(A.alloc(8 * NT), 8)
        gT = v3(A.alloc(8 * NT, BF16), 8)
        xkey = lambda kc: ("xT", kc)
        mS = A.mark()
        xr = Ring("xin", [A.alloc(1024) for _ in range(2)])
        for (src, ntok, col0) in x_loads:
            xk, xb = xr.next()
            I("sp", "dma_start", [], [xk], out=xb[:ntok, :], in_=src, slot="xin%d" % xk[1])
            transposes_to(lambda cb, col0=col0, ntok=ntok: (xT[:, cb, col0:col0 + ntok], ("xT", cb)), xb, xk, ntok, 1024, ident, XENG)
        G.barrier()
        A.release(mS)
        for l in range(2):
            mL = A.mark()
            hT = v3(A.alloc(8 * NT, BF16), 8)
            hkey = lambda kc: ("hT", kc)
            cbuf = A.alloc(NT)
            ybuf = A.alloc(NT)
            tbuf = A.alloc(NT)
            szb = A.alloc(NT)
            vexts = [A.alloc(sg["nseq"] * (sg["L"] + 2)) for sg in segs]
            rst = rms_stats(xT, xkey, 8, tcs, D)
            norm_to(hT, hkey, xT, xkey, 8, tcs, rst, V_PRE + l * 8)
            for j in range(8):
                hoff = (l * 8 + j) * 2
                cwb = V_CW + l * 24 + j
                for si, sg in enumerate(segs):
                    ve = vexts[si]
                    L = sg["L"]
                    if sg["kind"] == "meta":
                        I("dve", "memset", [], [("vext", si)], ve[:, 0:2], 0.0)
                    elif sg["kind"] == "prompt":
                        I("dve", "tensor_copy", [("halo", l, j)], [("vext", si)], out=ve[:, 0:2], in_=halo[:, hoff:hoff + 2])
                    else:
                        I("dve", "tensor_copy", ["stT"], [("vext", si)], out=bt(ve, L + 2)[:, :, 0:2],
                          in_=stT[:, j * 64 + l * 32:j * 64 + l * 32 + 32].rearrange("p (b k) -> p b k", k=2))
                for part, pname in ((1, "c"), (2, "u"), (0, "b"), (3, "z")):
                    if pname == "b":
                        for si, sg in enumerate(segs):
                            ve = vexts[si]
                            L = sg["L"]
                            s0 = sg["col0"]
                            if sg["nseq"] == 1:
                                src = [ve[:, k:k + L] for k in range(3)]
                                yv = ybuf[:, s0:s0 + L]
                                tail = ve[:, L:L + 2]
                            else:
                                ve3 = bt(ve, L + 2)
                                src = [ve3[:, :, k:k + L] for k in range(3)]
                                yv = bt(ybuf[:, s0:s0 + sg["nseq"] * L], L)
                                tail = ve3[:, :, L:L + 2]
                            vk, yk = ("vext", si), ("ybuf", si)
                            I("dve", "tensor_scalar", [vk, "vec"], [yk], out=yv, in0=src[0], scalar1=vcol(cwb), scalar2=None, op0=ALU.mult)
                            I("dve", "scalar_tensor_tensor", [vk, "vec", yk], [yk], out=yv, in0=src[1], scalar=vcol(cwb + 8), in1=yv, op0=ALU.mult, op1=ALU.add)
                            I("dve", "scalar_tensor_tensor", [vk, "vec", yk], [yk], out=yv, in0=src[2], scalar=vcol(cwb + 16), in1=yv, op0=ALU.mult, op1=ALU.add)
                            if sg["kind"] in ("meta", "prompt"):
                                I("act", "activation", [vk], [("halo", l, j)], out=halo[:, hoff:hoff + 2], in_=tail, func=AF.Copy)
                                if sg.get("last"):
                                    I("act", "activation", [vk], ["cvp"], out=cvp[:, j * 4 + l * 2:j * 4 + l * 2 + 2], in_=tail, func=AF.Copy)
                            else:
                                I("act", "activation", [vk], ["cvs"], out=cvs[:, j * 64 + l * 32:j * 64 + l * 32 + 32].rearrange("p (b k) -> p b k", k=2),
                                  in_=tail, func=AF.Copy)
                    wk, w = load_w(wic, (l * 32 + part * 8 + j) * 128)
                    for ti, (c0, n) in enumerate(tcs):
                        pk, pb = proj(wk, w, 8, hT, hkey, c0, n)
                        if pname == "c":
                            I("act", "activation", [pk], [("cbuf", ti)], out=cbuf[:, c0:c0 + n], in_=pb[:, :n], func=AF.Copy)
                        elif pname == "u":
                            for si, sg in enumerate(segs):
                                L = sg["L"]
                                s0 = sg["col0"]
                                lo = max(c0, s0)
                                hi = min(c0 + n, s0 + sg["nseq"] * L)
                                if hi <= lo:
                                    continue
                                ve = vexts[si]
                                if sg["nseq"] == 1:
                                    o = ve[:, 2 + lo - s0:2 + hi - s0]
                                    a = pb[:, lo - c0:hi - c0]
                                    b_ = cbuf[:, lo:hi]
                                else:
                                    assert lo == s0 and hi == s0 + sg["nseq"] * L
                                    o = bt(ve, L + 2)[:, :, 2:2 + L]
                                    a = bt(pb[:, lo - c0:hi - c0], L)
                                    b_ = bt(cbuf[:, lo:hi], L)
                                I("dve", "tensor_tensor", [pk, ("cbuf", ti)], [("vext", si)], out=o, in0=a, in1=b_, op=ALU.mult)
                        elif pname == "b":
                            I("dve", "tensor_tensor", [pk] + [("ybuf", si) for si in range(len(segs))], [("tbuf", ti)],
                              out=tbuf[:, c0:c0 + n], in0=pb[:, :n], in1=ybuf[:, c0:c0 + n], op=ALU.mult)
                        else:
                            I("act", "activation", [pk], [("szb", ti)], out=szb[:, c0:c0 + n], in_=pb[:, :n], func=AF.Silu)
                            I("dve", "tensor_tensor", [("tbuf", ti), ("szb", ti)], [("gT", j)],
                              out=gT[:, j, c0:c0 + n], in0=tbuf[:, c0:c0 + n], in1=szb[:, c0:c0 + n], op=ALU.mult)
            G.barrier()
            A.release(mL)
            out_proj_residual(woc, l * 8 * 128, gT, lambda kc: ("gT", kc), xT, xkey, tcs, NT, V_POST + l * 8)
            G.barrier()
            A.release(mL)
        mL = A.mark()
        hT = gT
        hkey = lambda kc: ("gT", kc)
        craw = v3(A.alloc(2 * NT), 2)
        tabb = A.alloc(NT)
        krf = A.alloc(NT)
        krt = A.alloc(NT)
        ctr = Ring("ctok", [A.alloc(256) for _ in range(2)])
        ktr = Ring("ktok", [A.alloc(32) for _ in range(2)])
        for (tsrc, c0, n) in tabs:
            I("sp", "dma_start", [], ["tabb"], out=tabb[:, c0:c0 + n], in_=tsrc, slot="tabb")
        rst = rms_stats(xT, xkey, 8, tcs, D)
        norm_to(hT, hkey, xT, xkey, 8, tcs, rst, V_KVN)
        for ocl in range(3):
            wk, w = load_w(wdkv, ocl * 128)
            for ti, (c0, n) in enumerate(tcs):
                pk, pb = proj(wk, w, 8, hT, hkey, c0, n)
                if ocl < 2:
                    I("act", "activation", [pk], [("craw", ocl, ti)], out=craw[:, ocl, c0:c0 + n], in_=pb[:, :n], func=AF.Copy)
                    qk, q = sqr.next()
                    I("act", "activation", [pk], [qk], out=q[:, :n], in_=pb[:, :n], func=AF.Square)
                    I("pe", "matmul", [qk, "onesb"], [("ssq", ti % 2)], ssq_banks[ti % 2][:, :n], onesb, q[:, :n], start=(ocl == 0), stop=(ocl == 1))
                else:
                    I("dve", "tensor_tensor", [pk, "tabb"], [("krf", ti)], out=krf[64:96, c0:c0 + n], in0=pb[64:96, :n], in1=tabb[64:96, c0:c0 + n], op=ALU.mult)
                    I("dve", "tensor_tensor", [pk, "tabb"], [("krt", ti)], out=krt[64:96, c0:c0 + n], in0=pb[96:128, :n], in1=tabb[96:128, c0:c0 + n], op=ALU.mult)
                    I("dve", "tensor_tensor", [("krf", ti), ("krt", ti)], [("krf", ti)], out=krf[64:96, c0:c0 + n], in0=krf[64:96, c0:c0 + n], in1=krt[64:96, c0:c0 + n], op=ALU.add)
        for ti, (c0, n) in enumerate(tcs):
            rk, r = rstd_from(ssq_banks[ti % 2], ("ssq", ti % 2), n, 256)
            for ocl in range(2):
                I("dve", "scalar_tensor_tensor", [("craw", ocl, ti), rk, "vec"], [("craw", ocl, ti)],
                  out=craw[:, ocl, c0:c0 + n], in0=craw[:, ocl, c0:c0 + n], scalar=vcol(V_LAT + ocl), in1=r[:, :n], op0=ALU.mult, op1=ALU.mult)
        for dd in dests:
            c0, n = dd["c0"], dd["n"]
            tis = sorted(set(ti for ti, (a, m) in enumerate(tcs) if a < c0 + n and a + m > c0))
            rkeys = [("craw", ocl, ti) for ocl in range(2) for ti in tis]
            kkeys = [("krf", ti) for ti in tis]
            for ocl in range(2):
                I("act", "activation", rkeys, [dd["cT_key"]], out=dd["cT"][:, ocl, :], in_=craw[:, ocl, c0:c0 + n], func=AF.Copy)
            for (kap, kkey) in dd["krT"]:
                I("act", "activation", kkeys, [kkey], out=kap, in_=krf[64:96, c0:c0 + n], func=AF.Copy)
            for t0 in range(0, n, 128):
                tw = min(128, n - t0)
                ck, cb_ = ctr.next()
                kk, kb_ = ktr.next()
                pk, pb = trring.next()
                for ocl in range(2):
                    I("pe", "transpose", rkeys + ["ident"], [pk], pb[:tw, ocl * 128:(ocl + 1) * 128], craw[:, ocl, c0 + t0:c0 + t0 + tw], ident)
                I("pe", "transpose", kkeys + ["ident"], [pk], pb[:tw, 256:288], krf[64:96, c0 + t0:c0 + t0 + tw], ident[64:96, 64:96])
                I("act", "activation", [pk], [ck], out=cb_[:tw, :], in_=pb[:tw, 0:256], func=AF.Copy)
                I("dve", "tensor_copy", [pk], [kk], out=kb_[:tw, :], in_=pb[:tw, 256:288])
                r0 = dd["row0"] + t0
                I("sp", "dma_start", [ck], [("out", dd["name"], "c", r0)], out=dd["ckv_out"][r0:r0 + tw, :], in_=cb_[:tw, :], slot="octok%d" % ck[1])
                I("sp", "dma_start", [kk], [("out", dd["name"], "k", r0)], out=dd["kr_out"][r0:r0 + tw, :], in_=kb_[:tw, :], slot="oktok%d" % kk[1])
                if dd.get("tok_bf") is not None:
                    I("dve", "tensor_copy", [pk], ["cns_tok"], out=dd["tok_bf"][:tw, :], in_=pb[:tw, 0:256])
        if own_slot is not None:
            s = own_slot
            for kc in range(8):
                xo = x_own[:, kc, s * 512:(s + 1) * 512]
                I("dve", "tensor_scalar", [("xT", kc), "vec"], [("x_own", s, kc)], out=xo, in0=xT[:, kc, 0:512], scalar1=vcol(V_BLEND + 2 * s), scalar2=None, op0=ALU.mult)
                I("dve", "scalar_tensor_tensor", [("xT", kc), "vec", ("x_own", s, kc)], [("x_own", s, kc)],
                  out=xo, in0=xT[:, kc, 512:1024], scalar=vcol(V_BLEND + 2 * s + 1), in1=xo, op0=ALU.mult, op1=ALU.add)
        else:
            for kc in range(8):
                I("act", "activation", [("xT", kc)], [("xs", kc)], out=xsT[:, kc, :], in_=xT[:, kc, 16:80], func=AF.Copy)
        G.barrier()
        A.release(mG)

    mS0 = A.mark()
    scb = A.alloc(1024)
    I("sp", "dma_start", [], ["scb"], out=scb[:64, :], in_=sconv, slot="scb")
    transposes_to(lambda cb: (stT[:, cb * 64:(cb + 1) * 64], "stT"), scb, "scb", 64, 1024, ident, ["act", "dve"])
    G.barrier()
    A.release(mS0)

    run_group(
        0, 80,
        [dict(col0=0, nseq=1, L=16, kind="meta"), dict(col0=16, nseq=NB, L=LS, kind="sample")],
        [(meta, 16, 0), (xs, 64, 16)],
        [(mtab, 0, 80)],
        [dict(c0=0, n=16, cT=cT[:, :, SEQ:SEQ + 16], cT_key=("cT", "m"), krT=[(Kbuf[i][64:96, SEQ:SEQ + 16], ("Kb", i, "kr")) for i in range(2)],
              ckv_out=ckv_p, kr_out=kr_p, row0=0, name="p"),
         dict(c0=16, n=64, cT=cTs, cT_key="cTs", krT=[(krTs[0:32, :], "krTs")],
              ckv_out=ckv_s, kr_out=kr_s, row0=0, name="s", tok_bf=cns_tok)],
        None)
    for g in range(4):
        run_group(
            1 + g, 1024,
            [dict(col0=0, nseq=1, L=1024, kind="prompt", last=(g == 3))],
            [(xp[g * 1024 + i * 128:g * 1024 + (i + 1) * 128, :], 128, i * 128) for i in range(8)],
            [(ktab[:, g * 1024:(g + 1) * 1024], 0, 1024)],
            [dict(c0=0, n=1024, cT=cT[:, :, g * 1024:(g + 1) * 1024], cT_key=("cT", g), krT=[(Kbuf[i][64:96, g * 1024:(g + 1) * 1024], ("Kb", i, "kr")) for i in range(2)],
                  ckv_out=ckv_p, kr_out=kr_p, row0=16 + g * 1024, name="p")],
            g)

    mO = A.mark()
    cvo = A.alloc(1024)
    cso = A.alloc(1024)
    for j in range(8):
        pk, pb = trring.next()
        I("pe", "transpose", ["cvp", "ident"], [pk], pb[:4, 0:128], cvp[:, j * 4:(j + 1) * 4], ident)
        I("pe", "transpose", ["cvs", "ident"], [pk], pb[:64, 128:256], cvs[:, j * 64:(j + 1) * 64], ident)
        I("dve", "tensor_copy", [pk], ["cvo"], out=cvo[:4, j * 128:(j + 1) * 128], in_=pb[:4, 0:128])
        I("act", "activation", [pk], ["cso"], out=cso[:64, j * 128:(j + 1) * 128], in_=pb[:64, 128:256], func=AF.Copy)
    I("sp", "dma_start", ["cvo"], [("out", "conv_p")], out=conv_p, in_=cvo[:4, :], slot="o_cvo")
    I("sp", "dma_start", ["cso"], [("out", "conv_s")], out=conv_s, in_=cso[:64, :], slot="o_cso")
    G.barrier()
    A.release(mO)
    A.release(mA)


    mB = A.mark()
    wring = Ring("w", [v3(A.alloc(1024, BF16), 8) for _ in range(4)])
    sqr = Ring("sq", [A.alloc(512, BF16) for _ in range(2)])
    rsr = Ring("rs", [A.alloc(512) for _ in range(2)])
    Vh = v3(A.alloc(33 * 128, BF16), 33)
    maskb = [A.alloc(4096, BF16) for _ in range(2)]
    kmax2 = A.alloc(16)
    half = A.alloc(1)
    wqr = Ring("wq", [v3(A.alloc(512, BF16), 4) for _ in range(2)])
    wkr = Ring("wkh", [v3(A.alloc(128, BF16), 2) for _ in range(2)])
    wvr = Ring("wvh", [v3(A.alloc(128, BF16), 2) for _ in range(2)])
    poring = Ring("ssq", [banks[4], banks[5]])
    I("dve", "memset", [], ["Vh"], Vh.rearrange("p a b -> p (a b)"), 1.0)
    I("dve", "memset", [], ["half"], half, 0.5)
    for par in range(2):
        I("pool", "dma_start", [], ["maskb"], out=maskb[par], in_=masks[par * 128:(par + 1) * 128, :], slot="maskb%d" % par)
    wuk3 = wuk.rearrange("p (kc f) -> p kc f", kc=2)
    wuv3 = wuv.rearrange("p (kc f) -> p kc f", kc=2)
    print("arena words used before phase-B halves:", A.top, "of", A.n)

    def mla_prompt(j, hf):
        NT = 1024
        tcs = chunks(NT)
        xT = x_own[:, :, hf * 1024:(hf + 1) * 1024]
        xkey = lambda kc: ("xo", hf, kc)
        first = (j == 0 and hf == 0)
        mH = A.mark()
        ogT = v3(A.alloc(8 * NT, BF16), 8)
        ogkey = lambda kc: ("og", kc)
        qln = v3(A.alloc(4 * NT, BF16), 4)
        qlnkey = lambda kc: ("qln", kc)
        mX = A.mark()
        hT = v3(A.alloc(8 * NT, BF16), 8)
        hkey = lambda kc: ("hT", kc)
        qraw = v3(A.alloc(4 * NT), 4)
        rst = rms_stats(xT, xkey, 8, tcs, D)
        norm_to(hT, hkey, xT, xkey, 8, tcs, rst, V_PRE + (2 + j) * 8)
        for oc in range(12):
            wk, w = load_w(wim, (j * 12 + oc) * 128)
            for ti, (c0, n) in enumerate(tcs):
                pk, pb = proj(wk, w, 8, hT, hkey, c0, n)
                if oc < 4:
                    I("act", "activation", [pk], [("qraw", oc, ti)], out=qraw[:, oc, c0:c0 + n], in_=pb[:, :n], func=AF.Copy)
                    qk, q = sqr.next()
                    I("act", "activation", [pk], [qk], out=q[:, :n], in_=pb[:, :n], func=AF.Square)
                    I("pe", "matmul", [qk, "onesb"], [("ssq", ti % 2)], ssq_banks[ti % 2][:, :n], onesb, q[:, :n], start=(oc == 0), stop=(oc == 3))
                else:
                    I("act", "activation", [pk], [("og", oc - 4)], out=ogT[:, oc - 4, c0:c0 + n], in_=pb[:, :n], func=AF.Silu)
            if oc == 3:
                for ti, (c0, n) in enumerate(tcs):
                    rk, r = rstd_from(ssq_banks[ti % 2], ("ssq", ti % 2), n, 512)
                    for kc in range(4):
                        I("dve", "scalar_tensor_tensor", [("qraw", kc, ti), rk, "vec"], [("qln", kc)],
                          out=qln[:, kc, c0:c0 + n], in0=qraw[:, kc, c0:c0 + n], scalar=vcol(V_QN + j * 4 + kc), in1=r[:, :n], op0=ALU.mult, op1=ALU.mult)
        G.barrier()
        A.release(mX)
        Qr = Ring("Qh", [A.alloc(NT, BF16) for _ in range(2)])
        Ptr = Ring("Pt", [A.alloc(512, BF16) for _ in range(4)])
        qtabb = A.alloc(NT)
        rsum = Ring("rsum", [A.alloc(512) for _ in range(2)])
        tmpf = Ring("tmpf", [A.alloc(512) for _ in range(2)])
        t1r = Ring("t1", [A.alloc(512) for _ in range(2)])
        t2r = Ring("t2", [A.alloc(512) for _ in range(2)])
        negr = Ring("negm", [A.alloc(1) for _ in range(2)])
        qmx = A.alloc(4)
        kmx = A.alloc(16)
        I("sp", "dma_start", [], ["qtabb"], out=qtabb, in_=qtab[:, hf * 1024:(hf + 1) * 1024], slot="qtabb")
        for (qk_, qb_) in [Qr.next(), Qr.next()]:
            I("pool", "memset", [], [qk_], qb_, 0.0)
        for h in range(16):
            hb = (h % 2) * 64
            pr = h // 2
            Kk = ("Kb", h % 2)
            Kh = Kbuf[h % 2]
            wqk, wq = wqr.next()
            I("pool", "dma_start", [], [wqk], out=wq.rearrange("p a b -> p (a b)"), in_=wuq[(j * 16 + h) * 128:(j * 16 + h + 1) * 128, :], slot="wq%d" % wqk[1])
            wkk, wkh = wkr.next()
            I("pool", "dma_start", [], [wkk], out=wkh, in_=wuk3[:, :, h * 64:(h + 1) * 64], slot="wkh%d" % wkk[1])
            wvk, wvh = wvr.next()
            I("pool", "dma_start", [], [wvk], out=wvh, in_=wuv3[:, :, h * 64:(h + 1) * 64], slot="wvh%d" % wvk[1])
            Qk, Qh = Qr.next()
            for ti, (c0, n) in enumerate(tcs):
                pk, pb = proj(wqk, wq, 4, qln, qlnkey, c0, n, ring=trring)
                I("act", "activation", [pk], [Qk], out=Qh[0:64, c0:c0 + n], in_=pb[0:64, :n], func=AF.Copy)
                k1, t1 = t1r.next()
                k2, t2 = t2r.next()
                I("dve", "tensor_tensor", [pk, "qtabb"], [k1], out=t1[64:96, :n], in0=pb[64:96, :n], in1=qtabb[64:96, c0:c0 + n], op=ALU.mult)
                I("dve", "tensor_tensor", [pk, "qtabb"], [k2], out=t2[64:96, :n], in0=pb[96:128, :n], in1=qtabb[96:128, c0:c0 + n], op=ALU.mult)
                I("dve", "tensor_tensor", [k1, k2], [Qk], out=Qh[64:96, c0:c0 + n], in0=t1[64:96, :n], in1=t2[64:96, :n], op=ALU.add)
            for ci, (c0, n) in enumerate(chunks(T)):
                pk, pb = trring.next()
                for kc in range(2):
                    I("pe", "matmul", [wkk, "cT"], [pk], pb[0:64, :n], wkh[:, kc, :], cT[:, kc, c0:c0 + n], start=(kc == 0), stop=(kc == 1))
                I("dve", "tensor_copy", [pk], [Kk], out=Kh[0:64, c0:c0 + n], in_=pb[0:64, :n])
            for g0 in range(0, 33, 8):
                pk, pb = trring.next()
                nb = min(8, 33 - g0)
                for i in range(nb):
                    kb = g0 + i
                    kk = 128 if kb < 32 else 16
                    for kc in range(2):
                        I("pe", "matmul", [wvk, "cT"], [pk], pb[:kk, i * 64:(i + 1) * 64], cT[:, kc, kb * 128:kb * 128 + kk], wvh[:, kc, :], start=(kc == 0), stop=(kc == 1))
                if nb == 8:
                    I("dve", "tensor_copy", [pk], ["Vh"], out=Vh[:, g0:g0 + 8, 0:64], in_=pb[:, :512].rearrange("p (a b) -> p a b", a=8))
                else:
                    I("dve", "tensor_copy", [pk], ["Vh"], out=Vh[:16, 32, 0:64], in_=pb[:16, 0:64])
            if first:
                for ci, (c0, n) in enumerate(chunks(T)):
                    qk, q = sqr.next()
                    I("act", "activation", [Kk, ("Kb", h % 2, "kr")], [qk], out=q[0:96, :n], in_=Kh[0:96, c0:c0 + n], func=AF.Square)
                    pk, pb = trring.next()
                    I("pe", "matmul", [qk, "onesb"], [pk], pb[:, :n], onesb[0:96, :], q[0:96, :n], start=True, stop=True)
                    I("dve", "tensor_reduce", [pk], [("kmx", ci)], out=kmx[:, ci:ci + 1], in_=pb[:, :n], axis=AX.X, op=ALU.max)
                I("dve", "tensor_reduce", [("kmx", ci) for ci in range(9)], ["kmax2"], out=kmax2[:, h:h + 1], in_=kmx[:, 0:9], axis=AX.X, op=ALU.max)
            for ti, (c0, n) in enumerate(tcs):
                qk, q = sqr.next()
                I("act", "activation", [Qk], [qk], out=q[0:96, :n], in_=Qh[0:96, c0:c0 + n], func=AF.Square)
                pk, pb = trring.next()
                I("pe", "matmul", [qk, "onesb"], [pk], pb[:, :n], onesb[0:96, :], q[0:96, :n], start=True, stop=True)
                I("dve", "tensor_reduce", [pk], [("qmx", ti)], out=qmx[:, ti:ti + 1], in_=pb[:, :n], axis=AX.X, op=ALU.max)
            nk, negm = negr.next()
            I("dve", "tensor_tensor", [("qmx", 0), ("qmx", 1)], [nk], out=negm, in0=qmx[:, 0:1], in1=qmx[:, 1:2], op=ALU.max)
            I("dve", "tensor_tensor", [nk, "kmax2"], [nk], out=negm, in0=negm, in1=kmax2[:, h:h + 1], op=ALU.mult)
            I("act", "activation", [nk], [nk], out=negm, in_=negm, func=AF.Ln)
            I("act", "activation", [nk], [nk], out=negm, in_=negm, func=AF.Exp, scale=0.5)
            I("dve", "tensor_scalar", [nk], [nk], out=negm, in0=negm, scalar1=-SCALE * 1.02, scalar2=None, op0=ALU.mult)
            LAG = 2
            items = []
            for sl in range(2):
                sg = 2 * hf + sl
                seq = list(range(8 * sg + 8)) + [32]
                for idx, kb in enumerate(seq):
                    items.append((sl, sg, idx, kb, len(seq)))
            pos = {}
            info = {}

            def qk_stage(it):
                sl, sg, idx, kb, nseq = it
                if idx == 0:
                    pos[sl] = poring.next()
                kk = 128 if kb < 32 else 16
                q0 = sl * 512
                pk, pb = psring.next()
                I("pe", "matmul", [Kk, ("Kb", h % 2, "kr"), Qk], [pk], pb[:kk, :512], Kh[:, kb * 128:kb * 128 + kk], Qh[:, q0:q0 + 512], start=True, stop=True)
                ptk, pt = Ptr.next()
                I("act", "activation", [pk, nk], [ptk], out=pt[:kk, :], in_=pb[:kk, :512], func=AF.Exp, scale=SCALE, bias=negm[:kk, :])
                if kb < 32 and kb >= 8 * sg:
                    mi = kb - 8 * sg
                    I("dve", "tensor_tensor", [ptk, "maskb"], [ptk], out=pt, in0=pt, in1=maskb[sg % 2][:, mi * 512:(mi + 1) * 512], op=ALU.mult)
                info[it] = (ptk, pt, kk)

            def pv_stage(it):
                sl, sg, idx, kb, nseq = it
                ptk, pt, kk = info.pop(it)
                pok, po = pos[sl]
                q0 = sl * 512
                I("pe", "matmul", [ptk, "Vh"], [pok], po[:, :512], Vh[:kk, kb, :], pt[:kk, :], start=(idx == 0), stop=(idx == nseq - 1))
                if idx == nseq - 1:
                    rsk, rs = rsum.next()
                    tmk, tm = tmpf.next()
                    I("act", "activation", [pok], [rsk], out=rs[hb:hb + 64, :], in_=po[64:128, :512], func=AF.Ln)
                    I("act", "activation", [rsk], [rsk], out=rs[hb:hb + 64, :], in_=rs[hb:hb + 64, :], func=AF.Exp, scale=-1.0)
                    I("dve", "tensor_tensor", [pok, rsk], [tmk], out=tm[hb:hb + 64, :], in0=po[0:64, :512], in1=rs[hb:hb + 64, :], op=ALU.mult)
                    I("dve", "tensor_tensor", [tmk, ("og", pr)], [("og", pr)], out=ogT[hb:hb + 64, pr, q0:q0 + 512], in0=tm[hb:hb + 64, :], in1=ogT[hb:hb + 64, pr, q0:q0 + 512], op=ALU.mult)

            for i in range(len(items) + LAG):
                if i < len(items):
                    qk_stage(items[i])
                if i - LAG >= 0:
                    pv_stage(items[i - LAG])
        G.barrier()
        A.release(mX)
        out_proj_residual(wom, j * 8 * 128, ogT, ogkey, xT, xkey, tcs, NT, V_POST + (2 + j) * 8)
        G.barrier()
        A.release(mH)

    if cfg.do_b:
        for j in range(2):
            for hf in range(2):
                mla_prompt(j, hf)
        mY = A.mark()
        ytr = Ring("ytok", [A.alloc(1024) for _ in range(2)])
        for tt in range(16):
            yk, yb = ytr.next()
            for g0 in range(0, 8, 4):
                pk, pb = trring.next()
                for i in range(4):
                    kc = g0 + i
                    I("pe", "transpose", [("xo", tt // 8, kc), "ident"], [pk], pb[:, i * 128:(i + 1) * 128], x_own[:, kc, tt * 128:(tt + 1) * 128], ident)
                copy_any("act" if g0 == 0 else "dve", [pk], [yk], yb[:, g0 * 128:(g0 + 4) * 128], pb[:, :512])
            I("sp", "dma_start", [yk], [("out", "y", tt)], out=y_own[tt * 128:(tt + 1) * 128, :], in_=yb, slot="oy%d" % yk[1])
        G.barrier()
        A.release(mY)
    A.release(mB)
    G.barrier()
    A.release(mSP)

    def bfv(bank):
        return bank[:, :].bitcast(BF16)

    if cfg.do_s:
        wring = Ring("w", [v3(A.alloc(1024, BF16), 8) for _ in range(4)])
        sqr = Ring("sq", [A.alloc(512, BF16) for _ in range(2)])
        rsr = Ring("rs", [A.alloc(512) for _ in range(2)])
        wqr = Ring("wq", [v3(A.alloc(512, BF16), 4) for _ in range(2)])
        wvr = Ring("wvh", [v3(A.alloc(128, BF16), 2) for _ in range(2)])
        wukT_sb = v3(A.alloc(16 * 256, BF16), 16)
        stab = A.alloc(64)
        smask_sb = A.alloc(NB * 64)
        ptxi = A.alloc(NB * NGRP, I32)
        ptxf = A.alloc(NB * NGRP)
        idxi = A.alloc(NB * NGRP, I32)
        I("pool", "dma_start", [], ["wukT"], out=wukT_sb[0:64].rearrange("p a b -> p (a b)"), in_=wukT, slot="wukT")
        I("sp", "dma_start", [], ["stab"], out=stab, in_=mtab[:, 16:80], slot="stab")
        I("sp", "dma_start", [], ["smask"], out=smask_sb[0:64, :], in_=smask, slot="smask")
        I("sp", "dma_start", [], ["ptxi"], out=ptxi, in_=ptx, slot="ptx")
        I("dve", "tensor_copy", ["ptxi"], ["ptxf"], out=ptxf, in_=ptxi)
        I("dve", "tensor_scalar", ["ptxf", "vec"], ["ptxf"], out=ptxf, in0=ptxf, scalar1=16.0, scalar2=vcol(V_R16), op0=ALU.mult, op1=ALU.add)
        I("dve", "tensor_copy", ["ptxf"], ["idx"], out=idxi, in_=ptxf)
        wuv3 = wuv.rearrange("p (kc f) -> p kc f", kc=2)
        NT = NS
        tcs = [(0, NS)]
        xkey = lambda kc: ("xs", kc)
        ogT = v3(A.alloc(8 * NT, BF16), 8)
        ogkey = lambda kc: ("ogs", kc)
        qln = v3(A.alloc(4 * NT, BF16), 4)
        qlnkey = lambda kc: ("qlns", kc)
        hT = v3(A.alloc(8 * NT, BF16), 8)
        hkey = lambda kc: ("hTs", kc)
        qraw = v3(A.alloc(4 * NT), 4)
        QA = [v3(A.alloc(16 * 64, BF16), 16) for _ in range(3)]
        OL = [v3(A.alloc(16 * 64, BF16), 16) for _ in range(2)]
        qnr = Ring("qn", [A.alloc(64, BF16) for _ in range(2)])
        t1r = Ring("t1s", [A.alloc(64) for _ in range(2)])
        t2r = Ring("t2s", [A.alloc(64) for _ in range(2)])
        gtr = Ring("gt", [A.alloc(2048, BF16) for _ in range(4)])
        gkr = Ring("gk", [A.alloc(256, BF16) for _ in range(4)])
        KTd = [[A.alloc(1024, BF16) for _ in range(3)] for _ in range(2)]
        Pr = Ring("P", [A.alloc(1024, BF16) for _ in range(2)])
        PTr = Ring("PT", [A.alloc(512, BF16) for _ in range(2)])
        Qbr = Ring("Qb", [v3(A.alloc(3 * 64, BF16), 3) for _ in range(2)])
        oacc = A.alloc(256)
        obf = A.alloc(256, BF16)
        Sn = A.alloc(64)
        sm = A.alloc(32)
        print("arena words used in sample phase:", A.top, "of", A.n)
        ktb = [bfv(banks[0]), bfv(banks[1]), bfv(banks[2])]
        ptb = bfv(banks[3])[:, 512:1024]
        pvb = banks[3]
        Sring = Ring("sS", [banks[4], banks[5], banks[6], banks[7]])

        def col(i):
            return sm[0:64, i:i + 1]

        for j in range(2):
            rst = rms_stats(xsT, xkey, 8, tcs, D)
            norm_to(hT, hkey, xsT, xkey, 8, tcs, rst, V_PRE + (2 + j) * 8)
            for oc in range(12):
                wk, w = load_w(wim, (j * 12 + oc) * 128)
                pk, pb = proj(wk, w, 8, hT, hkey, 0, NT)
                if oc < 4:
                    I("act", "activation", [pk], [("qraws", oc)], out=qraw[:, oc, :], in_=pb[:, :NT], func=AF.Copy)
                    qk, q = sqr.next()
                    I("act", "activation", [pk], [qk], out=q[:, :NT], in_=pb[:, :NT], func=AF.Square)
                    I("pe", "matmul", [qk, "onesb"], [("ssq", 0)], ssq_banks[0][:, :NT], onesb, q[:, :NT], start=(oc == 0), stop=(oc == 3))
                else:
                    I("act", "activation", [pk], [("ogs", oc - 4)], out=ogT[:, oc - 4, :], in_=pb[:, :NT], func=AF.Silu)
                if oc == 3:
                    rk, r = rstd_from(ssq_banks[0], ("ssq", 0), NT, 512)
                    for kc in range(4):
                        I("dve", "scalar_tensor_tensor", [("qraws", kc), rk, "vec"], [("qlns", kc)],
                          out=qln[:, kc, :], in0=qraw[:, kc, :], scalar=vcol(V_QN + j * 4 + kc), in1=r[:, :NT], op0=ALU.mult, op1=ALU.mult)
            for h in range(16):
                wqk, wq = wqr.next()
                I("pool", "dma_start", [], [wqk], out=wq.rearrange("p a b -> p (a b)"), in_=wuq[(j * 16 + h) * 128:(j * 16 + h + 1) * 128, :], slot="wq%d" % wqk[1])
                pk, pb = proj(wqk, wq, 4, qln, qlnkey, 0, NT, ring=trring)
                qnk, qn = qnr.next()
                I("act", "activation", [pk], [qnk], out=qn[0:64, :], in_=pb[0:64, :NT], func=AF.Copy)
                k1, t1 = t1r.next()
                k2, t2 = t2r.next()
                I("dve", "tensor_tensor", [pk, "stab"], [k1], out=t1[64:96, :], in0=pb[64:96, :NT], in1=stab[64:96, :], op=ALU.mult)
                I("dve", "tensor_tensor", [pk, "stab"], [k2], out=t2[64:96, :], in0=pb[96:128, :NT], in1=stab[96:128, :], op=ALU.mult)
                I("dve", "tensor_tensor", [k1, k2], [("QA", 2)], out=QA[2][0:32, h, :], in0=t1[64:96, :], in1=t2[64:96, :], op=ALU.add)
                for c2 in range(2):
                    pk2, pb2 = psring.next()
                    I("pe", "matmul", [qnk, "wukT"], [pk2], pb2[:, :NT], wukT_sb[0:64, h, c2 * 128:(c2 + 1) * 128], qn[0:64, :], start=True, stop=True)
                    copy_any("act" if c2 == 0 else "dve", [pk2], [("QA", c2)], QA[c2][:, h, :], pb2[:, :NT])
            G.barrier()
            items = [(b, g) for b in range(NB) for g in range(NGRP + 1)]
            st = {}

            def lcol(b):
                return col(10 + b % 2)

            def stA(it):
                b, g = it
                d = st.setdefault(it, {})
                if g == NGRP:
                    return
                gk_, gt = gtr.next()
                kk_, gkk = gkr.next()
                icol = idxi[:, b * NGRP + g:b * NGRP + g + 1]
                I("pool", "indirect_dma_start", ["idx"], [gk_], out=gt, out_offset=None, in_=cckv,
                  in_offset=bass.IndirectOffsetOnAxis(ap=icol, axis=0), bounds_check=cfg.n_pool * 16 - 1, oob_is_err=False, slot="gt%d" % gk_[1])
                I("pool", "indirect_dma_start", ["idx"], [kk_], out=gkk, out_offset=None, in_=ckr,
                  in_offset=bass.IndirectOffsetOnAxis(ap=icol, axis=0), bounds_check=cfg.n_pool * 16 - 1, oob_is_err=False, slot="gk%d" % kk_[1])
                kd = d["kd"] = (b * (NGRP + 1) + g) % 2
                KTb = d["KT"] = KTd[kd]
                for jj in range(8):
                    I("pe", "transpose", [gk_, "identb"], [("ps", 0)], ktb[0][:, jj * 128:(jj + 1) * 128], gt[:, jj * 256:jj * 256 + 128], identb)
                    I("pe", "transpose", [gk_, "identb"], [("ps", 1)], ktb[1][:, jj * 128:(jj + 1) * 128], gt[:, jj * 256 + 128:jj * 256 + 256], identb)
                    I("pe", "transpose", [kk_, "identb"], [("ps", 2)], ktb[2][0:32, jj * 128:(jj + 1) * 128], gkk[:, jj * 32:(jj + 1) * 32], identb)
                I("dve", "tensor_copy", [("ps", 0)], [("KT", kd, 0)], out=KTb[0], in_=ktb[0])
                I("act", "activation", [("ps", 1)], [("KT", kd, 1)], out=KTb[1], in_=ktb[1], func=AF.Copy)
                I("dve", "tensor_copy", [("ps", 2)], [("KT", kd, 2)], out=KTb[2][0:32, :], in_=ktb[2][0:32, :])
                d["gk_"] = gk_
                d["gt"] = gt

            def stB(it):
                b, g = it
                d = st[it]
                newk = (g == NGRP)
                if g == 0:
                    Qbk, Qb = Qbr.next()
                    for c in range(3):
                        np_ = 128 if c < 2 else 32
                        I("dve", "tensor_copy", [("QA", c)], [Qbk], out=Qb[:np_, c, :].rearrange("p (h t) -> p h t", t=LS), in_=QA[c][:np_, :, b * LS:(b + 1) * LS])
                    I("dve", "memset", [], ["m_old"], col(0), NEG)
                    I("dve", "memset", [], [("l", b % 2)], lcol(b), 0.0)
                    st[("Qb", b)] = (Qbk, Qb)
                Qbk, Qb = st[("Qb", b)]
                if not newk:
                    kd = d["kd"]
                    KTb = d["KT"]
                    Ssrc = []
                    for hh in range(2):
                        sk, sb = Sring.next()
                        I("pe", "matmul", [Qbk, ("KT", kd, 0)], [sk], sb[0:64, :512], Qb[:, 0, :], KTb[0][:, hh * 512:(hh + 1) * 512], start=True, stop=False)
                        I("pe", "matmul", [Qbk, ("KT", kd, 1)], [sk], sb[0:64, :512], Qb[:, 1, :], KTb[1][:, hh * 512:(hh + 1) * 512], start=False, stop=False)
                        I("pe", "matmul", [Qbk, ("KT", kd, 2)], [sk], sb[0:64, :512], Qb[0:32, 2, :], KTb[2][0:32, hh * 512:(hh + 1) * 512], start=False, stop=True)
                        I("dve", "tensor_reduce", [sk], [("gm", hh)], out=col(5 + hh), in_=sb[0:64, :512], axis=AX.X, op=ALU.max)
                        Ssrc.append((sk, sb[0:64, :512], 512))
                else:
                    sk, sb = Sring.next()
                    I("pe", "matmul", [Qbk, "cTs"], [sk], sb[0:64, :64], Qb[:, 0, :], cTs[:, 0, :], start=True, stop=False)
                    I("pe", "matmul", [Qbk, "cTs"], [sk], sb[0:64, :64], Qb[:, 1, :], cTs[:, 1, :], start=False, stop=False)
                    I("pe", "matmul", [Qbk, "krTs"], [sk], sb[0:64, :64], Qb[0:32, 2, :], krTs[0:32, :], start=False, stop=True)
                    I("dve", "tensor_tensor", [sk, "smask"], ["Sn"], out=Sn[0:64, :], in0=sb[0:64, :64], in1=smask_sb[0:64, b * 64:(b + 1) * 64], op=ALU.add)
                    I("dve", "tensor_reduce", ["Sn"], [("gm", 0)], out=col(5), in_=Sn[0:64, :], axis=AX.X, op=ALU.max)
                    I("dve", "tensor_copy", [("gm", 0)], [("gm", 1)], out=col(6), in_=col(5))
                    Ssrc = [("Sn", Sn[0:64, :], 64)]
                ai = (b * (NGRP + 1) + g) % 4
                alk = ("alpha", ai)
                al = col(12 + ai)
                I("dve", "tensor_tensor", [("gm", 0), ("gm", 1)], ["m_new"], out=col(1), in0=col(5), in1=col(6), op=ALU.max)
                I("dve", "tensor_tensor", ["m_new", "m_old"], ["m_new"], out=col(1), in0=col(1), in1=col(0), op=ALU.max)
                I("dve", "tensor_scalar", ["m_new"], ["negb"], out=col(2), in0=col(1), scalar1=-SCALE, scalar2=None, op0=ALU.mult)
                I("act", "activation", ["m_old", "negb"], [alk], out=al, in_=col(0), func=AF.Exp, scale=SCALE, bias=col(2))
                I("dve", "memset", [], ["rs"], sm[0:64, 7:9], 0.0)
                Pk, P = Pr.next()
                off = 0
                for hi, (sk, sap, w_) in enumerate(Ssrc):
                    I("act", "activation", [sk, "negb", "rs"], [Pk, "rs"], out=P[0:64, off:off + w_], in_=sap, func=AF.Exp, scale=SCALE, bias=col(2), accum_out=col(7 + hi))
                    off += w_
                I("dve", "scalar_tensor_tensor", [("l", b % 2), alk, "rs"], [("l", b % 2)], out=lcol(b), in0=lcol(b), scalar=al, in1=col(7), op0=ALU.mult, op1=ALU.add)
                if not newk:
                    I("dve", "tensor_tensor", [("l", b % 2), "rs"], [("l", b % 2)], out=lcol(b), in0=lcol(b), in1=col(8), op=ALU.add)
                I("dve", "tensor_copy", ["m_new"], ["m_old"], out=col(0), in_=col(1))
                d["Pk"] = Pk
                d["P"] = P
                d["alk"] = alk
                d["al"] = al

            def stC(it):
                b, g = it
                d = st.pop(it)
                newk = (g == NGRP)
                if g == 0:
                    I("dve", "memset", [], ["oacc"], oacc[0:64, :], 0.0)
                nblk = 8 if not newk else 1
                bw = 128 if not newk else 64
                Pk, P = d["Pk"], d["P"]
                PTk, PT = PTr.next()
                for jj in range(nblk):
                    I("pe", "transpose", [Pk, "identb"], [("pt", 0)], ptb[:bw, jj * 64:(jj + 1) * 64], P[0:64, jj * bw:(jj + 1) * bw], identb[0:64, 0:64])
                I("dve", "tensor_copy", [("pt", 0)], [PTk], out=PT[:bw, :nblk * 64], in_=ptb[:bw, :nblk * 64])
                for jj in range(nblk):
                    if not newk:
                        I("pe", "matmul", [PTk, d["gk_"]], [("pt", 0)], pvb[0:64, :256], PT[:, jj * 64:(jj + 1) * 64], d["gt"][:, jj * 256:(jj + 1) * 256], start=(jj == 0), stop=(jj == nblk - 1))
                    else:
                        I("pe", "matmul", [PTk, "cns_tok"], [("pt", 0)], pvb[0:64, :256], PT[0:64, 0:64], cns_tok[0:64, :], start=True, stop=True)
                I("dve", "scalar_tensor_tensor", [("pt", 0), d["alk"], "oacc"], ["oacc"], out=oacc[0:64, :], in0=oacc[0:64, :], scalar=d["al"], in1=pvb[0:64, :256], op0=ALU.mult, op1=ALU.add)
                if newk:
                    I("act", "activation", [("l", b % 2)], ["rl"], out=col(9), in_=lcol(b), func=AF.Ln)
                    I("act", "activation", ["rl"], ["rl"], out=col(9), in_=col(9), func=AF.Exp, scale=-1.0)
                    I("dve", "tensor_scalar", ["oacc", "rl"], ["obf"], out=obf[0:64, :], in0=oacc[0:64, :], scalar1=col(9), scalar2=None, op0=ALU.mult)
                    for c2 in range(2):
                        I("pe", "transpose", ["obf", "identb"], [("pt", 0)], ptb[:, c2 * 64:(c2 + 1) * 64], obf[0:64, c2 * 128:(c2 + 1) * 128], identb[0:64, 0:64])
                    for c2 in range(2):
                        I("dve", "tensor_copy", [("pt", 0)], [("OL", c2)], out=OL[c2][:, :, b * LS:(b + 1) * LS], in_=ptb[:, c2 * 64:(c2 + 1) * 64].rearrange("p (h t) -> p h t", t=LS))

            for i in range(len(items) + 2):
                if i < len(items):
                    stA(items[i])
                if 0 <= i - 1 < len(items):
                    stB(items[i - 1])
                if 0 <= i - 2 < len(items):
                    stC(items[i - 2])
            G.barrier()
            for h in range(16):
                hb = (h % 2) * 64
                pr = h // 2
                wvk, wvh = wvr.next()
                I("pool", "dma_start", [], [wvk], out=wvh, in_=wuv3[:, :, h * 64:(h + 1) * 64], slot="wvh%d" % wvk[1])
                pk, pb = trring.next()
                for c2 in range(2):
                    I("pe", "matmul", [wvk, ("OL", c2)], [pk], pb[0:64, :NT], wvh[:, c2, :], OL[c2][:, h, :], start=(c2 == 0), stop=(c2 == 1))
                I("dve", "tensor_tensor", [pk, ("ogs", pr)], [("ogs", pr)], out=ogT[hb:hb + 64, pr, :], in0=pb[0:64, :NT], in1=ogT[hb:hb + 64, pr, :], op=ALU.mult)
            G.barrier()
            mO2 = A.mark()
            out_proj_residual(wom, j * 8 * 128, ogT, ogkey, xsT, xkey, tcs, NT, V_POST + (2 + j) * 8)
            G.barrier()
            A.release(mO2)
        ysb = A.alloc(1024)
        for g0 in range(0, 8, 4):
            pk, pb = trring.next()
            for i in range(4):
                I("pe", "transpose", [("xs", g0 + i), "ident"], [pk], pb[:NS, i * 128:(i + 1) * 128], xsT[:, g0 + i, :], ident)
            copy_any("act" if g0 == 0 else "dve", [pk], ["ysb"], ysb[:NS, g0 * 128:(g0 + 4) * 128], pb[:NS, :512])
        I("sp", "dma_start", ["ysb"], [("out", "ys")], out=ys, in_=ysb[:NS, :], slot="oys")

    G.barrier()
    G.add("sp", lambda e: None)
    print('total ops', G.n, {e: len(v) for e, v in G.ops.items()})
    G.emit(nc, stack, getattr(cfg, 'limit', None))
    stack.close()
    return nc


OWN_CHUNKS = {0: (0, 3, 4, 7), 1: (1, 2, 5, 6)}


def _chunked_w(w, ncols_chunk=128):
    K, N = w.shape
    a = w.reshape(K // 128, 128, N // 128, 128)
    a = a.transpose(2, 1, 0, 3)
    return np.ascontiguousarray(a).reshape(N // 128 * 128, K // 128 * 128)


def _rope_tab(pos):
    half = 16
    inv = (10000.0 ** (-np.arange(half, dtype=np.float32) / half)).astype(np.float32)
    ang = pos.astype(np.float32)[None, :] * inv[:, None]
    cos = np.cos(ang).astype(np.float32)
    sin = np.sin(ang).astype(np.float32)
    tab = np.zeros((128, len(pos)), np.float32)
    tab[64:80] = cos
    tab[80:96] = cos
    tab[96:112] = -sin
    tab[112:128] = sin
    return tab


def prep_inputs(cfg, x_prompt, x_sample, cache_ckv, cache_krope, state_conv, page_table, meta_tokens,
                pre_norm_g, post_norm_g, w_in_conv, conv_w, w_out_conv, kv_norm_g, w_dkv,
                kv_lat_norm_g, w_uk, w_uv, w_in_mla, q_norm_g, w_uq, w_out_mla):
    f = np.float32
    shared = {}
    shared["meta"] = np.ascontiguousarray(meta_tokens, f)
    shared["ident"] = np.eye(128, dtype=f)
    shared["wic"] = np.concatenate([_chunked_w(np.asarray(w_in_conv[l])) for l in range(2)], 0)
    shared["woc"] = np.concatenate([_chunked_w(np.asarray(w_out_conv[l])) for l in range(2)], 0)
    wd = np.asarray(w_dkv)
    wkr = wd[:, 256:288]
    wd2 = np.concatenate([np.zeros((1024, 64), f), wkr, wkr[:, 16:32], wkr[:, 0:16]], 1)
    shared["wdkv"] = np.concatenate([_chunked_w(wd[:, 0:128]), _chunked_w(wd[:, 128:256]), _chunked_w(wd2)], 0)
    shared["wim"] = np.concatenate([_chunked_w(np.asarray(w_in_mla[j])) for j in range(2)], 0)
    uq = []
    for j in range(2):
        w = np.asarray(w_uq[j]).reshape(512, 16, 96)
        wh = np.concatenate([w[:, :, 0:64], w[:, :, 64:96], w[:, :, 80:96], w[:, :, 64:80]], 2)
        for h in range(16):
            uq.append(_chunked_w(np.ascontiguousarray(wh[:, h, :])))
    shared["wuq"] = np.concatenate(uq, 0)
    shared["wom"] = np.concatenate([_chunked_w(np.asarray(w_out_mla[j])) for j in range(2)], 0)
    shared["wuk"] = np.ascontiguousarray(np.asarray(w_uk).reshape(2, 128, 1024).transpose(1, 0, 2)).reshape(128, 2048)
    shared["wuv"] = np.ascontiguousarray(np.asarray(w_uv).reshape(2, 128, 1024).transpose(1, 0, 2)).reshape(128, 2048)
    shared["wukT"] = np.ascontiguousarray(np.asarray(w_uk).transpose(2, 1, 0)).reshape(64, 16 * 256)
    kpos = np.concatenate([np.arange(16, T), np.arange(0, 16)])
    shared["ktab"] = _rope_tab(kpos)
    npg = cfg.n_pages
    past = npg * PAGE
    spos = np.tile(past + np.arange(LS), NB)
    shared["mtab"] = np.concatenate([_rope_tab(np.arange(16)), _rope_tab(spos)], 1)
    shared["cckv"] = np.asarray(cache_ckv).reshape(cfg.n_pool * 16, 2048)
    shared["ckr"] = np.asarray(cache_krope).reshape(cfg.n_pool * 16, 256)
    sm = np.full((64, NB, NB, LS), NEG, f)
    for b in range(NB):
        for t in range(LS):
            sm[np.arange(16) * 4 + t, b, b, 0:t + 1] = 0.0
    shared["smask"] = sm.reshape(64, NB * 64)

    def fm(v):
        return np.asarray(v, f).reshape(-1, 128).T

    vec = np.zeros((128, NV), f)
    for l in range(4):
        vec[:, V_PRE + l * 8:V_PRE + l * 8 + 8] = fm(pre_norm_g[l])
        vec[:, V_POST + l * 8:V_POST + l * 8 + 8] = fm(post_norm_g[l])
    vec[:, V_KVN:V_KVN + 8] = fm(kv_norm_g)
    vec[:, V_LAT:V_LAT + 2] = fm(kv_lat_norm_g)
    for j in range(2):
        vec[:, V_QN + j * 4:V_QN + j * 4 + 4] = fm(q_norm_g[j])
    vec[:, V_R16] = np.arange(128) % 16
    vec[:, V_EPS] = EPS
    for l in range(2):
        for k in range(3):
            vec[:, V_CW + l * 24 + k * 8:V_CW + l * 24 + k * 8 + 8] = fm(conv_w[l, k])

    ki = np.arange(128)[:, None]
    qi = np.arange(512)[None, :]
    pat = np.zeros((2, 128, 8, 512), f)
    for d in range(4):
        tri = ((d * 128 + ki) <= qi).astype(f)
        pat[0, :, d, :] = tri
        pat[1, :, d, :] = 1.0
        pat[1, :, 4 + d, :] = tri

    in_maps = []
    xp_all = np.asarray(x_prompt)
    xs_all = np.asarray(x_sample)
    sc_all = np.asarray(state_conv)
    pt_all = np.asarray(page_table)
    for c in range(NCORES):
        b, r = c // 2, c % 2
        m = dict(shared)
        m["xp"] = np.ascontiguousarray(xp_all[b])
        m["xs"] = np.ascontiguousarray(xs_all[c * NB:(c + 1) * NB].reshape(NS, D))
        m["sconv"] = np.ascontiguousarray(sc_all[:, c * NB:(c + 1) * NB].reshape(64, D))
        v = vec.copy()
        own = OWN_CHUNKS[r]
        for s in range(4):
            v[:, V_BLEND + 2 * s] = 1.0 if own[s] == 2 * s else 0.0
            v[:, V_BLEND + 2 * s + 1] = 1.0 if own[s] == 2 * s + 1 else 0.0
        m["vecs"] = v
        qpos = np.concatenate([16 + ch * 512 + np.arange(512) for ch in own])
        m["qtab"] = _rope_tab(qpos)
        mk = np.stack([pat[0 if own[par] == 2 * par else 1] for par in range(2)], 0)
        m["masks"] = np.ascontiguousarray(mk).reshape(2 * 128, 4096)
        pt = pt_all[c * NB:(c + 1) * NB]
        e = pt.reshape(NB, npg // 8, 8)
        e = np.repeat(e[:, :, :, None], 16, 3)
        m["ptx"] = np.ascontiguousarray(e.transpose(2, 3, 0, 1)).reshape(128, NB * (npg // 8)).astype(np.int32)
        in_maps.append(m)
    return in_maps


def assemble(res, cfg):
    y_prompt = np.zeros((4, SEQ, D), np.float32)
    y_sample = np.zeros((128, LS, D), np.float32)
    ckv_p = np.zeros((4, T, 256), np.float32)
    kr_p = np.zeros((4, T, 32), np.float32)
    conv_p = np.zeros((2, 4, 2, D), np.float32)
    ckv_s = np.zeros((128, LS, 256), np.float32)
    kr_s = np.zeros((128, LS, 32), np.float32)
    conv_s = np.zeros((2, 128, 2, D), np.float32)
    for c in range(NCORES):
        b, r = c // 2, c % 2
        o = res[c]
        for s, ch in enumerate(OWN_CHUNKS[r]):
            y_prompt[b, ch * 512:(ch + 1) * 512] = o["y_own"][s * 512:(s + 1) * 512]
        y_sample[c * NB:(c + 1) * NB] = o["ys"].reshape(NB, LS, D)
        if r == 0:
            ckv_p[b] = o["ckv_p"]
            kr_p[b] = o["kr_p"]
            conv_p[:, b] = o["conv_p"].reshape(2, 2, D)
        ckv_s[c * NB:(c + 1) * NB] = o["ckv_s"].reshape(NB, LS, 256)
        kr_s[c * NB:(c + 1) * NB] = o["kr_s"].reshape(NB, LS, 32)
        conv_s[:, c * NB:(c + 1) * NB] = o["conv_s"].reshape(2, NB, 2, D)
    return (y_prompt, y_sample, ckv_p, kr_p, conv_p, ckv_s, kr_s, conv_s)


_NC_CACHE = {}


def kernel(**inputs):
    cfg = Cfg(n_pool=inputs["cache_ckv"].shape[0], n_pages=inputs["page_table"].shape[1])
    key = (cfg.n_pool, cfg.n_pages)
    if key not in _NC_CACHE:
        _NC_CACHE[key] = build(cfg)
    nc = _NC_CACHE[key]
    in_maps = prep_inputs(cfg, **inputs)
    res = run_bass_kernel_spmd(nc, in_maps, core_ids=list(range(NCORES)))
    return assemble(res.results, cfg)
```

```python
import contextlib
import numpy as np
import concourse.bass as bass
import concourse.mybir as mybir
from concourse.bass_utils import run_bass_kernel_spmd

F32 = mybir.dt.float32
BF16 = mybir.dt.bfloat16
I32 = mybir.dt.int32
ALU = mybir.AluOpType
AF = mybir.ActivationFunctionType
AX = mybir.AxisListType

D = 1024
KC = 8
SEQ = 4096
NMETA = 16
T = SEQ + NMETA
NCORES = 8
NB = 16
LS = 4
NS = NB * LS
EPS = 1e-6
SCALE = 96 ** -0.5
PAGE = 128
NEG = -30000.0

ENGS = ("pe", "act", "dve", "pool", "sp")
import os
XENG = os.environ.get("XENG", "act,dve").split(",")
PSUM_KEYS = ("ps", "pt", "ssq", "sS")


class Op:
    __slots__ = ("eng", "fn", "deps", "signal", "semval", "is_dma", "slot", "idx")

    def __init__(self, eng, fn):
        self.eng = eng
        self.fn = fn
        self.deps = ()
        self.signal = False
        self.semval = 0
        self.is_dma = False
        self.slot = None


class Graph:
    def __init__(self):
        self.ops = {e: [] for e in ENGS}
        self.last_w = {}
        self.readers = {}
        self.slot_count = {}
        self.n = 0
        self.barrier_ops = []
        self.pending_dma = []
        self.last_on = {}

    def add(self, eng, fn, reads=(), writes=(), slot=None):
        op = Op(eng, fn)
        op.idx = self.n
        self.n += 1
        deps = {}

        def dep(o):
            if o is not None and o is not op:
                deps[id(o)] = o

        for k in reads:
            dep(self.last_w.get(k))
            if isinstance(k, tuple) and k[0] in PSUM_KEYS:
                r = self.readers.get(k)
                if r:
                    for ek, o in r.items():
                        if ek != eng:
                            dep(o)
        for k in writes:
            dep(self.last_w.get(k))
            r = self.readers.get(k)
            if r:
                for o in r.values():
                    dep(o)
        for o in self.barrier_ops:
            dep(o)
        op.deps = list(deps.values())
        for o in op.deps:
            o.signal = True
        for k in reads:
            r = self.readers.setdefault(k, {})
            if slot is not None:
                r[("dma", op.idx)] = op
            else:
                r[eng] = op
        for k in writes:
            self.last_w[k] = op
            self.readers[k] = {}
        if slot is not None:
            op.is_dma = True
            op.slot = slot
            c = self.slot_count.get(slot, 0) + 1
            self.slot_count[slot] = c
            op.semval = 16 * c
            self.pending_dma.append(op)
        else:
            self.last_on[eng] = op
        self.ops[eng].append(op)
        return op

    def barrier(self):
        ops = list(self.last_on.values()) + self.pending_dma
        self.barrier_ops = ops
        self.pending_dma = []

    def emit(self, nc, stack, limit=None):
        if limit is not None:
            for e in ENGS:
                self.ops[e] = [o for o in self.ops[e] if o.idx < limit]
        esem = {e: stack.enter_context(nc.semaphore("s_" + e)) for e in ENGS if e != "sp"}
        ssem = {s: stack.enter_context(nc.semaphore("d_%d" % i)) for i, s in enumerate(self.slot_count)}
        for e in ENGS:
            c = 0
            for op in self.ops[e]:
                if not op.is_dma and op.signal:
                    c += 1
                    op.semval = c

        def sem_of(o):
            return ssem[o.slot] if o.is_dma else esem[o.eng]

        def run(ename, eng):
            known = {}
            for op in self.ops[ename]:
                need = {}
                for d in op.deps:
                    if ename == "pe" and d.eng == "pe" and not d.is_dma:
                        continue
                    s = sem_of(d)
                    v = d.semval
                    key = id(s)
                    if known.get(key, 0) >= v:
                        continue
                    if key not in need or need[key][1] < v:
                        need[key] = (s, v)
                for key, (s, v) in need.items():
                    eng.wait_ge(s, v)
                    known[key] = v
                inst = op.fn(eng)
                if inst is None:
                    continue
                if op.is_dma:
                    inst.then_inc(ssem[op.slot], 16)
                elif op.signal:
                    inst.then_inc(esem[ename], 1)

        block = stack.enter_context(nc.Block())

        @block.sync
        def _(e):
            run("sp", e)

        @block.scalar
        def _(e):
            run("act", e)

        @block.vector
        def _(e):
            run("dve", e)

        @block.gpsimd
        def _(e):
            run("pool", e)

        @block.tensor
        def _(e):
            run("pe", e)


class Arena:
    def __init__(self, big, nwords):
        self.big = big
        self.n = nwords
        self.top = 0

    def mark(self):
        return self.top

    def release(self, m):
        self.top = m

    def alloc(self, nelem, dtype=F32):
        words = (nelem * (2 if dtype == BF16 else 4) + 3) // 4
        words = (words + 7) // 8 * 8
        off = self.top
        self.top += words
        assert self.top <= self.n, "SBUF arena overflow: %d > %d words" % (self.top, self.n)
        ap = self.big[:, off:off + words]
        if dtype != F32:
            ap = ap.bitcast(dtype)
        return ap[:, 0:nelem]


class Ring:
    def __init__(self, name, aps):
        self.name = name
        self.aps = aps
        self.i = 0

    def next(self):
        i = self.i % len(self.aps)
        self.i += 1
        return (self.name, i), self.aps[i]


def chunks(n, c=512):
    return [(i, min(c, n - i)) for i in range(0, n, c)]


class Cfg:
    def __init__(self, n_pool=10240, n_pages=64, do_b=True, do_s=True):
        self.n_pool = n_pool
        self.n_pages = n_pages
        self.do_b = do_b
        self.do_s = do_s


V_PRE = 0
V_POST = 32
V_KVN = 64
V_LAT = 72
V_QN = 74
V_CW = 82
V_BLEND = 130
V_R16 = 138
V_EPS = 139
NV = 140


def build(cfg):
    nc = bass.Bass("TRN2", target_bir_lowering=False)
    G = Graph()
    stack = contextlib.ExitStack()

    regcache = {}

    def I(eng, method, reads, writes, *args, slot=None, **kw):
        def fn(e):
            try:
                if "bounds_check" in kw and not isinstance(kw["bounds_check"], (type(None),)) and isinstance(kw["bounds_check"], int):
                    if "bcreg" not in regcache:
                        regcache["bcreg"] = e.to_reg(kw["bounds_check"])
                    kw2 = dict(kw)
                    kw2["bounds_check"] = regcache["bcreg"]
                    return getattr(e, method)(*args, **kw2)
                return getattr(e, method)(*args, **kw)
            except Exception:
                print("FAILED OP", eng, method, writes, [str(a)[:200] for a in args], {k: str(v)[:200] for k, v in kw.items()})
                raise
        return G.add(eng, fn, reads=reads, writes=writes, slot=slot)

    def din(name, shape, dt=F32):
        return nc.dram_tensor(name, list(shape), dt, kind="ExternalInput").ap()

    def dout(name, shape, dt=F32):
        return nc.dram_tensor(name, list(shape), dt, kind="ExternalOutput").ap()

    NPG = cfg.n_pages
    NGRP = NPG // 8
    xp = din("xp", [SEQ, D])
    meta = din("meta", [NMETA, D])
    xs = din("xs", [NS, D])
    sconv = din("sconv", [64, D])
    vecs = din("vecs", [128, NV])
    ident_d = din("ident", [128, 128])
    wic = din("wic", [2 * 32 * 128, 1024])
    woc = din("woc", [2 * 8 * 128, 1024])
    wdkv = din("wdkv", [3 * 128, 1024])
    wim = din("wim", [2 * 12 * 128, 1024])
    wuq = din("wuq", [2 * 16 * 128, 512])
    wom = din("wom", [2 * 8 * 128, 1024])
    wuk = din("wuk", [128, 2048])
    wuv = din("wuv", [128, 2048])
    wukT = din("wukT", [64, 16 * 256])
    ktab = din("ktab", [128, T])
    mtab = din("mtab", [128, 80])
    qtab = din("qtab", [128, 2048])
    masks = din("masks", [2 * 128, 4096])
    ptx = din("ptx", [128, NB * NGRP], I32)
    smask = din("smask", [64, NB * 64])
    cckv = din("cckv", [cfg.n_pool * 16, 2048])
    ckr = din("ckr", [cfg.n_pool * 16, 256])

    y_own = dout("y_own", [2048, D])
    ys = dout("ys", [NS, D])
    ckv_p = dout("ckv_p", [T, 256])
    kr_p = dout("kr_p", [T, 32])
    conv_p = dout("conv_p", [4, D])
    ckv_s = dout("ckv_s", [NS, 256])
    kr_s = dout("kr_s", [NS, 32])
    conv_s = dout("conv_s", [64, D])

    NW = 53000
    big = stack.enter_context(nc.sbuf_tensor("arena", [128, NW], F32))
    A = Arena(big, NW)
    banks = [stack.enter_context(nc.psum_tensor("bank%d" % i, [128, 512], F32)) for i in range(8)]

    def v3(ap, a):
        return ap.rearrange("p (a b) -> p a b", a=a)

    def bt(ap, t):
        return ap.rearrange("p (b t) -> p b t", t=t)

    ident = A.alloc(128)
    identb = A.alloc(128, BF16)
    onesb = A.alloc(128, BF16)
    neghalf = A.alloc(512)
    vec = A.alloc(NV)
    I("sp", "dma_start", [], ["ident"], out=ident, in_=ident_d, slot="c_ident")
    I("sp", "dma_start", [], ["vec"], out=vec, in_=vecs, slot="c_vec")
    I("pool", "dma_start", [], ["identb"], out=identb, in_=ident_d, slot="c_identb")
    I("dve", "memset", [], ["onesb"], onesb, 1.0)
    I("dve", "memset", [], ["neghalf"], neghalf, -0.5)

    def vcol(off):
        return vec[:, off:off + 1]


    xsT = v3(A.alloc(8 * NS), 8)
    cTs = v3(A.alloc(2 * NS, BF16), 2)
    krTs = A.alloc(NS, BF16)
    cns_tok = A.alloc(256, BF16)
    mSP = A.mark()
    x_own = v3(A.alloc(8 * 2048), 8)
    cT = v3(A.alloc(2 * T, BF16), 2)
    Kbuf = [A.alloc(T, BF16) for _ in range(2)]
    halo = A.alloc(2 * 8 * 2)
    cvp = A.alloc(8 * 4)
    cvs = A.alloc(8 * 64)
    stT = A.alloc(8 * 64)

    for i in range(2):
        I("pool", "memset", [], [("Kb", i, "kr")], Kbuf[i], 0.0)

    psring = Ring("ps", [banks[i] for i in range(4)])
    ssq_banks = [banks[4], banks[5]]
    trring = Ring("pt", [banks[6], banks[7]])

    def copy_any(eng, reads, writes, out, in_):
        if eng == "act":
            I("act", "activation", reads, writes, out=out, in_=in_, func=AF.Copy)
        else:
            I(eng, "tensor_copy", reads, writes, out=out, in_=in_)

    def transposes_to(dst_fn, src, src_key, nrows, ncols, idn, eng_alt):
        nblk = (ncols + 127) // 128
        per = max(1, 512 // nrows)
        for g0 in range(0, nblk, per):
            pk, pb = trring.next()
            nb = min(per, nblk - g0)
            for i in range(nb):
                cb = g0 + i
                cw = min(128, ncols - cb * 128)
                I("pe", "transpose", [src_key, "ident"], [pk],
                  pb[:cw, i * nrows:(i + 1) * nrows], src[:nrows, cb * 128:cb * 128 + cw], idn[:nrows, :nrows])
            for i in range(nb):
                cb = g0 + i
                cw = min(128, ncols - cb * 128)
                dst, dkey = dst_fn(cb)
                copy_any(eng_alt[cb % len(eng_alt)], [pk], [dkey], dst, pb[:cw, i * nrows:(i + 1) * nrows])

    mA = A.mark()
    wring = Ring("w", [v3(A.alloc(1024, BF16), 8) for _ in range(4)])
    sqr = Ring("sq", [A.alloc(512, BF16) for _ in range(2)])
    rsr = Ring("rs", [A.alloc(512) for _ in range(2)])

    def load_w(dram, row0):
        k, w = wring.next()
        I("pool", "dma_start", [], [k], out=w.rearrange("p a b -> p (a b)"), in_=dram[row0:row0 + 128, :], slot="w%d" % k[1])
        return k, w

    def rstd_from(sb, skey, n, nfeat):
        rk, r = rsr.next()
        I("act", "activation", [skey, "vec"], [rk], out=r[:, :n], in_=sb[:, :n], func=AF.Ln, scale=1.0 / nfeat, bias=vcol(V_EPS))
        I("act", "activation", [rk], [rk], out=r[:, :n], in_=r[:, :n], func=AF.Exp, scale=-0.5)
        return rk, r

    def rms_stats(srcT, key_fn, nchunks, tcs, nfeat):
        res = []
        for ti, (c0, n) in enumerate(tcs):
            sb = ssq_banks[ti % 2]
            sk = ("ssq", ti % 2)
            for kc in range(nchunks):
                qk, q = sqr.next()
                I("act", "activation", [key_fn(kc)], [qk], out=q[:, :n], in_=srcT[:, kc, c0:c0 + n], func=AF.Square)
                I("pe", "matmul", [qk, "onesb"], [sk], sb[:, :n], onesb, q[:, :n], start=(kc == 0), stop=(kc == nchunks - 1))
            res.append(rstd_from(sb, sk, n, nfeat))
        return res

    def norm_to(dstT, dkey_fn, srcT, skey_fn, nchunks, tcs, rst, goff):
        for ti, (c0, n) in enumerate(tcs):
            rk, r = rst[ti]
            for kc in range(nchunks):
                I("dve", "scalar_tensor_tensor", [skey_fn(kc), rk, "vec"], [dkey_fn(kc)],
                  out=dstT[:, kc, c0:c0 + n], in0=srcT[:, kc, c0:c0 + n], scalar=vcol(goff + kc), in1=r[:, :n], op0=ALU.mult, op1=ALU.mult)

    def proj(wk, w, nk, srcT, skey_fn, c0, n, ring=None):
        pk, pb = (ring or psring).next()
        for kc in range(nk):
            I("pe", "matmul", [wk, skey_fn(kc)], [pk], pb[:, :n], w[:, kc, :], srcT[:, kc, c0:c0 + n], start=(kc == 0), stop=(kc == nk - 1))
        return pk, pb

    def out_proj_residual(wdram, wrow0, srcT, skey_fn, xT, xkey_fn, tcs, NT, goff):
        mT = v3(A.alloc(8 * NT), 8)
        assert len(tcs) <= 2
        for of in range(8):
            wk, w = load_w(wdram, wrow0 + of * 128)
            for ti, (c0, n) in enumerate(tcs):
                pk, pb = proj(wk, w, 8, srcT, skey_fn, c0, n)
                I("act", "activation", [pk], [("mT", of, ti)], out=mT[:, of, c0:c0 + n], in_=pb[:, :n], func=AF.Copy)
                qk, q = sqr.next()
                I("act", "activation", [pk], [qk], out=q[:, :n], in_=pb[:, :n], func=AF.Square)
                I("pe", "matmul", [qk, "onesb"], [("ssq", ti % 2)], ssq_banks[ti % 2][:, :n], onesb, q[:, :n], start=(of == 0), stop=(of == 7))
        for ti, (c0, n) in enumerate(tcs):
            rk, r = rstd_from(ssq_banks[ti % 2], ("ssq", ti % 2), n, D)
            for of in range(8):
                I("dve", "scalar_tensor_tensor", [("mT", of, ti), rk, "vec"], [("mT", of, ti)],
                  out=mT[:, of, c0:c0 + n], in0=mT[:, of, c0:c0 + n], scalar=vcol(goff + of), in1=r[:, :n], op0=ALU.mult, op1=ALU.mult)
                I("dve", "tensor_tensor", [("mT", of, ti), xkey_fn(of)], [xkey_fn(of)],
                  out=xT[:, of, c0:c0 + n], in0=xT[:, of, c0:c0 + n], in1=mT[:, of, c0:c0 + n], op=ALU.add)

    def run_group(gi, NT, segs, x_loads, tabs, dests, own_slot):
        tcs = chunks(NT)
        mG = A.mark()
        xT = v3(A.alloc(8 * NT), 8)
        gT = v3(A.alloc(8 * NT, BF16), 8)
        xkey = lambda kc: ("xT", kc)
        mS = A.mark()
        xr = Ring("xin", [A.alloc(1024) for _ in range(2)])
        for (src, ntok, col0) in x_loads:
            xk, xb = xr.next()
            I("sp", "dma_start", [], [xk], out=xb[:ntok, :], in_=src, slot="xin%d" % xk[1])
            transposes_to(lambda cb, col0=col0, ntok=ntok: (xT[:, cb, col0:col0 + ntok], ("xT", cb)), xb, xk, ntok, 1024, ident, XENG)
        G.barrier()
        A.release(mS)
        for l in range(2):
            mL = A.mark()
            hT = v3(A.alloc(8 * NT, BF16), 8)
            hkey = lambda kc: ("hT", kc)
            cbuf = A.alloc(NT)
            ybuf = A.alloc(NT)
            tbuf = A.alloc(NT)
            szb = A.alloc(NT)
            vexts = [A.alloc(sg["nseq"] * (sg["L"] + 2)) for sg in segs]
            rst = rms_stats(xT, xkey, 8, tcs, D)
            norm_to(hT, hkey, xT, xkey, 8, tcs, rst, V_PRE + l * 8)
            for j in range(8):
                hoff = (l * 8 + j) * 2
                cwb = V_CW + l * 24 + j
                for si, sg in enumerate(segs):
                    ve = vexts[si]
                    L = sg["L"]
                    if sg["kind"] == "meta":
                        I("dve", "memset", [], [("vext", si)], ve[:, 0:2], 0.0)
                    elif sg["kind"] == "prompt":
                        I("dve", "tensor_copy", [("halo", l, j)], [("vext", si)], out=ve[:, 0:2], in_=halo[:, hoff:hoff + 2])
                    else:
                        I("dve", "tensor_copy", ["stT"], [("vext", si)], out=bt(ve, L + 2)[:, :, 0:2],
                          in_=stT[:, j * 64 + l * 32:j * 64 + l * 32 + 32].rearrange("p (b k) -> p b k", k=2))
                for part, pname in ((1, "c"), (2, "u"), (0, "b"), (3, "z")):
                    if pname == "b":
                        for si, sg in enumerate(segs):
                            ve = vexts[si]
                            L = sg["L"]
                            s0 = sg["col0"]
                            if sg["nseq"] == 1:
                                src = [ve[:, k:k + L] for k in range(3)]
                                yv = ybuf[:, s0:s0 + L]
                                tail = ve[:, L:L + 2]
                            else:
                                ve3 = bt(ve, L + 2)
                                src = [ve3[:, :, k:k + L] for k in range(3)]
                                yv = bt(ybuf[:, s0:s0 + sg["nseq"] * L], L)
                                tail = ve3[:, :, L:L + 2]
                            vk, yk = ("vext", si), ("ybuf", si)
                            I("dve", "tensor_scalar", [vk, "vec"], [yk], out=yv, in0=src[0], scalar1=vcol(cwb), scalar2=None, op0=ALU.mult)
                            I("dve", "scalar_tensor_tensor", [vk, "vec", yk], [yk], out=yv, in0=src[1], scalar=vcol(cwb + 8), in1=yv, op0=ALU.mult, op1=ALU.add)
                            I("dve", "scalar_tensor_tensor", [vk, "vec", yk], [yk], out=yv, in0=src[2], scalar=vcol(cwb + 16), in1=yv, op0=ALU.mult, op1=ALU.add)
                            if sg["kind"] in ("meta", "prompt"):
                                I("act", "activation", [vk], [("halo", l, j)], out=halo[:, hoff:hoff + 2], in_=tail, func=AF.Copy)
                                if sg.get("last"):
                                    I("act", "activation", [vk], ["cvp"], out=cvp[:, j * 4 + l * 2:j * 4 + l * 2 + 2], in_=tail, func=AF.Copy)
                            else:
                                I("act", "activation", [vk], ["cvs"], out=cvs[:, j * 64 + l * 32:j * 64 + l * 32 + 32].rearrange("p (b k) -> p b k", k=2),
                                  in_=tail, func=AF.Copy)
                    wk, w = load_w(wic, (l * 32 + part * 8 + j) * 128)
                    for ti, (c0, n) in enumerate(tcs):
                        pk, pb = proj(wk, w, 8, hT, hkey, c0, n)
                        if pname == "c":
                            I("act", "activation", [pk], [("cbuf", ti)], out=cbuf[:, c0:c0 + n], in_=pb[:, :n], func=AF.Copy)
                        elif pname == "u":
                            for si, sg in enumerate(segs):
                                L = sg["L"]
                                s0 = sg["col0"]
                                lo = max(c0, s0)
                                hi = min(c0 + n, s0 + sg["nseq"] * L)
                                if hi <= lo:
                                    continue
                                ve = vexts[si]
                                if sg["nseq"] == 1:
                                    o = ve[:, 2 + lo - s0:2 + hi - s0]
                                    a = pb[:, lo - c0:hi - c0]
                                    b_ = cbuf[:, lo:hi]
                                else:
                                    assert lo == s0 and hi == s0 + sg["nseq"] * L
                                    o = bt(ve, L + 2)[:, :, 2:2 + L]
                                    a = bt(pb[:, lo - c0:hi - c0], L)
                                    b_ = bt(cbuf[:, lo:hi], L)
                                I("dve", "tensor_tensor", [pk, ("cbuf", ti)], [("vext", si)], out=o, in0=a, in1=b_, op=ALU.mult)
                        elif pname == "b":
                            I("dve", "tensor_tensor", [pk] + [("ybuf", si) for si in range(len(segs))], [("tbuf", ti)],
                              out=tbuf[:, c0:c0 + n], in0=pb[:, :n], in1=ybuf[:, c0:c0 + n], op=ALU.mult)
                        else:
                            I("act", "activation", [pk], [("szb", ti)], out=szb[:, c0:c0 + n], in_=pb[:, :n], func=AF.Silu)
                            I("dve", "tensor_tensor", [("tbuf", ti), ("szb", ti)], [("gT", j)],
                              out=gT[:, j, c0:c0 + n], in0=tbuf[:, c0:c0 + n], in1=szb[:, c0:c0 + n], op=ALU.mult)
            G.barrier()
            A.release(mL)
            out_proj_residual(woc, l * 8 * 128, gT, lambda kc: ("gT", kc), xT, xkey, tcs, NT, V_POST + l * 8)
            G.barrier()
            A.release(mL)
        mL = A.mark()
        hT = gT
        hkey = lambda kc: ("gT", kc)
        craw = v3(A.alloc(2 * NT), 2)
        tabb = A.alloc(NT)
        krf = A.alloc(NT)
        krt = A.alloc(NT)
        ctr = Ring("ctok", [A.alloc(256) for _ in range(2)])
        ktr = Ring("ktok", [A.alloc(32) for _ in range(2)])
        for (tsrc, c0, n) in tabs:
            I("sp", "dma_start", [], ["tabb"], out=tabb[:, c0:c0 + n], in_=tsrc, slot="tabb")
        rst = rms_stats(xT, xkey, 8, tcs, D)
        norm_to(hT, hkey, xT, xkey, 8, tcs, rst, V_KVN)
        for ocl in range(3):
            wk, w = load_w(wdkv, ocl * 128)
            for ti, (c0, n) in enumerate(tcs):
                pk, pb = proj(wk, w, 8, hT, hkey, c0, n)
                if ocl < 2:
                    I("act", "activation", [pk], [("craw", ocl, ti)], out=craw[:, ocl, c0:c0 + n], in_=pb[:, :n], func=AF.Copy)
                    qk, q = sqr.next()
                    I("act", "activation", [pk], [qk], out=q[:, :n], in_=pb[:, :n], func=AF.Square)
                    I("pe", "matmul", [qk, "onesb"], [("ssq", ti % 2)], ssq_banks[ti % 2][:, :n], onesb, q[:, :n], start=(ocl == 0), stop=(ocl == 1))
                else:
                    I("dve", "tensor_tensor", [pk, "tabb"], [("krf", ti)], out=krf[64:96, c0:c0 + n], in0=pb[64:96, :n], in1=tabb[64:96, c0:c0 + n], op=ALU.mult)
                    I("dve", "tensor_tensor", [pk, "tabb"], [("krt", ti)], out=krt[64:96, c0:c0 + n], in0=pb[96:128, :n], in1=tabb[96:128, c0:c0 + n], op=ALU.mult)
                    I("dve", "tensor_tensor", [("krf", ti), ("krt", ti)], [("krf", ti)], out=krf[64:96, c0:c0 + n], in0=krf[64:96, c0:c0 + n], in1=krt[64:96, c0:c0 + n], op=ALU.add)
        for ti, (c0, n) in enumerate(tcs):
            rk, r = rstd_from(ssq_banks[ti % 2], ("ssq", ti % 2), n, 256)
            for ocl in range(2):
                I("dve", "scalar_tensor_tensor", [("craw", ocl, ti), rk, "vec"], [("craw", ocl, ti)],
                  out=craw[:, ocl, c0:c0 + n], in0=craw[:, ocl, c0:c0 + n], scalar=vcol(V_LAT + ocl), in1=r[:, :n], op0=ALU.mult, op1=ALU.mult)
        for dd in dests:
            c0, n = dd["c0"], dd["n"]
            tis = sorted(set(ti for ti, (a, m) in enumerate(tcs) if a < c0 + n and a + m > c0))
            rkeys = [("craw", ocl, ti) for ocl in range(2) for ti in tis]
            kkeys = [("krf", ti) for ti in tis]
            for ocl in range(2):
                I("act", "activation", rkeys, [dd["cT_key"]], out=dd["cT"][:, ocl, :], in_=craw[:, ocl, c0:c0 + n], func=AF.Copy)
            for (kap, kkey) in dd["krT"]:
                I("act", "activation", kkeys, [kkey], out=kap, in_=krf[64:96, c0:c0 + n], func=AF.Copy)
            for t0 in range(0, n, 128):
                tw = min(128, n - t0)
                ck, cb_ = ctr.next()
                kk, kb_ = ktr.next()
                pk, pb = trring.next()
                for ocl in range(2):
                    I("pe", "transpose", rkeys + ["ident"], [pk], pb[:tw, ocl * 128:(ocl + 1) * 128], craw[:, ocl, c0 + t0:c0 + t0 + tw], ident)
                I("pe", "transpose", kkeys + ["ident"], [pk], pb[:tw, 256:288], krf[64:96, c0 + t0:c0 + t0 + tw], ident[64:96, 64:96])
                I("act", "activation", [pk], [ck], out=cb_[:tw, :], in_=pb[:tw, 0:256], func=AF.Copy)
                I("dve", "tensor_copy", [pk], [kk], out=kb_[:tw, :], in_=pb[:tw, 256:288])
                r0 = dd["row0"] + t0
                I("sp", "dma_start", [ck], [("out", dd["name"], "c", r0)], out=dd["ckv_out"][r0:r0 + tw, :], in_=cb_[:tw, :], slot="octok%d" % ck[1])
                I("sp", "dma_start", [kk], [("out", dd["name"], "k", r0)], out=dd["kr_out"][r0:r0 + tw, :], in_=kb_[:tw, :], slot="oktok%d" % kk[1])
                if dd.get("tok_bf") is not None:
                    I("dve", "tensor_copy", [pk], ["cns_tok"], out=dd["tok_bf"][:tw, :], in_=pb[:tw, 0:256])
        if own_slot is not None:
            s = own_slot
            for kc in range(8):
                xo = x_own[:, kc, s * 512:(s + 1) * 512]
                I("dve", "tensor_scalar", [("xT", kc), "vec"], [("x_own", s, kc)], out=xo, in0=xT[:, kc, 0:512], scalar1=vcol(V_BLEND + 2 * s), scalar2=None, op0=ALU.mult)
                I("dve", "scalar_tensor_tensor", [("xT", kc), "vec", ("x_own", s, kc)], [("x_own", s, kc)],
                  out=xo, in0=xT[:, kc, 512:1024], scalar=vcol(V_BLEND + 2 * s + 1), in1=xo, op0=ALU.mult, op1=ALU.add)
        else:
            for kc in range(8):
                I("act", "activation", [("xT", kc)], [("xs", kc)], out=xsT[:, kc, :], in_=xT[:, kc, 16:80], func=AF.Copy)
        G.barrier()
        A.release(mG)

    mS0 = A.mark()
    scb = A.alloc(1024)
    I("sp", "dma_start", [], ["scb"], out=scb[:64, :], in_=sconv, slot="scb")
    transposes_to(lambda cb: (stT[:, cb * 64:(cb + 1) * 64], "stT"), scb, "scb", 64, 1024, ident, ["act", "dve"])
    G.barrier()
    A.release(mS0)

    run_group(
        0, 80,
        [dict(col0=0, nseq=1, L=16, kind="meta"), dict(col0=16, nseq=NB, L=LS, kind="sample")],
        [(meta, 16, 0), (xs, 64, 16)],
        [(mtab, 0, 80)],
        [dict(c0=0, n=16, cT=cT[:, :, SEQ:SEQ + 16], cT_key=("cT", "m"), krT=[(Kbuf[i][64:96, SEQ:SEQ + 16], ("Kb", i, "kr")) for i in range(2)],
              ckv_out=ckv_p, kr_out=kr_p, row0=0, name="p"),
         dict(c0=16, n=64, cT=cTs, cT_key="cTs", krT=[(krTs[0:32, :], "krTs")],
              ckv_out=ckv_s, kr_out=kr_s, row0=0, name="s", tok_bf=cns_tok)],
        None)
    for g in range(4):
        run_group(
            1 + g, 1024,
            [dict(col0=0, nseq=1, L=1024, kind="prompt", last=(g == 3))],
            [(xp[g * 1024 + i * 128:g * 1024 + (i + 1) * 128, :], 128, i * 128) for i in range(8)],
            [(ktab[:, g * 1024:(g + 1) * 1024], 0, 1024)],
            [dict(c0=0, n=1024, cT=cT[:, :, g * 1024:(g + 1) * 1024], cT_key=("cT", g), krT=[(Kbuf[i][64:96, g * 1024:(g + 1) * 1024], ("Kb", i, "kr")) for i in range(2)],
                  ckv_out=ckv_p, kr_out=kr_p, row0=16 + g * 1024, name="p")],
            g)

    mO = A.mark()
    cvo = A.alloc(1024)
    cso = A.alloc(1024)
    for j in range(8):
        pk, pb = trring.next()
        I("pe", "transpose", ["cvp", "ident"], [pk], pb[:4, 0:128], cvp[:, j * 4:(j + 1) * 4], ident)
        I("pe", "transpose", ["cvs", "ident"], [pk], pb[:64, 128:256], cvs[:, j * 64:(j + 1) * 64], ident)
        I("dve", "tensor_copy", [pk], ["cvo"], out=cvo[:4, j * 128:(j + 1) * 128], in_=pb[:4, 0:128])
        I("act", "activation", [pk], ["cso"], out=cso[:64, j * 128:(j + 1) * 128], in_=pb[:64, 128:256], func=AF.Copy)
    I("sp", "dma_start", ["cvo"], [("out", "conv_p")], out=conv_p, in_=cvo[:4, :], slot="o_cvo")
    I("sp", "dma_start", ["cso"], [("out", "conv_s")], out=conv_s, in_=cso[:64, :], slot="o_cso")
    G.barrier()
    A.release(mO)
    A.release(mA)


    mB = A.mark()
    wring = Ring("w", [v3(A.alloc(1024, BF16), 8) for _ in range(4)])
    sqr = Ring("sq", [A.alloc(512, BF16) for _ in range(2)])
    rsr = Ring("rs", [A.alloc(512) for _ in range(2)])
    Vh = v3(A.alloc(33 * 128, BF16), 33)
    maskb = [A.alloc(4096, BF16) for _ in range(2)]
    kmax2 = A.alloc(16)
    half = A.alloc(1)
    wqr = Ring("wq", [v3(A.alloc(512, BF16), 4) for _ in range(2)])
    wkr = Ring("wkh", [v3(A.alloc(128, BF16), 2) for _ in range(2)])
    wvr = Ring("wvh", [v3(A.alloc(128, BF16), 2) for _ in range(2)])
    poring = Ring("ssq", [banks[4], banks[5]])
    I("dve", "memset", [], ["Vh"], Vh.rearrange("p a b -> p (a b)"), 1.0)
    I("dve", "memset", [], ["half"], half, 0.5)
    for par in range(2):
        I("pool", "dma_start", [], ["maskb"], out=maskb[par], in_=masks[par * 128:(par + 1) * 128, :], slot="maskb%d" % par)
    wuk3 = wuk.rearrange("p (kc f) -> p kc f", kc=2)
    wuv3 = wuv.rearrange("p (kc f) -> p kc f", kc=2)
    print("arena words used before phase-B halves:", A.top, "of", A.n)

    def mla_prompt(j, hf):
        NT = 1024
        tcs = chunks(NT)
        xT = x_own[:, :, hf * 1024:(hf + 1) * 1024]
        xkey = lambda kc: ("xo", hf, kc)
        first = (j == 0 and hf == 0)
        mH = A.mark()
        ogT = v3(A.alloc(8 * NT, BF16), 8)
        ogkey = lambda kc: ("og", kc)
        qln = v3(A.alloc(4 * NT, BF16), 4)
        qlnkey = lambda kc: ("qln", kc)
        mX = A.mark()
        hT = v3(A.alloc(8 * NT, BF16), 8)
        hkey = lambda kc: ("hT", kc)
        qraw = v3(A.alloc(4 * NT), 4)
        rst = rms_stats(xT, xkey, 8, tcs, D)
        norm_to(hT, hkey, xT, xkey, 8, tcs, rst, V_PRE + (2 + j) * 8)
        for oc in range(12):
            wk, w = load_w(wim, (j * 12 + oc) * 128)
            for ti, (c0, n) in enumerate(tcs):
                pk, pb = proj(wk, w, 8, hT, hkey, c0, n)
                if oc < 4:
                    I("act", "activation", [pk], [("qraw", oc, ti)], out=qraw[:, oc, c0:c0 + n], in_=pb[:, :n], func=AF.Copy)
                    qk, q = sqr.next()
                    I("act", "activation", [pk], [qk], out=q[:, :n], in_=pb[:, :n], func=AF.Square)
                    I("pe", "matmul", [qk, "onesb"], [("ssq", ti % 2)], ssq_banks[ti % 2][:, :n], onesb, q[:, :n], start=(oc == 0), stop=(oc == 3))
                else:
                    I("act", "activation", [pk], [("og", oc - 4)], out=ogT[:, oc - 4, c0:c0 + n], in_=pb[:, :n], func=AF.Silu)
            if oc == 3:
                for ti, (c0, n) in enumerate(tcs):
                    rk, r = rstd_from(ssq_banks[ti % 2], ("ssq", ti % 2), n, 512)
                    for kc in range(4):
                        I("dve", "scalar_tensor_tensor", [("qraw", kc, ti), rk, "vec"], [("qln", kc)],
                          out=qln[:, kc, c0:c0 + n], in0=qraw[:, kc, c0:c0 + n], scalar=vcol(V_QN + j * 4 + kc), in1=r[:, :n], op0=ALU.mult, op1=ALU.mult)
        G.barrier()
        A.release(mX)
        Qr = Ring("Qh", [A.alloc(NT, BF16) for _ in range(2)])
        Ptr = Ring("Pt", [A.alloc(512, BF16) for _ in range(5)])
        qtabb = A.alloc(NT)
        rsum = Ring("rsum", [A.alloc(512) for _ in range(2)])
        tmpf = Ring("tmpf", [A.alloc(512) for _ in range(2)])
        t1r = Ring("t1", [A.alloc(512) for _ in range(2)])
        t2r = Ring("t2", [A.alloc(512) for _ in range(2)])
        negr = Ring("negm", [A.alloc(1) for _ in range(2)])
        qmx = A.alloc(4)
        kmx = A.alloc(16)
        I("sp", "dma_start", [], ["qtabb"], out=qtabb, in_=qtab[:, hf * 1024:(hf + 1) * 1024], slot="qtabb")
        for (qk_, qb_) in [Qr.next(), Qr.next()]:
            I("pool", "memset", [], [qk_], qb_, 0.0)
        for h in range(16):
            hb = (h % 2) * 64
            pr = h // 2
            Kk = ("Kb", h % 2)
            Kh = Kbuf[h % 2]
            wqk, wq = wqr.next()
            I("pool", "dma_start", [], [wqk], out=wq.rearrange("p a b -> p (a b)"), in_=wuq[(j * 16 + h) * 128:(j * 16 + h + 1) * 128, :], slot="wq%d" % wqk[1])
            wkk, wkh = wkr.next()
            I("pool", "dma_start", [], [wkk], out=wkh, in_=wuk3[:, :, h * 64:(h + 1) * 64], slot="wkh%d" % wkk[1])
            wvk, wvh = wvr.next()
            I("pool", "dma_start", [], [wvk], out=wvh, in_=wuv3[:, :, h * 64:(h + 1) * 64], slot="wvh%d" % wvk[1])
            Qk, Qh = Qr.next()
            for ti, (c0, n) in enumerate(tcs):
                pk, pb = proj(wqk, wq, 4, qln, qlnkey, c0, n, ring=trring)
                I("act", "activation", [pk], [Qk], out=Qh[0:64, c0:c0 + n], in_=pb[0:64, :n], func=AF.Copy)
                k1, t1 = t1r.next()
                k2, t2 = t2r.next()
                I("dve", "tensor_tensor", [pk, "qtabb"], [k1], out=t1[64:96, :n], in0=pb[64:96, :n], in1=qtabb[64:96, c0:c0 + n], op=ALU.mult)
                I("dve", "tensor_tensor", [pk, "qtabb"], [k2], out=t2[64:96, :n], in0=pb[96:128, :n], in1=qtabb[96:128, c0:c0 + n], op=ALU.mult)
                I("dve", "tensor_tensor", [k1, k2], [Qk], out=Qh[64:96, c0:c0 + n], in0=t1[64:96, :n], in1=t2[64:96, :n], op=ALU.add)
            for ci, (c0, n) in enumerate(chunks(T)):
                pk, pb = trring.next()
                for kc in range(2):
                    I("pe", "matmul", [wkk, "cT"], [pk], pb[0:64, :n], wkh[:, kc, :], cT[:, kc, c0:c0 + n], start=(kc == 0), stop=(kc == 1))
                I("dve", "tensor_copy", [pk], [Kk], out=Kh[0:64, c0:c0 + n], in_=pb[0:64, :n])
            for g0 in range(0, 33, 8):
                pk, pb = trring.next()
                nb = min(8, 33 - g0)
                for i in range(nb):
                    kb = g0 + i
                    kk = 128 if kb < 32 else 16
                    for kc in range(2):
                        I("pe", "matmul", [wvk, "cT"], [pk], pb[:kk, i * 64:(i + 1) * 64], cT[:, kc, kb * 128:kb * 128 + kk], wvh[:, kc, :], start=(kc == 0), stop=(kc == 1))
                if nb == 8:
                    I("dve", "tensor_copy", [pk], ["Vh"], out=Vh[:, g0:g0 + 8, 0:64], in_=pb[:, :512].rearrange("p (a b) -> p a b", a=8))
                else:
                    I("dve", "tensor_copy", [pk], ["Vh"], out=Vh[:16, 32, 0:64], in_=pb[:16, 0:64])
            if first:
                for ci, (c0, n) in enumerate(chunks(T)):
                    qk, q = sqr.next()
                    I("act", "activation", [Kk, ("Kb", h % 2, "kr")], [qk], out=q[0:96, :n], in_=Kh[0:96, c0:c0 + n], func=AF.Square)
                    pk, pb = trring.next()
                    I("pe", "matmul", [qk, "onesb"], [pk], pb[:, :n], onesb[0:96, :], q[0:96, :n], start=True, stop=True)
                    I("dve", "tensor_reduce", [pk], [("kmx", ci)], out=kmx[:, ci:ci + 1], in_=pb[:, :n], axis=AX.X, op=ALU.max)
                I("dve", "tensor_reduce", [("kmx", ci) for ci in range(9)], ["kmax2"], out=kmax2[:, h:h + 1], in_=kmx[:, 0:9], axis=AX.X, op=ALU.max)
            for ti, (c0, n) in enumerate(tcs):
                qk, q = sqr.next()
                I("act", "activation", [Qk], [qk], out=q[0:96, :n], in_=Qh[0:96, c0:c0 + n], func=AF.Square)
                pk, pb = trring.next()
                I("pe", "matmul", [qk, "onesb"], [pk], pb[:, :n], onesb[0:96, :], q[0:96, :n], start=True, stop=True)
                I("dve", "tensor_reduce", [pk], [("qmx", ti)], out=qmx[:, ti:ti + 1], in_=pb[:, :n], axis=AX.X, op=ALU.max)
            nk, negm = negr.next()
            I("dve", "tensor_tensor", [("qmx", 0), ("qmx", 1)], [nk], out=negm, in0=qmx[:, 0:1], in1=qmx[:, 1:2], op=ALU.max)
            I("dve", "tensor_tensor", [nk, "kmax2"], [nk], out=negm, in0=negm, in1=kmax2[:, h:h + 1], op=ALU.mult)
            I("act", "activation", [nk], [nk], out=negm, in_=negm, func=AF.Ln)
            I("act", "activation", [nk], [nk], out=negm, in_=negm, func=AF.Exp, scale=0.5)
            I("dve", "tensor_scalar", [nk], [nk], out=negm, in0=negm, scalar1=-SCALE * 1.02, scalar2=None, op0=ALU.mult)
            LAG = 3
            items = []
            for sl in range(2):
                sg = 2 * hf + sl
                seq = list(range(8 * sg + 8)) + [32]
                for idx, kb in enumerate(seq):
                    items.append((sl, sg, idx, kb, len(seq)))
            pos = {}
            info = {}

            def qk_stage(it):
                sl, sg, idx, kb, nseq = it
                if idx == 0:
                    pos[sl] = poring.next()
                kk = 128 if kb < 32 else 16
                q0 = sl * 512
                pk, pb = psring.next()
                I("pe", "matmul", [Kk, ("Kb", h % 2, "kr"), Qk], [pk], pb[:kk, :512], Kh[:, kb * 128:kb * 128 + kk], Qh[:, q0:q0 + 512], start=True, stop=True)
                ptk, pt = Ptr.next()
                I("act", "activation", [pk, nk], [ptk], out=pt[:kk, :], in_=pb[:kk, :512], func=AF.Exp, scale=SCALE, bias=negm[:kk, :])
                if kb < 32 and kb >= 8 * sg:
                    mi = kb - 8 * sg
                    I("dve", "tensor_tensor", [ptk, "maskb"], [ptk], out=pt, in0=pt, in1=maskb[sg % 2][:, mi * 512:(mi + 1) * 512], op=ALU.mult)
                info[it] = (ptk, pt, kk)

            def pv_stage(it):
                sl, sg, idx, kb, nseq = it
                ptk, pt, kk = info.pop(it)
                pok, po = pos[sl]
                q0 = sl * 512
                I("pe", "matmul", [ptk, "Vh"], [pok], po[:, :512], Vh[:kk, kb, :], pt[:kk, :], start=(idx == 0), stop=(idx == nseq - 1))
                if idx == nseq - 1:
                    rsk, rs = rsum.next()
                    tmk, tm = tmpf.next()
                    I("act", "activation", [pok], [rsk], out=rs[hb:hb + 64, :], in_=po[64:128, :512], func=AF.Ln)
                    I("act", "activation", [rsk], [rsk], out=rs[hb:hb + 64, :], in_=rs[hb:hb + 64, :], func=AF.Exp, scale=-1.0)
                    I("dve", "tensor_tensor", [pok, rsk], [tmk], out=tm[hb:hb + 64, :], in0=po[0:64, :512], in1=rs[hb:hb + 64, :], op=ALU.mult)
                    I("dve", "tensor_tensor", [tmk, ("og", pr)], [("og", pr)], out=ogT[hb:hb + 64, pr, q0:q0 + 512], in0=tm[hb:hb + 64, :], in1=ogT[hb:hb + 64, pr, q0:q0 + 512], op=ALU.mult)

            for i in range(len(items) + LAG):
                if i < len(items):
                    qk_stage(items[i])
                if i - LAG >= 0:
                    pv_stage(items[i - LAG])
        G.barrier()
        A.release(mX)
        out_proj_residual(wom, j * 8 * 128, ogT, ogkey, xT, xkey, tcs, NT, V_POST + (2 + j) * 8)
        G.barrier()
        A.release(mH)

    if cfg.do_b:
        for j in range(2):
            for hf in range(2):
                mla_prompt(j, hf)
        mY = A.mark()
        ytr = Ring("ytok", [A.alloc(1024) for _ in range(2)])
        for tt in range(16):
            yk, yb = ytr.next()
            for g0 in range(0, 8, 4):
                pk, pb = trring.next()
                for i in range(4):
                    kc = g0 + i
                    I("pe", "transpose", [("xo", tt // 8, kc), "ident"], [pk], pb[:, i * 128:(i + 1) * 128], x_own[:, kc, tt * 128:(tt + 1) * 128], ident)
                copy_any("act" if g0 == 0 else "dve", [pk], [yk], yb[:, g0 * 128:(g0 + 4) * 128], pb[:, :512])
            I("sp", "dma_start", [yk], [("out", "y", tt)], out=y_own[tt * 128:(tt + 1) * 128, :], in_=yb, slot="oy%d" % yk[1])
        G.barrier()
        A.release(mY)
    A.release(mB)
    G.barrier()
    A.release(mSP)

    def bfv(bank):
        return bank[:, :].bitcast(BF16)

    if cfg.do_s:
        wring = Ring("w", [v3(A.alloc(1024, BF16), 8) for _ in range(4)])
        sqr = Ring("sq", [A.alloc(512, BF16) for _ in range(2)])
        rsr = Ring("rs", [A.alloc(512) for _ in range(2)])
        wqr = Ring("wq", [v3(A.alloc(512, BF16), 4) for _ in range(2)])
        wvr = Ring("wvh", [v3(A.alloc(128, BF16), 2) for _ in range(2)])
        wukT_sb = v3(A.alloc(16 * 256, BF16), 16)
        stab = A.alloc(64)
        smask_sb = A.alloc(NB * 64)
        ptxi = A.alloc(NB * NGRP, I32)
        ptxf = A.alloc(NB * NGRP)
        idxi = A.alloc(NB * NGRP, I32)
        I("pool", "dma_start", [], ["wukT"], out=wukT_sb[0:64].rearrange("p a b -> p (a b)"), in_=wukT, slot="wukT")
        I("sp", "dma_start", [], ["stab"], out=stab, in_=mtab[:, 16:80], slot="stab")
        I("sp", "dma_start", [], ["smask"], out=smask_sb[0:64, :], in_=smask, slot="smask")
        I("sp", "dma_start", [], ["ptxi"], out=ptxi, in_=ptx, slot="ptx")
        I("dve", "tensor_copy", ["ptxi"], ["ptxf"], out=ptxf, in_=ptxi)
        I("dve", "tensor_scalar", ["ptxf", "vec"], ["ptxf"], out=ptxf, in0=ptxf, scalar1=16.0, scalar2=vcol(V_R16), op0=ALU.mult, op1=ALU.add)
        I("dve", "tensor_copy", ["ptxf"], ["idx"], out=idxi, in_=ptxf)
        wuv3 = wuv.rearrange("p (kc f) -> p kc f", kc=2)
        NT = NS
        tcs = [(0, NS)]
        xkey = lambda kc: ("xs", kc)
        ogT = v3(A.alloc(8 * NT, BF16), 8)
        ogkey = lambda kc: ("ogs", kc)
        qln = v3(A.alloc(4 * NT, BF16), 4)
        qlnkey = lambda kc: ("qlns", kc)
        hT = v3(A.alloc(8 * NT, BF16), 8)
        hkey = lambda kc: ("hTs", kc)
        qraw = v3(A.alloc(4 * NT), 4)
        QA = [v3(A.alloc(16 * 64, BF16), 16) for _ in range(3)]
        OL = [v3(A.alloc(16 * 64, BF16), 16) for _ in range(2)]
        qnr = Ring("qn", [A.alloc(64, BF16) for _ in range(2)])
        t1r = Ring("t1s", [A.alloc(64) for _ in range(2)])
        t2r = Ring("t2s", [A.alloc(64) for _ in range(2)])
        gtr = Ring("gt", [A.alloc(2048, BF16) for _ in range(4)])
        gkr = Ring("gk", [A.alloc(256, BF16) for _ in range(4)])
        KTd = [[A.alloc(1024, BF16) for _ in range(3)] for _ in range(2)]
        Pr = Ring("P", [A.alloc(1024, BF16) for _ in range(2)])
        PTr = Ring("PT", [A.alloc(512, BF16) for _ in range(2)])
        Qbr = Ring("Qb", [v3(A.alloc(3 * 64, BF16), 3) for _ in range(2)])
        oacc = A.alloc(256)
        obf = A.alloc(256, BF16)
        Sn = A.alloc(64)
        sm = A.alloc(32)
        print("arena words used in sample phase:", A.top, "of", A.n)
        ktb = [bfv(banks[0]), bfv(banks[1]), bfv(banks[2])]
        ptb = bfv(banks[3])[:, 512:1024]
        pvb = banks[3]
        Sring = Ring("sS", [banks[4], banks[5], banks[6], banks[7]])

        def col(i):
            return sm[0:64, i:i + 1]

        for j in range(2):
            rst = rms_stats(xsT, xkey, 8, tcs, D)
            norm_to(hT, hkey, xsT, xkey, 8, tcs, rst, V_PRE + (2 + j) * 8)
            for oc in range(12):
                wk, w = load_w(wim, (j * 12 + oc) * 128)
                pk, pb = proj(wk, w, 8, hT, hkey, 0, NT)
                if oc < 4:
                    I("act", "activation", [pk], [("qraws", oc)], out=qraw[:, oc, :], in_=pb[:, :NT], func=AF.Copy)
                    qk, q = sqr.next()
                    I("act", "activation", [pk], [qk], out=q[:, :NT], in_=pb[:, :NT], func=AF.Square)
                    I("pe", "matmul", [qk, "onesb"], [("ssq", 0)], ssq_banks[0][:, :NT], onesb, q[:, :NT], start=(oc == 0), stop=(oc == 3))
                else:
                    I("act", "activation", [pk], [("ogs", oc - 4)], out=ogT[:, oc - 4, :], in_=pb[:, :NT], func=AF.Silu)
                if oc == 3:
                    rk, r = rstd_from(ssq_banks[0], ("ssq", 0), NT, 512)
                    for kc in range(4):
                        I("dve", "scalar_tensor_tensor", [("qraws", kc), rk, "vec"], [("qlns", kc)],
                          out=qln[:, kc, :], in0=qraw[:, kc, :], scalar=vcol(V_QN + j * 4 + kc), in1=r[:, :NT], op0=ALU.mult, op1=ALU.mult)
            for h in range(16):
                wqk, wq = wqr.next()
                I("pool", "dma_start", [], [wqk], out=wq.rearrange("p a b -> p (a b)"), in_=wuq[(j * 16 + h) * 128:(j * 16 + h + 1) * 128, :], slot="wq%d" % wqk[1])
                pk, pb = proj(wqk, wq, 4, qln, qlnkey, 0, NT, ring=trring)
                qnk, qn = qnr.next()
                I("act", "activation", [pk], [qnk], out=qn[0:64, :], in_=pb[0:64, :NT], func=AF.Copy)
                k1, t1 = t1r.next()
                k2, t2 = t2r.next()
                I("dve", "tensor_tensor", [pk, "stab"], [k1], out=t1[64:96, :], in0=pb[64:96, :NT], in1=stab[64:96, :], op=ALU.mult)
                I("dve", "tensor_tensor", [pk, "stab"], [k2], out=t2[64:96, :], in0=pb[96:128, :NT], in1=stab[96:128, :], op=ALU.mult)
                I("dve", "tensor_tensor", [k1, k2], [("QA", 2)], out=QA[2][0:32, h, :], in0=t1[64:96, :], in1=t2[64:96, :], op=ALU.add)
                for c2 in range(2):
                    pk2, pb2 = psring.next()
                    I("pe", "matmul", [qnk, "wukT"], [pk2], pb2[:, :NT], wukT_sb[0:64, h, c2 * 128:(c2 + 1) * 128], qn[0:64, :], start=True, stop=True)
                    copy_any("act" if c2 == 0 else "dve", [pk2], [("QA", c2)], QA[c2][:, h, :], pb2[:, :NT])
            G.barrier()
            items = [(b, g) for b in range(NB) for g in range(NGRP + 1)]
            st = {}

            def lcol(b):
                return col(10 + b % 2)

            def stA(it):
                b, g = it
                d = st.setdefault(it, {})
                if g == NGRP:
                    return
                gk_, gt = gtr.next()
                kk_, gkk = gkr.next()
                icol = idxi[:, b * NGRP + g:b * NGRP + g + 1]
                I("pool", "indirect_dma_start", ["idx"], [gk_], out=gt, out_offset=None, in_=cckv,
                  in_offset=bass.IndirectOffsetOnAxis(ap=icol, axis=0), bounds_check=cfg.n_pool * 16 - 1, oob_is_err=False, slot="gt%d" % gk_[1])
                I("pool", "indirect_dma_start", ["idx"], [kk_], out=gkk, out_offset=None, in_=ckr,
                  in_offset=bass.IndirectOffsetOnAxis(ap=icol, axis=0), bounds_check=cfg.n_pool * 16 - 1, oob_is_err=False, slot="gk%d" % kk_[1])
                kd = d["kd"] = (b * (NGRP + 1) + g) % 2
                KTb = d["KT"] = KTd[kd]
                for jj in range(8):
                    I("pe", "transpose", [gk_, "identb"], [("ps", 0)], ktb[0][:, jj * 128:(jj + 1) * 128], gt[:, jj * 256:jj * 256 + 128], identb)
                    I("pe", "transpose", [gk_, "identb"], [("ps", 1)], ktb[1][:, jj * 128:(jj + 1) * 128], gt[:, jj * 256 + 128:jj * 256 + 256], identb)
                    I("pe", "transpose", [kk_, "identb"], [("ps", 2)], ktb[2][0:32, jj * 128:(jj + 1) * 128], gkk[:, jj * 32:(jj + 1) * 32], identb)
                I("dve", "tensor_copy", [("ps", 0)], [("KT", kd, 0)], out=KTb[0], in_=ktb[0])
                I("act", "activation", [("ps", 1)], [("KT", kd, 1)], out=KTb[1], in_=ktb[1], func=AF.Copy)
                I("dve", "tensor_copy", [("ps", 2)], [("KT", kd, 2)], out=KTb[2][0:32, :], in_=ktb[2][0:32, :])
                d["gk_"] = gk_
                d["gt"] = gt

            def stB(it):
                b, g = it
                d = st[it]
                newk = (g == NGRP)
                if g == 0:
                    Qbk, Qb = Qbr.next()
                    for c in range(3):
                        np_ = 128 if c < 2 else 32
                        I("dve", "tensor_copy", [("QA", c)], [Qbk], out=Qb[:np_, c, :].rearrange("p (h t) -> p h t", t=LS), in_=QA[c][:np_, :, b * LS:(b + 1) * LS])
                    I("dve", "memset", [], ["m_old"], col(0), NEG)
                    I("dve", "memset", [], [("l", b % 2)], lcol(b), 0.0)
                    st[("Qb", b)] = (Qbk, Qb)
                Qbk, Qb = st[("Qb", b)]
                if not newk:
                    kd = d["kd"]
                    KTb = d["KT"]
                    Ssrc = []
                    for hh in range(2):
                        sk, sb = Sring.next()
                        I("pe", "matmul", [Qbk, ("KT", kd, 0)], [sk], sb[0:64, :512], Qb[:, 0, :], KTb[0][:, hh * 512:(hh + 1) * 512], start=True, stop=False)
                        I("pe", "matmul", [Qbk, ("KT", kd, 1)], [sk], sb[0:64, :512], Qb[:, 1, :], KTb[1][:, hh * 512:(hh + 1) * 512], start=False, stop=False)
                        I("pe", "matmul", [Qbk, ("KT", kd, 2)], [sk], sb[0:64, :512], Qb[0:32, 2, :], KTb[2][0:32, hh * 512:(hh + 1) * 512], start=False, stop=True)
                        I("dve", "tensor_reduce", [sk], [("gm", hh)], out=col(5 + hh), in_=sb[0:64, :512], axis=AX.X, op=ALU.max)
                        Ssrc.append((sk, sb[0:64, :512], 512))
                else:
                    sk, sb = Sring.next()
                    I("pe", "matmul", [Qbk, "cTs"], [sk], sb[0:64, :64], Qb[:, 0, :], cTs[:, 0, :], start=True, stop=False)
                    I("pe", "matmul", [Qbk, "cTs"], [sk], sb[0:64, :64], Qb[:, 1, :], cTs[:, 1, :], start=False, stop=False)
                    I("pe", "matmul", [Qbk, "krTs"], [sk], sb[0:64, :64], Qb[0:32, 2, :], krTs[0:32, :], start=False, stop=True)
                    I("dve", "tensor_tensor", [sk, "smask"], ["Sn"], out=Sn[0:64, :], in0=sb[0:64, :64], in1=smask_sb[0:64, b * 64:(b + 1) * 64], op=ALU.add)
                    I("dve", "tensor_reduce", ["Sn"], [("gm", 0)], out=col(5), in_=Sn[0:64, :], axis=AX.X, op=ALU.max)
                    I("dve", "tensor_copy", [("gm", 0)], [("gm", 1)], out=col(6), in_=col(5))
                    Ssrc = [("Sn", Sn[0:64, :], 64)]
                ai = (b * (NGRP + 1) + g) % 4
                alk = ("alpha", ai)
                al = col(12 + ai)
                I("dve", "tensor_tensor", [("gm", 0), ("gm", 1)], ["m_new"], out=col(1), in0=col(5), in1=col(6), op=ALU.max)
                I("dve", "tensor_tensor", ["m_new", "m_old"], ["m_new"], out=col(1), in0=col(1), in1=col(0), op=ALU.max)
                I("dve", "tensor_scalar", ["m_new"], ["negb"], out=col(2), in0=col(1), scalar1=-SCALE, scalar2=None, op0=ALU.mult)
                I("act", "activation", ["m_old", "negb"], [alk], out=al, in_=col(0), func=AF.Exp, scale=SCALE, bias=col(2))
                I("dve", "memset", [], ["rs"], sm[0:64, 7:9], 0.0)
                Pk, P = Pr.next()
                off = 0
                for hi, (sk, sap, w_) in enumerate(Ssrc):
                    I("act", "activation", [sk, "negb", "rs"], [Pk, "rs"], out=P[0:64, off:off + w_], in_=sap, func=AF.Exp, scale=SCALE, bias=col(2), accum_out=col(7 + hi))
                    off += w_
                I("dve", "scalar_tensor_tensor", [("l", b % 2), alk, "rs"], [("l", b % 2)], out=lcol(b), in0=lcol(b), scalar=al, in1=col(7), op0=ALU.mult, op1=ALU.add)
                if not newk:
                    I("dve", "tensor_tensor", [("l", b % 2), "rs"], [("l", b % 2)], out=lcol(b), in0=lcol(b), in1=col(8), op=ALU.add)
                I("dve", "tensor_copy", ["m_new"], ["m_old"], out=col(0), in_=col(1))
                d["Pk"] = Pk
                d["P"] = P
                d["alk"] = alk
                d["al"] = al

            def stC(it):
                b, g = it
                d = st.pop(it)
                newk = (g == NGRP)
                if g == 0:
                    I("dve", "memset", [], ["oacc"], oacc[0:64, :], 0.0)
                nblk = 8 if not newk else 1
                bw = 128 if not newk else 64
                Pk, P = d["Pk"], d["P"]
                PTk, PT = PTr.next()
                for jj in range(nblk):
                    I("pe", "transpose", [Pk, "identb"], [("pt", 0)], ptb[:bw, jj * 64:(jj + 1) * 64], P[0:64, jj * bw:(jj + 1) * bw], identb[0:64, 0:64])
                I("dve", "tensor_copy", [("pt", 0)], [PTk], out=PT[:bw, :nblk * 64], in_=ptb[:bw, :nblk * 64])
                for jj in range(nblk):
                    if not newk:
                        I("pe", "matmul", [PTk, d["gk_"]], [("pt", 0)], pvb[0:64, :256], PT[:, jj * 64:(jj + 1) * 64], d["gt"][:, jj * 256:(jj + 1) * 256], start=(jj == 0), stop=(jj == nblk - 1))
                    else:
                        I("pe", "matmul", [PTk, "cns_tok"], [("pt", 0)], pvb[0:64, :256], PT[0:64, 0:64], cns_tok[0:64, :], start=True, stop=True)
                I("dve", "scalar_tensor_tensor", [("pt", 0), d["alk"], "oacc"], ["oacc"], out=oacc[0:64, :], in0=oacc[0:64, :], scalar=d["al"], in1=pvb[0:64, :256], op0=ALU.mult, op1=ALU.add)
                if newk:
                    I("act", "activation", [("l", b % 2)], ["rl"], out=col(9), in_=lcol(b), func=AF.Ln)
                    I("act", "activation", ["rl"], ["rl"], out=col(9), in_=col(9), func=AF.Exp, scale=-1.0)
                    I("dve", "tensor_scalar", ["oacc", "rl"], ["obf"], out=obf[0:64, :], in0=oacc[0:64, :], scalar1=col(9), scalar2=None, op0=ALU.mult)
                    for c2 in range(2):
                        I("pe", "transpose", ["obf", "identb"], [("pt", 0)], ptb[:, c2 * 64:(c2 + 1) * 64], obf[0:64, c2 * 128:(c2 + 1) * 128], identb[0:64, 0:64])
                    for c2 in range(2):
                        I("dve", "tensor_copy", [("pt", 0)], [("OL", c2)], out=OL[c2][:, :, b * LS:(b + 1) * LS], in_=ptb[:, c2 * 64:(c2 + 1) * 64].rearrange("p (h t) -> p h t", t=LS))

            for i in range(len(items) + 2):
                if i < len(items):
                    stA(items[i])
                if 0 <= i - 1 < len(items):
                    stB(items[i - 1])
                if 0 <= i - 2 < len(items):
                    stC(items[i - 2])
            G.barrier()
            for h in range(16):
                hb = (h % 2) * 64
                pr = h // 2
                wvk, wvh = wvr.next()
                I("pool", "dma_start", [], [wvk], out=wvh, in_=wuv3[:, :, h * 64:(h + 1) * 64], slot="wvh%d" % wvk[1])
                pk, pb = trring.next()
                for c2 in range(2):
                    I("pe", "matmul", [wvk, ("OL", c2)], [pk], pb[0:64, :NT], wvh[:, c2, :], OL[c2][:, h, :], start=(c2 == 0), stop=(c2 == 1))
                I("dve", "tensor_tensor", [pk, ("ogs", pr)], [("ogs", pr)], out=ogT[hb:hb + 64, pr, :], in0=pb[0:64, :NT], in1=ogT[hb:hb + 64, pr, :], op=ALU.mult)
            G.barrier()
            mO2 = A.mark()
            out_proj_residual(wom, j * 8 * 128, ogT, ogkey, xsT, xkey, tcs, NT, V_POST + (2 + j) * 8)
            G.barrier()
            A.release(mO2)
        ysb = A.alloc(1024)
        for g0 in range(0, 8, 4):
            pk, pb = trring.next()
            for i in range(4):
                I("pe", "transpose", [("xs", g0 + i), "ident"], [pk], pb[:NS, i * 128:(i + 1) * 128], xsT[:, g0 + i, :], ident)
            copy_any("act" if g0 == 0 else "dve", [pk], ["ysb"], ysb[:NS, g0 * 128:(g0 + 4) * 128], pb[:NS, :512])
        I("sp", "dma_start", ["ysb"], [("out", "ys")], out=ys, in_=ysb[:NS, :], slot="oys")

    G.barrier()
    G.add("sp", lambda e: None)
    print('total ops', G.n, {e: len(v) for e, v in G.ops.items()})
    G.emit(nc, stack, getattr(cfg, 'limit', None))
    stack.close()
    return nc


OWN_CHUNKS = {0: (0, 3, 4, 7), 1: (1, 2, 5, 6)}


def _chunked_w(w, ncols_chunk=128):
    K, N = w.shape
    a = w.reshape(K // 128, 128, N // 128, 128)
    a = a.transpose(2, 1, 0, 3)
    return np.ascontiguousarray(a).reshape(N // 128 * 128, K // 128 * 128)


def _rope_tab(pos):
    half = 16
    inv = (10000.0 ** (-np.arange(half, dtype=np.float32) / half)).astype(np.float32)
    ang = pos.astype(np.float32)[None, :] * inv[:, None]
    cos = np.cos(ang).astype(np.float32)
    sin = np.sin(ang).astype(np.float32)
    tab = np.zeros((128, len(pos)), np.float32)
    tab[64:80] = cos
    tab[80:96] = cos
    tab[96:112] = -sin
    tab[112:128] = sin
    return tab


def prep_inputs(cfg, x_prompt, x_sample, cache_ckv, cache_krope, state_conv, page_table, meta_tokens,
                pre_norm_g, post_norm_g, w_in_conv, conv_w, w_out_conv, kv_norm_g, w_dkv,
                kv_lat_norm_g, w_uk, w_uv, w_in_mla, q_norm_g, w_uq, w_out_mla):
    f = np.float32
    shared = {}
    shared["meta"] = np.ascontiguousarray(meta_tokens, f)
    shared["ident"] = np.eye(128, dtype=f)
    shared["wic"] = np.concatenate([_chunked_w(np.asarray(w_in_conv[l])) for l in range(2)], 0)
    shared["woc"] = np.concatenate([_chunked_w(np.asarray(w_out_conv[l])) for l in range(2)], 0)
    wd = np.asarray(w_dkv)
    wkr = wd[:, 256:288]
    wd2 = np.concatenate([np.zeros((1024, 64), f), wkr, wkr[:, 16:32], wkr[:, 0:16]], 1)
    shared["wdkv"] = np.concatenate([_chunked_w(wd[:, 0:128]), _chunked_w(wd[:, 128:256]), _chunked_w(wd2)], 0)
    shared["wim"] = np.concatenate([_chunked_w(np.asarray(w_in_mla[j])) for j in range(2)], 0)
    uq = []
    for j in range(2):
        w = np.asarray(w_uq[j]).reshape(512, 16, 96)
        wh = np.concatenate([w[:, :, 0:64], w[:, :, 64:96], w[:, :, 80:96], w[:, :, 64:80]], 2)
        for h in range(16):
            uq.append(_chunked_w(np.ascontiguousarray(wh[:, h, :])))
    shared["wuq"] = np.concatenate(uq, 0)
    shared["wom"] = np.concatenate([_chunked_w(np.asarray(w_out_mla[j])) for j in range(2)], 0)
    shared["wuk"] = np.ascontiguousarray(np.asarray(w_uk).reshape(2, 128, 1024).transpose(1, 0, 2)).reshape(128, 2048)
    shared["wuv"] = np.ascontiguousarray(np.asarray(w_uv).reshape(2, 128, 1024).transpose(1, 0, 2)).reshape(128, 2048)
    shared["wukT"] = np.ascontiguousarray(np.asarray(w_uk).transpose(2, 1, 0)).reshape(64, 16 * 256)
    kpos = np.concatenate([np.arange(16, T), np.arange(0, 16)])
    shared["ktab"] = _rope_tab(kpos)
    npg = cfg.n_pages
    past = npg * PAGE
    spos = np.tile(past + np.arange(LS), NB)
    shared["mtab"] = np.concatenate([_rope_tab(np.arange(16)), _rope_tab(spos)], 1)
    shared["cckv"] = np.asarray(cache_ckv).reshape(cfg.n_pool * 16, 2048)
    shared["ckr"] = np.asarray(cache_krope).reshape(cfg.n_pool * 16, 256)
    sm = np.full((64, NB, NB, LS), NEG, f)
    for b in range(NB):
        for t in range(LS):
            sm[np.arange(16) * 4 + t, b, b, 0:t + 1] = 0.0
    shared["smask"] = sm.reshape(64, NB * 64)

    def fm(v):
        return np.asarray(v, f).reshape(-1, 128).T

    vec = np.zeros((128, NV), f)
    for l in range(4):
        vec[:, V_PRE + l * 8:V_PRE + l * 8 + 8] = fm(pre_norm_g[l])
        vec[:, V_POST + l * 8:V_POST + l * 8 + 8] = fm(post_norm_g[l])
    vec[:, V_KVN:V_KVN + 8] = fm(kv_norm_g)
    vec[:, V_LAT:V_LAT + 2] = fm(kv_lat_norm_g)
    for j in range(2):
        vec[:, V_QN + j * 4:V_QN + j * 4 + 4] = fm(q_norm_g[j])
    vec[:, V_R16] = np.arange(128) % 16
    vec[:, V_EPS] = EPS
    for l in range(2):
        for k in range(3):
            vec[:, V_CW + l * 24 + k * 8:V_CW + l * 24 + k * 8 + 8] = fm(conv_w[l, k])

    ki = np.arange(128)[:, None]
    qi = np.arange(512)[None, :]
    pat = np.zeros((2, 128, 8, 512), f)
    for d in range(4):
        tri = ((d * 128 + ki) <= qi).astype(f)
        pat[0, :, d, :] = tri
        pat[1, :, d, :] = 1.0
        pat[1, :, 4 + d, :] = tri

    in_maps = []
    xp_all = np.asarray(x_prompt)
    xs_all = np.asarray(x_sample)
    sc_all = np.asarray(state_conv)
    pt_all = np.asarray(page_table)
    for c in range(NCORES):
        b, r = c // 2, c % 2
        m = dict(shared)
        m["xp"] = np.ascontiguousarray(xp_all[b])
        m["xs"] = np.ascontiguousarray(xs_all[c * NB:(c + 1) * NB].reshape(NS, D))
        m["sconv"] = np.ascontiguousarray(sc_all[:, c * NB:(c + 1) * NB].reshape(64, D))
        v = vec.copy()
        own = OWN_CHUNKS[r]
        for s in range(4):
            v[:, V_BLEND + 2 * s] = 1.0 if own[s] == 2 * s else 0.0
            v[:, V_BLEND + 2 * s + 1] = 1.0 if own[s] == 2 * s + 1 else 0.0
        m["vecs"] = v
        qpos = np.concatenate([16 + ch * 512 + np.arange(512) for ch in own])
        m["qtab"] = _rope_tab(qpos)
        mk = np.stack([pat[0 if own[par] == 2 * par else 1] for par in range(2)], 0)
        m["masks"] = np.ascontiguousarray(mk).reshape(2 * 128, 4096)
        pt = pt_all[c * NB:(c + 1) * NB]
        e = pt.reshape(NB, npg // 8, 8)
        e = np.repeat(e[:, :, :, None], 16, 3)
        m["ptx"] = np.ascontiguousarray(e.transpose(2, 3, 0, 1)).reshape(128, NB * (npg // 8)).astype(np.int32)
        in_maps.append(m)
    return in_maps


def assemble(res, cfg):
    y_prompt = np.zeros((4, SEQ, D), np.float32)
    y_sample = np.zeros((128, LS, D), np.float32)
    ckv_p = np.zeros((4, T, 256), np.float32)
    kr_p = np.zeros((4, T, 32), np.float32)
    conv_p = np.zeros((2, 4, 2, D), np.float32)
    ckv_s = np.zeros((128, LS, 256), np.float32)
    kr_s = np.zeros((128, LS, 32), np.float32)
    conv_s = np.zeros((2, 128, 2, D), np.float32)
    for c in range(NCORES):
        b, r = c // 2, c % 2
        o = res[c]
        for s, ch in enumerate(OWN_CHUNKS[r]):
            y_prompt[b, ch * 512:(ch + 1) * 512] = o["y_own"][s * 512:(s + 1) * 512]
        y_sample[c * NB:(c + 1) * NB] = o["ys"].reshape(NB, LS, D)
        if r == 0:
            ckv_p[b] = o["ckv_p"]
            kr_p[b] = o["kr_p"]
            conv_p[:, b] = o["conv_p"].reshape(2, 2, D)
        ckv_s[c * NB:(c + 1) * NB] = o["ckv_s"].reshape(NB, LS, 256)
        kr_s[c * NB:(c + 1) * NB] = o["kr_s"].reshape(NB, LS, 32)
        conv_s[:, c * NB:(c + 1) * NB] = o["conv_s"].reshape(2, NB, 2, D)
    return (y_prompt, y_sample, ckv_p, kr_p, conv_p, ckv_s, kr_s, conv_s)


_NC_CACHE = {}


def kernel(**inputs):
    cfg = Cfg(n_pool=inputs["cache_ckv"].shape[0], n_pages=inputs["page_table"].shape[1])
    key = (cfg.n_pool, cfg.n_pages)
    if key not in _NC_CACHE:
        _NC_CACHE[key] = build(cfg)
    nc = _NC_CACHE[key]
    in_maps = prep_inputs(cfg, **inputs)
    res = run_bass_kernel_spmd(nc, in_maps, core_ids=list(range(NCORES)))
    return assemble(res.results, cfg)
```

```python
import contextlib
import numpy as np
import concourse.bass as bass
import concourse.mybir as mybir
from concourse.bass_utils import run_bass_kernel_spmd

F32 = mybir.dt.float32
BF16 = mybir.dt.bfloat16
I32 = mybir.dt.int32
ALU = mybir.AluOpType
AF = mybir.ActivationFunctionType
AX = mybir.AxisListType

D = 1024
KC = 8
SEQ = 4096
NMETA = 16
T = SEQ + NMETA
NCORES = 8
NB = 16
LS = 4
NS = NB * LS
EPS = 1e-6
SCALE = 96 ** -0.5
PAGE = 128
NEG = -30000.0

ENGS = ("pe", "act", "dve", "pool", "sp")
import os
XENG = os.environ.get("XENG", "act,dve").split(",")
PSUM_KEYS = ("ps", "pt", "ssq", "sS")


class Op:
    __slots__ = ("eng", "fn", "deps", "signal", "semval", "is_dma", "slot", "idx")

    def __init__(self, eng, fn):
        self.eng = eng
        self.fn = fn
        self.deps = ()
        self.signal = False
        self.semval = 0
        self.is_dma = False
        self.slot = None


class Graph:
    def __init__(self):
        self.ops = {e: [] for e in ENGS}
        self.last_w = {}
        self.readers = {}
        self.slot_count = {}
        self.n = 0
        self.barrier_ops = []
        self.pending_dma = []
        self.last_on = {}

    def add(self, eng, fn, reads=(), writes=(), slot=None):
        op = Op(eng, fn)
        op.idx = self.n
        self.n += 1
        deps = {}

        def dep(o):
            if o is not None and o is not op:
                deps[id(o)] = o

        for k in reads:
            dep(self.last_w.get(k))
            if isinstance(k, tuple) and k[0] in PSUM_KEYS:
                r = self.readers.get(k)
                if r:
                    for ek, o in r.items():
                        if ek != eng:
                            dep(o)
        for k in writes:
            dep(self.last_w.get(k))
            r = self.readers.get(k)
            if r:
                for o in r.values():
                    dep(o)
        for o in self.barrier_ops:
            dep(o)
        op.deps = list(deps.values())
        for o in op.deps:
            o.signal = True
        for k in reads:
            r = self.readers.setdefault(k, {})
            if slot is not None:
                r[("dma", op.idx)] = op
            else:
                r[eng] = op
        for k in writes:
            self.last_w[k] = op
            self.readers[k] = {}
        if slot is not None:
            op.is_dma = True
            op.slot = slot
            c = self.slot_count.get(slot, 0) + 1
            self.slot_count[slot] = c
            op.semval = 16 * c
            self.pending_dma.append(op)
        else:
            self.last_on[eng] = op
        self.ops[eng].append(op)
        return op

    def barrier(self):
        ops = list(self.last_on.values()) + self.pending_dma
        self.barrier_ops = ops
        self.pending_dma = []

    def emit(self, nc, stack, limit=None):
        if limit is not None:
            for e in ENGS:
                self.ops[e] = [o for o in self.ops[e] if o.idx < limit]
        esem = {e: stack.enter_context(nc.semaphore("s_" + e)) for e in ENGS if e != "sp"}
        ssem = {s: stack.enter_context(nc.semaphore("d_%d" % i)) for i, s in enumerate(self.slot_count)}
        for e in ENGS:
            c = 0
            for op in self.ops[e]:
                if not op.is_dma and op.signal:
                    c += 1
                    op.semval = c

        def sem_of(o):
            return ssem[o.slot] if o.is_dma else esem[o.eng]

        def run(ename, eng):
            known = {}
            for op in self.ops[ename]:
                need = {}
                for d in op.deps:
                    if ename == "pe" and d.eng == "pe" and not d.is_dma:
                        continue
                    s = sem_of(d)
                    v = d.semval
                    key = id(s)
                    if known.get(key, 0) >= v:
                        continue
                    if key not in need or need[key][1] < v:
                        need[key] = (s, v)
                for key, (s, v) in need.items():
                    eng.wait_ge(s, v)
                    known[key] = v
                inst = op.fn(eng)
                if inst is None:
                    continue
                if op.is_dma:
                    inst.then_inc(ssem[op.slot], 16)
                elif op.signal:
                    inst.then_inc(esem[ename], 1)

        block = stack.enter_context(nc.Block())

        @block.sync
        def _(e):
            run("sp", e)

        @block.scalar
        def _(e):
            run("act", e)

        @block.vector
        def _(e):
            run("dve", e)

        @block.gpsimd
        def _(e):
            run("pool", e)

        @block.tensor
        def _(e):
            run("pe", e)


class Arena:
    def __init__(self, big, nwords):
        self.big = big
        self.n = nwords
        self.top = 0

    def mark(self):
        return self.top

    def release(self, m):
        self.top = m

    def alloc(self, nelem, dtype=F32):
        words = (nelem * (2 if dtype == BF16 else 4) + 3) // 4
        words = (words + 7) // 8 * 8
        off = self.top
        self.top += words
        assert self.top <= self.n, "SBUF arena overflow: %d > %d words" % (self.top, self.n)
        ap = self.big[:, off:off + words]
        if dtype != F32:
            ap = ap.bitcast(dtype)
        return ap[:, 0:nelem]


class Ring:
    def __init__(self, name, aps):
        self.name = name
        self.aps = aps
        self.i = 0

    def next(self):
        i = self.i % len(self.aps)
        self.i += 1
        return (self.name, i), self.aps[i]


def chunks(n, c=512):
    return [(i, min(c, n - i)) for i in range(0, n, c)]


class Cfg:
    def __init__(self, n_pool=10240, n_pages=64, do_b=True, do_s=True):
        self.n_pool = n_pool
        self.n_pages = n_pages
        self.do_b = do_b
        self.do_s = do_s


V_PRE = 0
V_POST = 32
V_KVN = 64
V_LAT = 72
V_QN = 74
V_CW = 82
V_BLEND = 130
V_R16 = 138
V_EPS = 139
NV = 140


def build(cfg):
    nc = bass.Bass("TRN2", target_bir_lowering=False)
    G = Graph()
    stack = contextlib.ExitStack()

    regcache = {}

    def I(eng, method, reads, writes, *args, slot=None, **kw):
        def fn(e):
            try:
                if "bounds_check" in kw and not isinstance(kw["bounds_check"], (type(None),)) and isinstance(kw["bounds_check"], int):
                    if "bcreg" not in regcache:
                        regcache["bcreg"] = e.to_reg(kw["bounds_check"])
                    kw2 = dict(kw)
                    kw2["bounds_check"] = regcache["bcreg"]
                    return getattr(e, method)(*args, **kw2)
                return getattr(e, method)(*args, **kw)
            except Exception:
                print("FAILED OP", eng, method, writes, [str(a)[:200] for a in args], {k: str(v)[:200] for k, v in kw.items()})
                raise
        return G.add(eng, fn, reads=reads, writes=writes, slot=slot)

    def din(name, shape, dt=F32):
        return nc.dram_tensor(name, list(shape), dt, kind="ExternalInput").ap()

    def dout(name, shape, dt=F32):
        return nc.dram_tensor(name, list(shape), dt, kind="ExternalOutput").ap()

    NPG = cfg.n_pages
    NGRP = NPG // 8
    xp = din("xp", [SEQ, D])
    meta = din("meta", [NMETA, D])
    xs = din("xs", [NS, D])
    sconv = din("sconv", [64, D])
    vecs = din("vecs", [128, NV])
    ident_d = din("ident", [128, 128])
    wic = din("wic", [2 * 32 * 128, 1024])
    woc = din("woc", [2 * 8 * 128, 1024])
    wdkv = din("wdkv", [3 * 128, 1024])
    wim = din("wim", [2 * 12 * 128, 1024])
    wuq = din("wuq", [2 * 16 * 128, 512])
    wom = din("wom", [2 * 8 * 128, 1024])
    wuk = din("wuk", [128, 2048])
    wuv = din("wuv", [128, 2048])
    wukT = din("wukT", [64, 16 * 256])
    ktab = din("ktab", [128, T])
    mtab = din("mtab", [128, 80])
    qtab = din("qtab", [128, 2048])
    masks = din("masks", [2 * 128, 4096])
    ptx = din("ptx", [128, NB * NGRP], I32)
    smask = din("smask", [64, NB * 64])
    cckv = din("cckv", [cfg.n_pool * 16, 2048])
    ckr = din("ckr", [cfg.n_pool * 16, 256])

    y_own = dout("y_own", [2048, D])
    ys = dout("ys", [NS, D])
    ckv_p = dout("ckv_p", [T, 256])
    kr_p = dout("kr_p", [T, 32])
    conv_p = dout("conv_p", [4, D])
    ckv_s = dout("ckv_s", [NS, 256])
    kr_s = dout("kr_s", [NS, 32])
    conv_s = dout("conv_s", [64, D])

    NW = 53000
    big = stack.enter_context(nc.sbuf_tensor("arena", [128, NW], F32))
    A = Arena(big, NW)
    banks = [stack.enter_context(nc.psum_tensor("bank%d" % i, [128, 512], F32)) for i in range(8)]

    def v3(ap, a):
        return ap.rearrange("p (a b) -> p a b", a=a)

    def bt(ap, t):
        return ap.rearrange("p (b t) -> p b t", t=t)

    ident = A.alloc(128)
    identb = A.alloc(128, BF16)
    onesb = A.alloc(128, BF16)
    neghalf = A.alloc(512)
    vec = A.alloc(NV)
    I("sp", "dma_start", [], ["ident"], out=ident, in_=ident_d, slot="c_ident")
    I("sp", "dma_start", [], ["vec"], out=vec, in_=vecs, slot="c_vec")
    I("pool", "dma_start", [], ["identb"], out=identb, in_=ident_d, slot="c_identb")
    I("dve", "memset", [], ["onesb"], onesb, 1.0)
    I("dve", "memset", [], ["neghalf"], neghalf, -0.5)

    def vcol(off):
        return vec[:, off:off + 1]


    xsT = v3(A.alloc(8 * NS), 8)
    cTs = v3(A.alloc(2 * NS, BF16), 2)
    krTs = A.alloc(NS, BF16)
    cns_tok = A.alloc(256, BF16)
    mSP = A.mark()
    x_own = v3(A.alloc(8 * 2048), 8)
    cT = v3(A.alloc(2 * T, BF16), 2)
    Kbuf = [A.alloc(T, BF16) for _ in range(2)]
    halo = A.alloc(2 * 8 * 2)
    cvp = A.alloc(8 * 4)
    cvs = A.alloc(8 * 64)
    stT = A.alloc(8 * 64)

    for i in range(2):
        I("pool", "memset", [], [("Kb", i, "kr")], Kbuf[i], 0.0)

    psring = Ring("ps", [banks[i] for i in range(4)])
    ssq_banks = [banks[4], banks[5]]
    trring = Ring("pt", [banks[6], banks[7]])

    def copy_any(eng, reads, writes, out, in_):
        if eng == "act":
            I("act", "activation", reads, writes, out=out, in_=in_, func=AF.Copy)
        else:
            I(eng, "tensor_copy", reads, writes, out=out, in_=in_)

    def transposes_to(dst_fn, src, src_key, nrows, ncols, idn, eng_alt):
        nblk = (ncols + 127) // 128
        per = max(1, 512 // nrows)
        for g0 in range(0, nblk, per):
            pk, pb = trring.next()
            nb = min(per, nblk - g0)
            for i in range(nb):
                cb = g0 + i
                cw = min(128, ncols - cb * 128)
                I("pe", "transpose", [src_key, "ident"], [pk],
                  pb[:cw, i * nrows:(i + 1) * nrows], src[:nrows, cb * 128:cb * 128 + cw], idn[:nrows, :nrows])
            for i in range(nb):
                cb = g0 + i
                cw = min(128, ncols - cb * 128)
                dst, dkey = dst_fn(cb)
                copy_any(eng_alt[cb % len(eng_alt)], [pk], [dkey], dst, pb[:cw, i * nrows:(i + 1) * nrows])

    mA = A.mark()
    wring = Ring("w", [v3(A.alloc(1024, BF16), 8) for _ in range(4)])
    sqr = Ring("sq", [A.alloc(512, BF16) for _ in range(2)])
    rsr = Ring("rs", [A.alloc(512) for _ in range(2)])

    def load_w(dram, row0):
        k, w = wring.next()
        I("pool", "dma_start", [], [k], out=w.rearrange("p a b -> p (a b)"), in_=dram[row0:row0 + 128, :], slot="w%d" % k[1])
        return k, w

    def rstd_from(sb, skey, n, nfeat):
        rk, r = rsr.next()
        I("act", "activation", [skey, "vec"], [rk], out=r[:, :n], in_=sb[:, :n], func=AF.Ln, scale=1.0 / nfeat, bias=vcol(V_EPS))
        I("act", "activation", [rk], [rk], out=r[:, :n], in_=r[:, :n], func=AF.Exp, scale=-0.5)
        return rk, r

    def rms_stats(srcT, key_fn, nchunks, tcs, nfeat):
        res = []
        for ti, (c0, n) in enumerate(tcs):
            sb = ssq_banks[ti % 2]
            sk = ("ssq", ti % 2)
            for kc in range(nchunks):
                qk, q = sqr.next()
                I("act", "activation", [key_fn(kc)], [qk], out=q[:, :n], in_=srcT[:, kc, c0:c0 + n], func=AF.Square)
                I("pe", "matmul", [qk, "onesb"], [sk], sb[:, :n], onesb, q[:, :n], start=(kc == 0), stop=(kc == nchunks - 1))
            res.append(rstd_from(sb, sk, n, nfeat))
        return res

    def norm_to(dstT, dkey_fn, srcT, skey_fn, nchunks, tcs, rst, goff):
        for ti, (c0, n) in enumerate(tcs):
            rk, r = rst[ti]
            for kc in range(nchunks):
                I("dve", "scalar_tensor_tensor", [skey_fn(kc), rk, "vec"], [dkey_fn(kc)],
                  out=dstT[:, kc, c0:c0 + n], in0=srcT[:, kc, c0:c0 + n], scalar=vcol(goff + kc), in1=r[:, :n], op0=ALU.mult, op1=ALU.mult)

    def proj(wk, w, nk, srcT, skey_fn, c0, n, ring=None):
        pk, pb = (ring or psring).next()
        for kc in range(nk):
            I("pe", "matmul", [wk, skey_fn(kc)], [pk], pb[:, :n], w[:, kc, :], srcT[:, kc, c0:c0 + n], start=(kc == 0), stop=(kc == nk - 1))
        return pk, pb

    def out_proj_residual(wdram, wrow0, srcT, skey_fn, xT, xkey_fn, tcs, NT, goff):
        mT = v3(A.alloc(8 * NT), 8)
        assert len(tcs) <= 2
        for of in range(8):
            wk, w = load_w(wdram, wrow0 + of * 128)
            for ti, (c0, n) in enumerate(tcs):
                pk, pb = proj(wk, w, 8, srcT, skey_fn, c0, n)
                I("act", "activation", [pk], [("mT", of, ti)], out=mT[:, of, c0:c0 + n], in_=pb[:, :n], func=AF.Copy)
                qk, q = sqr.next()
                I("act", "activation", [pk], [qk], out=q[:, :n], in_=pb[:, :n], func=AF.Square)
                I("pe", "matmul", [qk, "onesb"], [("ssq", ti % 2)], ssq_banks[ti % 2][:, :n], onesb, q[:, :n], start=(of == 0), stop=(of == 7))
        for ti, (c0, n) in enumerate(tcs):
            rk, r = rstd_from(ssq_banks[ti % 2], ("ssq", ti % 2), n, D)
            for of in range(8):
                I("dve", "scalar_tensor_tensor", [("mT", of, ti), rk, "vec"], [("mT", of, ti)],
                  out=mT[:, of, c0:c0 + n], in0=mT[:, of, c0:c0 + n], scalar=vcol(goff + of), in1=r[:, :n], op0=ALU.mult, op1=ALU.mult)
                I("dve", "tensor_tensor", [("mT", of, ti), xkey_fn(of)], [xkey_fn(of)],
                  out=xT[:, of, c0:c0 + n], in0=xT[:, of, c0:c0 + n], in1=mT[:, of, c0:c0 + n], op=ALU.add)

    def run_group(gi, NT, segs, x_loads, tabs, dests, own_slot):
        tcs = chunks(NT)
        mG = A.mark()
        xT = v3(A.alloc(8 * NT), 8)
        gT = v3(A.alloc(8 * NT, BF16), 8)
        xkey = lambda kc: ("xT", kc)
        mS = A.mark()
        xr = Ring("xin", [A.alloc(1024) for _ in range(2)])
        for (src, ntok, col0) in x_loads:
            xk, xb = xr.next()
            I("sp", "dma_start", [], [xk], out=xb[:ntok, :], in_=src, slot="xin%d" % xk[1])
            transposes_to(lambda cb, col0=col0, ntok=ntok: (xT[:, cb, col0:col0 + ntok], ("xT", cb)), xb, xk, ntok, 1024, ident, XENG)
        G.barrier()
        A.release(mS)
        for l in range(2):
            mL = A.mark()
            hT = v3(A.alloc(8 * NT, BF16), 8)
            hkey = lambda kc: ("hT", kc)
            cbuf = A.alloc(NT)
            ybuf = A.alloc(NT)
            tbuf = A.alloc(NT)
            szb = A.alloc(NT)
            vexts = [A.alloc(sg["nseq"] * (sg["L"] + 2)) for sg in segs]
            rst = rms_stats(xT, xkey, 8, tcs, D)
            norm_to(hT, hkey, xT, xkey, 8, tcs, rst, V_PRE + l * 8)
            for j in range(8):
                hoff = (l * 8 + j) * 2
                cwb = V_CW + l * 24 + j
                for si, sg in enumerate(segs):
                    ve = vexts[si]
                    L = sg["L"]
                    if sg["kind"] == "meta":
                        I("dve", "memset", [], [("vext", si)], ve[:, 0:2], 0.0)
                    elif sg["kind"] == "prompt":
                        I("dve", "tensor_copy", [("halo", l, j)], [("vext", si)], out=ve[:, 0:2], in_=halo[:, hoff:hoff + 2])
                    else:
                        I("dve", "tensor_copy", ["stT"], [("vext", si)], out=bt(ve, L + 2)[:, :, 0:2],
                          in_=stT[:, j * 64 + l * 32:j * 64 + l * 32 + 32].rearrange("p (b k) -> p b k", k=2))
                for part, pname in ((1, "c"), (2, "u"), (0, "b"), (3, "z")):
                    if pname == "b":
                        for si, sg in enumerate(segs):
                            ve = vexts[si]
                            L = sg["L"]
                            s0 = sg["col0"]
                            if sg["nseq"] == 1:
                                src = [ve[:, k:k + L] for k in range(3)]
                                yv = ybuf[:, s0:s0 + L]
                                tail = ve[:, L:L + 2]
                            else:
                                ve3 = bt(ve, L + 2)
                                src = [ve3[:, :, k:k + L] for k in range(3)]
                                yv = bt(ybuf[:, s0:s0 + sg["nseq"] * L], L)
                                tail = ve3[:, :, L:L + 2]
                            vk, yk = ("vext", si), ("ybuf", si)
                            I("dve", "tensor_scalar", [vk, "vec"], [yk], out=yv, in0=src[0], scalar1=vcol(cwb), scalar2=None, op0=ALU.mult)
                            I("dve", "scalar_tensor_tensor", [vk, "vec", yk], [yk], out=yv, in0=src[1], scalar=vcol(cwb + 8), in1=yv, op0=ALU.mult, op1=ALU.add)
                            I("dve", "scalar_tensor_tensor", [vk, "vec", yk], [yk], out=yv, in0=src[2], scalar=vcol(cwb + 16), in1=yv, op0=ALU.mult, op1=ALU.add)
                            if sg["kind"] in ("meta", "prompt"):
                                I("act", "activation", [vk], [("halo", l, j)], out=halo[:, hoff:hoff + 2], in_=tail, func=AF.Copy)
                                if sg.get("last"):
                                    I("act", "activation", [vk], ["cvp"], out=cvp[:, j * 4 + l * 2:j * 4 + l * 2 + 2], in_=tail, func=AF.Copy)
                            else:
                                I("act", "activation", [vk], ["cvs"], out=cvs[:, j * 64 + l * 32:j * 64 + l * 32 + 32].rearrange("p (b k) -> p b k", k=2),
                                  in_=tail, func=AF.Copy)
                    wk, w = load_w(wic, (l * 32 + part * 8 + j) * 128)
                    for ti, (c0, n) in enumerate(tcs):
                        pk, pb = proj(wk, w, 8, hT, hkey, c0, n)
                        if pname == "c":
                            I("act", "activation", [pk], [("cbuf", ti)], out=cbuf[:, c0:c0 + n], in_=pb[:, :n], func=AF.Copy)
                        elif pname == "u":
                            for si, sg in enumerate(segs):
                                L = sg["L"]
                                s0 = sg["col0"]
                                lo = max(c0, s0)
                                hi = min(c0 + n, s0 + sg["nseq"] * L)
                                if hi <= lo:
                                    continue
                                ve = vexts[si]
                                if sg["nseq"] == 1:
                                    o = ve[:, 2 + lo - s0:2 + hi - s0]
                                    a = pb[:, lo - c0:hi - c0]
                                    b_ = cbuf[:, lo:hi]
                                else:
                                    assert lo == s0 and hi == s0 + sg["nseq"] * L
                                    o = bt(ve, L + 2)[:, :, 2:2 + L]
                                    a = bt(pb[:, lo - c0:hi - c0], L)
                                    b_ = bt(cbuf[:, lo:hi], L)
                                I("dve", "tensor_tensor", [pk, ("cbuf", ti)], [("vext", si)], out=o, in0=a, in1=b_, op=ALU.mult)
                        elif pname == "b":
                            I("dve", "tensor_tensor", [pk] + [("ybuf", si) for si in range(len(segs))], [("tbuf", ti)],
                              out=tbuf[:, c0:c0 + n], in0=pb[:, :n], in1=ybuf[:, c0:c0 + n], op=ALU.mult)
                        else:
                            I("act", "activation", [pk], [("szb", ti)], out=szb[:, c0:c0 + n], in_=pb[:, :n], func=AF.Silu)
                            I("dve", "tensor_tensor", [("tbuf", ti), ("szb", ti)], [("gT", j)],
                              out=gT[:, j, c0:c0 + n], in0=tbuf[:, c0:c0 + n], in1=szb[:, c0:c0 + n], op=ALU.mult)
            G.barrier()
            A.release(mL)
            out_proj_residual(woc, l * 8 * 128, gT, lambda kc: ("gT", kc), xT, xkey, tcs, NT, V_POST + l * 8)
            G.barrier()
            A.release(mL)
        mL = A.mark()
        hT = gT
        hkey = lambda kc: ("gT", kc)
        craw = v3(A.alloc(2 * NT), 2)
        tabb = A.alloc(NT)
        krf = A.alloc(NT)
        krt = A.alloc(NT)
        ctr = Ring("ctok", [A.alloc(256) for _ in range(2)])
        ktr = Ring("ktok", [A.alloc(32) for _ in range(2)])
        for (tsrc, c0, n) in tabs:
            I("sp", "dma_start", [], ["tabb"], out=tabb[:, c0:c0 + n], in_=tsrc, slot="tabb")
        rst = rms_stats(xT, xkey, 8, tcs, D)
        norm_to(hT, hkey, xT, xkey, 8, tcs, rst, V_KVN)
        for ocl in range(3):
            wk, w = load_w(wdkv, ocl * 128)
            for ti, (c0, n) in enumerate(tcs):
                pk, pb = proj(wk, w, 8, hT, hkey, c0, n)
                if ocl < 2:
                    I("act", "activation", [pk], [("craw", ocl, ti)], out=craw[:, ocl, c0:c0 + n], in_=pb[:, :n], func=AF.Copy)
                    qk, q = sqr.next()
                    I("act", "activation", [pk], [qk], out=q[:, :n], in_=pb[:, :n], func=AF.Square)
                    I("pe", "matmul", [qk, "onesb"], [("ssq", ti % 2)], ssq_banks[ti % 2][:, :n], onesb, q[:, :n], start=(ocl == 0), stop=(ocl == 1))
                else:
                    I("dve", "tensor_tensor", [pk, "tabb"], [("krf", ti)], out=krf[64:96, c0:c0 + n], in0=pb[64:96, :n], in1=tabb[64:96, c0:c0 + n], op=ALU.mult)
                    I("dve", "tensor_tensor", [pk, "tabb"], [("krt", ti)], out=krt[64:96, c0:c0 + n], in0=pb[96:128, :n], in1=tabb[96:128, c0:c0 + n], op=ALU.mult)
                    I("dve", "tensor_tensor", [("krf", ti), ("krt", ti)], [("krf", ti)], out=krf[64:96, c0:c0 + n], in0=krf[64:96, c0:c0 + n], in1=krt[64:96, c0:c0 + n], op=ALU.add)
        for ti, (c0, n) in enumerate(tcs):
            rk, r = rstd_from(ssq_banks[ti % 2], ("ssq", ti % 2), n, 256)
            for ocl in range(2):
                I("dve", "scalar_tensor_tensor", [("craw", ocl, ti), rk, "vec"], [("craw", ocl, ti)],
                  out=craw[:, ocl, c0:c0 + n], in0=craw[:, ocl, c0:c0 + n], scalar=vcol(V_LAT + ocl), in1=r[:, :n], op0=ALU.mult, op1=ALU.mult)
        for dd in dests:
            c0, n = dd["c0"], dd["n"]
            tis = sorted(set(ti for ti, (a, m) in enumerate(tcs) if a < c0 + n and a + m > c0))
            rkeys = [("craw", ocl, ti) for ocl in range(2) for ti in tis]
            kkeys = [("krf", ti) for ti in tis]
            for ocl in range(2):
                I("act", "activation", rkeys, [dd["cT_key"]], out=dd["cT"][:, ocl, :], in_=craw[:, ocl, c0:c0 + n], func=AF.Copy)
            for (kap, kkey) in dd["krT"]:
                I("act", "activation", kkeys, [kkey], out=kap, in_=krf[64:96, c0:c0 + n], func=AF.Copy)
            for t0 in range(0, n, 128):
                tw = min(128, n - t0)
                ck, cb_ = ctr.next()
                kk, kb_ = ktr.next()
                pk, pb = trring.next()
                for ocl in range(2):
                    I("pe", "transpose", rkeys + ["ident"], [pk], pb[:tw, ocl * 128:(ocl + 1) * 128], craw[:, ocl, c0 + t0:c0 + t0 + tw], ident)
                I("pe", "transpose", kkeys + ["ident"], [pk], pb[:tw, 256:288], krf[64:96, c0 + t0:c0 + t0 + tw], ident[64:96, 64:96])
                I("act", "activation", [pk], [ck], out=cb_[:tw, :], in_=pb[:tw, 0:256], func=AF.Copy)
                I("dve", "tensor_copy", [pk], [kk], out=kb_[:tw, :], in_=pb[:tw, 256:288])
                r0 = dd["row0"] + t0
                I("sp", "dma_start", [ck], [("out", dd["name"], "c", r0)], out=dd["ckv_out"][r0:r0 + tw, :], in_=cb_[:tw, :], slot="octok%d" % ck[1])
                I("sp", "dma_start", [kk], [("out", dd["name"], "k", r0)], out=dd["kr_out"][r0:r0 + tw, :], in_=kb_[:tw, :], slot="oktok%d" % kk[1])
                if dd.get("tok_bf") is not None:
                    I("dve", "tensor_copy", [pk], ["cns_tok"], out=dd["tok_bf"][:tw, :], in_=pb[:tw, 0:256])
        if own_slot is not None:
            s = own_slot
            for kc in range(8):
                xo = x_own[:, kc, s * 512:(s + 1) * 512]
                I("dve", "tensor_scalar", [("xT", kc), "vec"], [("x_own", s, kc)], out=xo, in0=xT[:, kc, 0:512], scalar1=vcol(V_BLEND + 2 * s), scalar2=None, op0=ALU.mult)
                I("dve", "scalar_tensor_tensor", [("xT", kc), "vec", ("x_own", s, kc)], [("x_own", s, kc)],
                  out=xo, in0=xT[:, kc, 512:1024], scalar=vcol(V_BLEND + 2 * s + 1), in1=xo, op0=ALU.mult, op1=ALU.add)
        else:
            for kc in range(8):
                I("act", "activation", [("xT", kc)], [("xs", kc)], out=xsT[:, kc, :], in_=xT[:, kc, 16:80], func=AF.Copy)
        G.barrier()
        A.release(mG)

    mS0 = A.mark()
    scb = A.alloc(1024)
    I("sp", "dma_start", [], ["scb"], out=scb[:64, :], in_=sconv, slot="scb")
    transposes_to(lambda cb: (stT[:, cb * 64:(cb + 1) * 64], "stT"), scb, "scb", 64, 1024, ident, ["act", "dve"])
    G.barrier()
    A.release(mS0)

    run_group(
        0, 80,
        [dict(col0=0, nseq=1, L=16, kind="meta"), dict(col0=16, nseq=NB, L=LS, kind="sample")],
        [(meta, 16, 0), (xs, 64, 16)],
        [(mtab, 0, 80)],
        [dict(c0=0, n=16, cT=cT[:, :, SEQ:SEQ + 16], cT_key=("cT", "m"), krT=[(Kbuf[i][64:96, SEQ:SEQ + 16], ("Kb", i, "kr")) for i in range(2)],
              ckv_out=ckv_p, kr_out=kr_p, row0=0, name="p"),
         dict(c0=16, n=64, cT=cTs, cT_key="cTs", krT=[(krTs[0:32, :], "krTs")],
              ckv_out=ckv_s, kr_out=kr_s, row0=0, name="s", tok_bf=cns_tok)],
        None)
    for g in range(4):
        run_group(
            1 + g, 1024,
            [dict(col0=0, nseq=1, L=1024, kind="prompt", last=(g == 3))],
            [(xp[g * 1024 + i * 128:g * 1024 + (i + 1) * 128, :], 128, i * 128) for i in range(8)],
            [(ktab[:, g * 1024:(g + 1) * 1024], 0, 1024)],
            [dict(c0=0, n=1024, cT=cT[:, :, g * 1024:(g + 1) * 1024], cT_key=("cT", g), krT=[(Kbuf[i][64:96, g * 1024:(g + 1) * 1024], ("Kb", i, "kr")) for i in range(2)],
                  ckv_out=ckv_p, kr_out=kr_p, row0=16 + g * 1024, name="p")],
            g)

    mO = A.mark()
    cvo = A.alloc(1024)
    cso = A.alloc(1024)
    for j in range(8):
        pk, pb = trring.next()
        I("pe", "transpose", ["cvp", "ident"], [pk], pb[:4, 0:128], cvp[:, j * 4:(j + 1) * 4], ident)
        I("pe", "transpose", ["cvs", "ident"], [pk], pb[:64, 128:256], cvs[:, j * 64:(j + 1) * 64], ident)
        I("dve", "tensor_copy", [pk], ["cvo"], out=cvo[:4, j * 128:(j + 1) * 128], in_=pb[:4, 0:128])
        I("act", "activation", [pk], ["cso"], out=cso[:64, j * 128:(j + 1) * 128], in_=pb[:64, 128:256], func=AF.Copy)
    I("sp", "dma_start", ["cvo"], [("out", "conv_p")], out=conv_p, in_=cvo[:4, :], slot="o_cvo")
    I("sp", "dma_start", ["cso"], [("out", "conv_s")], out=conv_s, in_=cso[:64, :], slot="o_cso")
    G.barrier()
    A.release(mO)
    A.release(mA)


    mB = A.mark()
    wring = Ring("w", [v3(A.alloc(1024, BF16), 8) for _ in range(4)])
    sqr = Ring("sq", [A.alloc(512, BF16) for _ in range(2)])
    rsr = Ring("rs", [A.alloc(512) for _ in range(2)])
    Vh = v3(A.alloc(33 * 128, BF16), 33)
    maskb = [A.alloc(4096, BF16) for _ in range(2)]
    kmax2 = A.alloc(16)
    half = A.alloc(1)
    wqr = Ring("wq", [v3(A.alloc(512, BF16), 4) for _ in range(2)])
    wkr = Ring("wkh", [v3(A.alloc(128, BF16), 2) for _ in range(2)])
    wvr = Ring("wvh", [v3(A.alloc(128, BF16), 2) for _ in range(2)])
    poring = Ring("ssq", [banks[4], banks[5]])
    I("dve", "memset", [], ["Vh"], Vh.rearrange("p a b -> p (a b)"), 1.0)
    I("dve", "memset", [], ["half"], half, 0.5)
    for par in range(2):
        I("pool", "dma_start", [], ["maskb"], out=maskb[par], in_=masks[par * 128:(par + 1) * 128, :], slot="maskb%d" % par)
    wuk3 = wuk.rearrange("p (kc f) -> p kc f", kc=2)
    wuv3 = wuv.rearrange("p (kc f) -> p kc f", kc=2)
    print("arena words used before phase-B halves:", A.top, "of", A.n)

    def mla_prompt(j, hf):
        NT = 1024
        tcs = chunks(NT)
        xT = x_own[:, :, hf * 1024:(hf + 1) * 1024]
        xkey = lambda kc: ("xo", hf, kc)
        first = (j == 0 and hf == 0)
        mH = A.mark()
        ogT = v3(A.alloc(8 * NT, BF16), 8)
        ogkey = lambda kc: ("og", kc)
        qln = v3(A.alloc(4 * NT, BF16), 4)
        qlnkey = lambda kc: ("qln", kc)
        mX = A.mark()
        hT = v3(A.alloc(8 * NT, BF16), 8)
        hkey = lambda kc: ("hT", kc)
        qraw = v3(A.alloc(4 * NT), 4)
        rst = rms_stats(xT, xkey, 8, tcs, D)
        norm_to(hT, hkey, xT, xkey, 8, tcs, rst, V_PRE + (2 + j) * 8)
        for oc in range(12):
            wk, w = load_w(wim, (j * 12 + oc) * 128)
            for ti, (c0, n) in enumerate(tcs):
                pk, pb = proj(wk, w, 8, hT, hkey, c0, n)
                if oc < 4:
                    I("act", "activation", [pk], [("qraw", oc, ti)], out=qraw[:, oc, c0:c0 + n], in_=pb[:, :n], func=AF.Copy)
                    qk, q = sqr.next()
                    I("act", "activation", [pk], [qk], out=q[:, :n], in_=pb[:, :n], func=AF.Square)
                    I("pe", "matmul", [qk, "onesb"], [("ssq", ti % 2)], ssq_banks[ti % 2][:, :n], onesb, q[:, :n], start=(oc == 0), stop=(oc == 3))
                else:
                    I("act", "activation", [pk], [("og", oc - 4)], out=ogT[:, oc - 4, c0:c0 + n], in_=pb[:, :n], func=AF.Silu)
            if oc == 3:
                for ti, (c0, n) in enumerate(tcs):
                    rk, r = rstd_from(ssq_banks[ti % 2], ("ssq", ti % 2), n, 512)
                    for kc in range(4):
                        I("dve", "scalar_tensor_tensor", [("qraw", kc, ti), rk, "vec"], [("qln", kc)],
                          out=qln[:, kc, c0:c0 + n], in0=qraw[:, kc, c0:c0 + n], scalar=vcol(V_QN + j * 4 + kc), in1=r[:, :n], op0=ALU.mult, op1=ALU.mult)
        G.barrier()
        A.release(mX)
        Qr = Ring("Qh", [A.alloc(NT, BF16) for _ in range(2)])
        Ptr = Ring("Pt", [A.alloc(512, BF16) for _ in range(5)])
        qtabb = A.alloc(NT)
        rsum = Ring("rsum", [A.alloc(512) for _ in range(2)])
        tmpf = Ring("tmpf", [A.alloc(512) for _ in range(2)])
        t1r = Ring("t1", [A.alloc(512) for _ in range(2)])
        t2r = Ring("t2", [A.alloc(512) for _ in range(2)])
        negr = Ring("negm", [A.alloc(1) for _ in range(2)])
        qmx = A.alloc(4)
        kmx = A.alloc(16)
        I("sp", "dma_start", [], ["qtabb"], out=qtabb, in_=qtab[:, hf * 1024:(hf + 1) * 1024], slot="qtabb")
        for (qk_, qb_) in [Qr.next(), Qr.next()]:
            I("pool", "memset", [], [qk_], qb_, 0.0)
        for h in range(16):
            hb = (h % 2) * 64
            pr = h // 2
            Kk = ("Kb", h % 2)
            Kh = Kbuf[h % 2]
            wqk, wq = wqr.next()
            I("pool", "dma_start", [], [wqk], out=wq.rearrange("p a b -> p (a b)"), in_=wuq[(j * 16 + h) * 128:(j * 16 + h + 1) * 128, :], slot="wq%d" % wqk[1])
            wkk, wkh = wkr.next()
            I("pool", "dma_start", [], [wkk], out=wkh, in_=wuk3[:, :, h * 64:(h + 1) * 64], slot="wkh%d" % wkk[1])
            wvk, wvh = wvr.next()
            I("pool", "dma_start", [], [wvk], out=wvh, in_=wuv3[:, :, h * 64:(h + 1) * 64], slot="wvh%d" % wvk[1])
            Qk, Qh = Qr.next()
            for ti, (c0, n) in enumerate(tcs):
                pk, pb = proj(wqk, wq, 4, qln, qlnkey, c0, n, ring=trring)
                I("act", "activation", [pk], [Qk], out=Qh[0:64, c0:c0 + n], in_=pb[0:64, :n], func=AF.Copy)
                k1, t1 = t1r.next()
                k2, t2 = t2r.next()
                I("dve", "tensor_tensor", [pk, "qtabb"], [k1], out=t1[64:96, :n], in0=pb[64:96, :n], in1=qtabb[64:96, c0:c0 + n], op=ALU.mult)
                I("dve", "tensor_tensor", [pk, "qtabb"], [k2], out=t2[64:96, :n], in0=pb[96:128, :n], in1=qtabb[96:128, c0:c0 + n], op=ALU.mult)
                I("dve", "tensor_tensor", [k1, k2], [Qk], out=Qh[64:96, c0:c0 + n], in0=t1[64:96, :n], in1=t2[64:96, :n], op=ALU.add)
            for ci, (c0, n) in enumerate(chunks(T)):
                pk, pb = trring.next()
                for kc in range(2):
                    I("pe", "matmul", [wkk, "cT"], [pk], pb[0:64, :n], wkh[:, kc, :], cT[:, kc, c0:c0 + n], start=(kc == 0), stop=(kc == 1))
                I("dve", "tensor_copy", [pk], [Kk], out=Kh[0:64, c0:c0 + n], in_=pb[0:64, :n])
            for g0 in range(0, 33, 8):
                pk, pb = trring.next()
                nb = min(8, 33 - g0)
                for i in range(nb):
                    kb = g0 + i
                    kk = 128 if kb < 32 else 16
                    for kc in range(2):
                        I("pe", "matmul", [wvk, "cT"], [pk], pb[:kk, i * 64:(i + 1) * 64], cT[:, kc, kb * 128:kb * 128 + kk], wvh[:, kc, :], start=(kc == 0), stop=(kc == 1))
                if nb == 8:
                    I("dve", "tensor_copy", [pk], ["Vh"], out=Vh[:, g0:g0 + 8, 0:64], in_=pb[:, :512].rearrange("p (a b) -> p a b", a=8))
                else:
                    I("dve", "tensor_copy", [pk], ["Vh"], out=Vh[:16, 32, 0:64], in_=pb[:16, 0:64])
            if first:
                for ci, (c0, n) in enumerate(chunks(T)):
                    qk, q = sqr.next()
                    I("act", "activation", [Kk, ("Kb", h % 2, "kr")], [qk], out=q[0:96, :n], in_=Kh[0:96, c0:c0 + n], func=AF.Square)
                    pk, pb = trring.next()
                    I("pe", "matmul", [qk, "onesb"], [pk], pb[:, :n], onesb[0:96, :], q[0:96, :n], start=True, stop=True)
                    I("dve", "tensor_reduce", [pk], [("kmx", ci)], out=kmx[:, ci:ci + 1], in_=pb[:, :n], axis=AX.X, op=ALU.max)
                I("dve", "tensor_reduce", [("kmx", ci) for ci in range(9)], ["kmax2"], out=kmax2[:, h:h + 1], in_=kmx[:, 0:9], axis=AX.X, op=ALU.max)
            for ti, (c0, n) in enumerate(tcs):
                qk, q = sqr.next()
                I("act", "activation", [Qk], [qk], out=q[0:96, :n], in_=Qh[0:96, c0:c0 + n], func=AF.Square)
                pk, pb = trring.next()
                I("pe", "matmul", [qk, "onesb"], [pk], pb[:, :n], onesb[0:96, :], q[0:96, :n], start=True, stop=True)
                I("dve", "tensor_reduce", [pk], [("qmx", ti)], out=qmx[:, ti:ti + 1], in_=pb[:, :n], axis=AX.X, op=ALU.max)
            nk, negm = negr.next()
            I("dve", "tensor_tensor", [("qmx", 0), ("qmx", 1)], [nk], out=negm, in0=qmx[:, 0:1], in1=qmx[:, 1:2], op=ALU.max)
            I("dve", "tensor_tensor", [nk, "kmax2"], [nk], out=negm, in0=negm, in1=kmax2[:, h:h + 1], op=ALU.mult)
            I("act", "activation", [nk], [nk], out=negm, in_=negm, func=AF.Ln)
            I("act", "activation", [nk], [nk], out=negm, in_=negm, func=AF.Exp, scale=0.5)
            I("dve", "tensor_scalar", [nk], [nk], out=negm, in0=negm, scalar1=-SCALE * 1.02, scalar2=None, op0=ALU.mult)
            LAG = 3
            items = []
            for sl in range(2):
                sg = 2 * hf + sl
                seq = list(range(8 * sg + 8)) + [32]
                for idx, kb in enumerate(seq):
                    items.append((sl, sg, idx, kb, len(seq)))
            pos = {}
            info = {}

            def qk_stage(it):
                sl, sg, idx, kb, nseq = it
                if idx == 0:
                    pos[sl] = poring.next()
                kk = 128 if kb < 32 else 16
                q0 = sl * 512
                pk, pb = psring.next()
                I("pe", "matmul", [Kk, ("Kb", h % 2, "kr"), Qk], [pk], pb[:kk, :512], Kh[:, kb * 128:kb * 128 + kk], Qh[:, q0:q0 + 512], start=True, stop=True)
                ptk, pt = Ptr.next()
                I("act", "activation", [pk, nk], [ptk], out=pt[:kk, :], in_=pb[:kk, :512], func=AF.Exp, scale=SCALE, bias=negm[:kk, :])
                if kb < 32 and kb >= 8 * sg:
                    mi = kb - 8 * sg
                    I("dve", "tensor_tensor", [ptk, "maskb"], [ptk], out=pt, in0=pt, in1=maskb[sg % 2][:, mi * 512:(mi + 1) * 512], op=ALU.mult)
                info[it] = (ptk, pt, kk)

            def pv_stage(it):
                sl, sg, idx, kb, nseq = it
                ptk, pt, kk = info.pop(it)
                pok, po = pos[sl]
                q0 = sl * 512
                I("pe", "matmul", [ptk, "Vh"], [pok], po[:, :512], Vh[:kk, kb, :], pt[:kk, :], start=(idx == 0), stop=(idx == nseq - 1))
                if idx == nseq - 1:
                    rsk, rs = rsum.next()
                    tmk, tm = tmpf.next()
                    I("act", "activation", [pok], [rsk], out=rs[hb:hb + 64, :], in_=po[64:128, :512], func=AF.Ln)
                    I("act", "activation", [rsk], [rsk], out=rs[hb:hb + 64, :], in_=rs[hb:hb + 64, :], func=AF.Exp, scale=-1.0)
                    I("dve", "tensor_tensor", [pok, rsk], [tmk], out=tm[hb:hb + 64, :], in0=po[0:64, :512], in1=rs[hb:hb + 64, :], op=ALU.mult)
                    I("dve", "tensor_tensor", [tmk, ("og", pr)], [("og", pr)], out=ogT[hb:hb + 64, pr, q0:q0 + 512], in0=tm[hb:hb + 64, :], in1=ogT[hb:hb + 64, pr, q0:q0 + 512], op=ALU.mult)

            for i in range(len(items) + LAG):
                if i < len(items):
                    qk_stage(items[i])
                if i - LAG >= 0:
                    pv_stage(items[i - LAG])
        G.barrier()
        A.release(mX)
        out_proj_residual(wom, j * 8 * 128, ogT, ogkey, xT, xkey, tcs, NT, V_POST + (2 + j) * 8)
        G.barrier()
        A.release(mH)

    if cfg.do_b:
        for j in range(2):
            for hf in range(2):
                mla_prompt(j, hf)
        mY = A.mark()
        ytr = Ring("ytok", [A.alloc(1024) for _ in range(2)])
        for tt in range(16):
            yk, yb = ytr.next()
            for g0 in range(0, 8, 4):
                pk, pb = trring.next()
                for i in range(4):
                    kc = g0 + i
                    I("pe", "transpose", [("xo", tt // 8, kc), "ident"], [pk], pb[:, i * 128:(i + 1) * 128], x_own[:, kc, tt * 128:(tt + 1) * 128], ident)
                copy_any("act" if g0 == 0 else "dve", [pk], [yk], yb[:, g0 * 128:(g0 + 4) * 128], pb[:, :512])
            I("sp", "dma_start", [yk], [("out", "y", tt)], out=y_own[tt * 128:(tt + 1) * 128, :], in_=yb, slot="oy%d" % yk[1])
        G.barrier()
        A.release(mY)
    A.release(mB)
    G.barrier()
    A.release(mSP)

    def bfv(bank):
        return bank[:, :].bitcast(BF16)

    if cfg.do_s:
        wring = Ring("w", [v3(A.alloc(1024, BF16), 8) for _ in range(4)])
        sqr = Ring("sq", [A.alloc(512, BF16) for _ in range(2)])
        rsr = Ring("rs", [A.alloc(512) for _ in range(2)])
        wqr = Ring("wq", [v3(A.alloc(512, BF16), 4) for _ in range(2)])
        wvr = Ring("wvh", [v3(A.alloc(128, BF16), 2) for _ in range(2)])
        wukT_sb = v3(A.alloc(16 * 256, BF16), 16)
        stab = A.alloc(64)
        smask_sb = A.alloc(NB * 64)
        ptxi = A.alloc(NB * NGRP, I32)
        ptxf = A.alloc(NB * NGRP)
        idxi = A.alloc(NB * NGRP, I32)
        I("pool", "dma_start", [], ["wukT"], out=wukT_sb[0:64].rearrange("p a b -> p (a b)"), in_=wukT, slot="wukT")
        I("sp", "dma_start", [], ["stab"], out=stab, in_=mtab[:, 16:80], slot="stab")
        I("sp", "dma_start", [], ["smask"], out=smask_sb[0:64, :], in_=smask, slot="smask")
        I("sp", "dma_start", [], ["ptxi"], out=ptxi, in_=ptx, slot="ptx")
        I("dve", "tensor_copy", ["ptxi"], ["ptxf"], out=ptxf, in_=ptxi)
        I("dve", "tensor_scalar", ["ptxf", "vec"], ["ptxf"], out=ptxf, in0=ptxf, scalar1=16.0, scalar2=vcol(V_R16), op0=ALU.mult, op1=ALU.add)
        I("dve", "tensor_copy", ["ptxf"], ["idx"], out=idxi, in_=ptxf)
        wuv3 = wuv.rearrange("p (kc f) -> p kc f", kc=2)
        NT = NS
        tcs = [(0, NS)]
        xkey = lambda kc: ("xs", kc)
        ogT = v3(A.alloc(8 * NT, BF16), 8)
        ogkey = lambda kc: ("ogs", kc)
        qln = v3(A.alloc(4 * NT, BF16), 4)
        qlnkey = lambda kc: ("qlns", kc)
        hT = v3(A.alloc(8 * NT, BF16), 8)
        hkey = lambda kc: ("hTs", kc)
        qraw = v3(A.alloc(4 * NT), 4)
        QA = [v3(A.alloc(16 * 64, BF16), 16) for _ in range(3)]
        OL = [v3(A.alloc(16 * 64, BF16), 16) for _ in range(2)]
        qnr = Ring("qn", [A.alloc(64, BF16) for _ in range(2)])
        t1r = Ring("t1s", [A.alloc(64) for _ in range(2)])
        t2r = Ring("t2s", [A.alloc(64) for _ in range(2)])
        gtr = Ring("gt", [A.alloc(2048, BF16) for _ in range(4)])
        gkr = Ring("gk", [A.alloc(256, BF16) for _ in range(4)])
        KTd = [[A.alloc(1024, BF16) for _ in range(3)] for _ in range(2)]
        Pr = Ring("P", [A.alloc(1024, BF16) for _ in range(2)])
        PTr = Ring("PT", [A.alloc(512, BF16) for _ in range(2)])
        Qbr = Ring("Qb", [v3(A.alloc(3 * 64, BF16), 3) for _ in range(2)])
        oacc = A.alloc(256)
        obf = A.alloc(256, BF16)
        Sn = A.alloc(64)
        sm = A.alloc(32)
        print("arena words used in sample phase:", A.top, "of", A.n)
        ktb = [bfv(banks[0]), bfv(banks[1]), bfv(banks[2])]
        ptb = bfv(banks[3])[:, 512:1024]
        pvb = banks[3]
        Sring = Ring("sS", [banks[4], banks[5], banks[6], banks[7]])

        def col(i):
            return sm[0:64, i:i + 1]

        for j in range(2):
            rst = rms_stats(xsT, xkey, 8, tcs, D)
            norm_to(hT, hkey, xsT, xkey, 8, tcs, rst, V_PRE + (2 + j) * 8)
            for oc in range(12):
                wk, w = load_w(wim, (j * 12 + oc) * 128)
                pk, pb = proj(wk, w, 8, hT, hkey, 0, NT)
                if oc < 4:
                    I("act", "activation", [pk], [("qraws", oc)], out=qraw[:, oc, :], in_=pb[:, :NT], func=AF.Copy)
                    qk, q = sqr.next()
                    I("act", "activation", [pk], [qk], out=q[:, :NT], in_=pb[:, :NT], func=AF.Square)
                    I("pe", "matmul", [qk, "onesb"], [("ssq", 0)], ssq_banks[0][:, :NT], onesb, q[:, :NT], start=(oc == 0), stop=(oc == 3))
                else:
                    I("act", "activation", [pk], [("ogs", oc - 4)], out=ogT[:, oc - 4, :], in_=pb[:, :NT], func=AF.Silu)
                if oc == 3:
                    rk, r = rstd_from(ssq_banks[0], ("ssq", 0), NT, 512)
                    for kc in range(4):
                        I("dve", "scalar_tensor_tensor", [("qraws", kc), rk, "vec"], [("qlns", kc)],
                          out=qln[:, kc, :], in0=qraw[:, kc, :], scalar=vcol(V_QN + j * 4 + kc), in1=r[:, :NT], op0=ALU.mult, op1=ALU.mult)
            for h in range(16):
                wqk, wq = wqr.next()
                I("pool", "dma_start", [], [wqk], out=wq.rearrange("p a b -> p (a b)"), in_=wuq[(j * 16 + h) * 128:(j * 16 + h + 1) * 128, :], slot="wq%d" % wqk[1])
                pk, pb = proj(wqk, wq, 4, qln, qlnkey, 0, NT, ring=trring)
                qnk, qn = qnr.next()
                I("act", "activation", [pk], [qnk], out=qn[0:64, :], in_=pb[0:64, :NT], func=AF.Copy)
                k1, t1 = t1r.next()
                k2, t2 = t2r.next()
                I("dve", "tensor_tensor", [pk, "stab"], [k1], out=t1[64:96, :], in0=pb[64:96, :NT], in1=stab[64:96, :], op=ALU.mult)
                I("dve", "tensor_tensor", [pk, "stab"], [k2], out=t2[64:96, :], in0=pb[96:128, :NT], in1=stab[96:128, :], op=ALU.mult)
                I("dve", "tensor_tensor", [k1, k2], [("QA", 2)], out=QA[2][0:32, h, :], in0=t1[64:96, :], in1=t2[64:96, :], op=ALU.add)
                for c2 in range(2):
                    pk2, pb2 = psring.next()
                    I("pe", "matmul", [qnk, "wukT"], [pk2], pb2[:, :NT], wukT_sb[0:64, h, c2 * 128:(c2 + 1) * 128], qn[0:64, :], start=True, stop=True)
                    copy_any("act" if c2 == 0 else "dve", [pk2], [("QA", c2)], QA[c2][:, h, :], pb2[:, :NT])
            G.barrier()
            items = [(b, g) for b in range(NB) for g in range(NGRP + 1)]
            st = {}

            def lcol(b):
                return col(10 + b % 2)

            def stA(it):
                b, g = it
                d = st.setdefault(it, {})
                if g == NGRP:
                    return
                gk_, gt = gtr.next()
                kk_, gkk = gkr.next()
                icol = idxi[:, b * NGRP + g:b * NGRP + g + 1]
                I("pool", "indirect_dma_start", ["idx"], [gk_], out=gt, out_offset=None, in_=cckv,
                  in_offset=bass.IndirectOffsetOnAxis(ap=icol, axis=0), bounds_check=cfg.n_pool * 16 - 1, oob_is_err=False, slot="gt%d" % gk_[1])
                I("pool", "indirect_dma_start", ["idx"], [kk_], out=gkk, out_offset=None, in_=ckr,
                  in_offset=bass.IndirectOffsetOnAxis(ap=icol, axis=0), bounds_check=cfg.n_pool * 16 - 1, oob_is_err=False, slot="gk%d" % kk_[1])
                kd = d["kd"] = (b * (NGRP + 1) + g) % 2
                KTb = d["KT"] = KTd[kd]
                for jj in range(8):
                    I("pe", "transpose", [gk_, "identb"], [("ps", 0)], ktb[0][:, jj * 128:(jj + 1) * 128], gt[:, jj * 256:jj * 256 + 128], identb)
                    I("pe", "transpose", [gk_, "identb"], [("ps", 1)], ktb[1][:, jj * 128:(jj + 1) * 128], gt[:, jj * 256 + 128:jj * 256 + 256], identb)
                    I("pe", "transpose", [kk_, "identb"], [("ps", 2)], ktb[2][0:32, jj * 128:(jj + 1) * 128], gkk[:, jj * 32:(jj + 1) * 32], identb)
                I("dve", "tensor_copy", [("ps", 0)], [("KT", kd, 0)], out=KTb[0], in_=ktb[0])
                I("act", "activation", [("ps", 1)], [("KT", kd, 1)], out=KTb[1], in_=ktb[1], func=AF.Copy)
                I("dve", "tensor_copy", [("ps", 2)], [("KT", kd, 2)], out=KTb[2][0:32, :], in_=ktb[2][0:32, :])
                d["gk_"] = gk_
                d["gt"] = gt

            def stB(it):
                b, g = it
                d = st[it]
                newk = (g == NGRP)
                if g == 0:
                    Qbk, Qb = Qbr.next()
                    for c in range(3):
                        np_ = 128 if c < 2 else 32
                        I("dve", "tensor_copy", [("QA", c)], [Qbk], out=Qb[:np_, c, :].rearrange("p (h t) -> p h t", t=LS), in_=QA[c][:np_, :, b * LS:(b + 1) * LS])
                    I("dve", "memset", [], ["m_old"], col(0), NEG)
                    I("dve", "memset", [], [("l", b % 2)], lcol(b), 0.0)
                    st[("Qb", b)] = (Qbk, Qb)
                Qbk, Qb = st[("Qb", b)]
                if not newk:
                    kd = d["kd"]
                    KTb = d["KT"]
                    Ssrc = []
                    for hh in range(2):
                        sk, sb = Sring.next()
                        I("pe", "matmul", [Qbk, ("KT", kd, 0)], [sk], sb[0:64, :512], Qb[:, 0, :], KTb[0][:, hh * 512:(hh + 1) * 512], start=True, stop=False)
                        I("pe", "matmul", [Qbk, ("KT", kd, 1)], [sk], sb[0:64, :512], Qb[:, 1, :], KTb[1][:, hh * 512:(hh + 1) * 512], start=False, stop=False)
                        I("pe", "matmul", [Qbk, ("KT", kd, 2)], [sk], sb[0:64, :512], Qb[0:32, 2, :], KTb[2][0:32, hh * 512:(hh + 1) * 512], start=False, stop=True)
                        I("dve", "tensor_reduce", [sk], [("gm", hh)], out=col(5 + hh), in_=sb[0:64, :512], axis=AX.X, op=ALU.max)
                        Ssrc.append((sk, sb[0:64, :512], 512))
                else:
                    sk, sb = Sring.next()
                    I("pe", "matmul", [Qbk, "cTs"], [sk], sb[0:64, :64], Qb[:, 0, :], cTs[:, 0, :], start=True, stop=False)
                    I("pe", "matmul", [Qbk, "cTs"], [sk], sb[0:64, :64], Qb[:, 1, :], cTs[:, 1, :], start=False, stop=False)
                    I("pe", "matmul", [Qbk, "krTs"], [sk], sb[0:64, :64], Qb[0:32, 2, :], krTs[0:32, :], start=False, stop=True)
                    I("dve", "tensor_tensor", [sk, "smask"], ["Sn"], out=Sn[0:64, :], in0=sb[0:64, :64], in1=smask_sb[0:64, b * 64:(b + 1) * 64], op=ALU.add)
                    I("dve", "tensor_reduce", ["Sn"], [("gm", 0)], out=col(5), in_=Sn[0:64, :], axis=AX.X, op=ALU.max)
                    I("dve", "tensor_copy", [("gm", 0)], [("gm", 1)], out=col(6), in_=col(5))
                    Ssrc = [("Sn", Sn[0:64, :], 64)]
                ai = (b * (NGRP + 1) + g) % 4
                alk = ("alpha", ai)
                al = col(12 + ai)
                I("dve", "tensor_tensor", [("gm", 0), ("gm", 1)], ["m_new"], out=col(1), in0=col(5), in1=col(6), op=ALU.max)
                I("dve", "tensor_tensor", ["m_new", "m_old"], ["m_new"], out=col(1), in0=col(1), in1=col(0), op=ALU.max)
                I("dve", "tensor_scalar", ["m_new"], ["negb"], out=col(2), in0=col(1), scalar1=-SCALE, scalar2=None, op0=ALU.mult)
                I("act", "activation", ["m_old", "negb"], [alk], out=al, in_=col(0), func=AF.Exp, scale=SCALE, bias=col(2))
                I("dve", "memset", [], ["rs"], sm[0:64, 7:9], 0.0)
                Pk, P = Pr.next()
                off = 0
                for hi, (sk, sap, w_) in enumerate(Ssrc):
                    I("act", "activation", [sk, "negb", "rs"], [Pk, "rs"], out=P[0:64, off:off + w_], in_=sap, func=AF.Exp, scale=SCALE, bias=col(2), accum_out=col(7 + hi))
                    off += w_
                I("dve", "scalar_tensor_tensor", [("l", b % 2), alk, "rs"], [("l", b % 2)], out=lcol(b), in0=lcol(b), scalar=al, in1=col(7), op0=ALU.mult, op1=ALU.add)
                if not newk:
                    I("dve", "tensor_tensor", [("l", b % 2), "rs"], [("l", b % 2)], out=lcol(b), in0=lcol(b), in1=col(8), op=ALU.add)
                I("dve", "tensor_copy", ["m_new"], ["m_old"], out=col(0), in_=col(1))
                d["Pk"] = Pk
                d["P"] = P
                d["alk"] = alk
                d["al"] = al

            def stC(it):
                b, g = it
                d = st.pop(it)
                newk = (g == NGRP)
                if g == 0:
                    I("dve", "memset", [], ["oacc"], oacc[0:64, :], 0.0)
                nblk = 8 if not newk else 1
                bw = 128 if not newk else 64
                Pk, P = d["Pk"], d["P"]
                PTk, PT = PTr.next()
                for jj in range(nblk):
                    I("pe", "transpose", [Pk, "identb"], [("pt", 0)], ptb[:bw, jj * 64:(jj + 1) * 64], P[0:64, jj * bw:(jj + 1) * bw], identb[0:64, 0:64])
                I("dve", "tensor_copy", [("pt", 0)], [PTk], out=PT[:bw, :nblk * 64], in_=ptb[:bw, :nblk * 64])
                for jj in range(nblk):
                    if not newk:
                        I("pe", "matmul", [PTk, d["gk_"]], [("pt", 0)], pvb[0:64, :256], PT[:, jj * 64:(jj + 1) * 64], d["gt"][:, jj * 256:(jj + 1) * 256], start=(jj == 0), stop=(jj == nblk - 1))
                    else:
                        I("pe", "matmul", [PTk, "cns_tok"], [("pt", 0)], pvb[0:64, :256], PT[0:64, 0:64], cns_tok[0:64, :], start=True, stop=True)
                I("dve", "scalar_tensor_tensor", [("pt", 0), d["alk"], "oacc"], ["oacc"], out=oacc[0:64, :], in0=oacc[0:64, :], scalar=d["al"], in1=pvb[0:64, :256], op0=ALU.mult, op1=ALU.add)
                if newk:
                    I("act", "activation", [("l", b % 2)], ["rl"], out=col(9), in_=lcol(b), func=AF.Ln)
                    I("act", "activation", ["rl"], ["rl"], out=col(9), in_=col(9), func=AF.Exp, scale=-1.0)
                    I("dve", "tensor_scalar", ["oacc", "rl"], ["obf"], out=obf[0:64, :], in0=oacc[0:64, :], scalar1=col(9), scalar2=None, op0=ALU.mult)
                    for c2 in range(2):
                        I("pe", "transpose", ["obf", "identb"], [("pt", 0)], ptb[:, c2 * 64:(c2 + 1) * 64], obf[0:64, c2 * 128:(c2 + 1) * 128], identb[0:64, 0:64])
                    for c2 in range(2):
                        I("dve", "tensor_copy", [("pt", 0)], [("OL", c2)], out=OL[c2][:, :, b * LS:(b + 1) * LS], in_=ptb[:, c2 * 64:(c2 + 1) * 64].rearrange("p (h t) -> p h t", t=LS))

            for i in range(len(items) + 2):
                if i < len(items):
                    stA(items[i])
                if 0 <= i - 2 < len(items):
                    stC(items[i - 2])
                if 0 <= i - 1 < len(items):
                    stB(items[i - 1])
            G.barrier()
            for h in range(16):
                hb = (h % 2) * 64
                pr = h // 2
                wvk, wvh = wvr.next()
                I("pool", "dma_start", [], [wvk], out=wvh, in_=wuv3[:, :, h * 64:(h + 1) * 64], slot="wvh%d" % wvk[1])
                pk, pb = trring.next()
                for c2 in range(2):
                    I("pe", "matmul", [wvk, ("OL", c2)], [pk], pb[0:64, :NT], wvh[:, c2, :], OL[c2][:, h, :], start=(c2 == 0), stop=(c2 == 1))
                I("dve", "tensor_tensor", [pk, ("ogs", pr)], [("ogs", pr)], out=ogT[hb:hb + 64, pr, :], in0=pb[0:64, :NT], in1=ogT[hb:hb + 64, pr, :], op=ALU.mult)
            G.barrier()
            mO2 = A.mark()
            out_proj_residual(wom, j * 8 * 128, ogT, ogkey, xsT, xkey, tcs, NT, V_POST + (2 + j) * 8)
            G.barrier()
            A.release(mO2)
        ysb = A.alloc(1024)
        for g0 in range(0, 8, 4):
            pk, pb = trring.next()
            for i in range(4):
                I("pe", "transpose", [("xs", g0 + i), "ident"], [pk], pb[:NS, i * 128:(i + 1) * 128], xsT[:, g0 + i, :], ident)
            copy_any("act" if g0 == 0 else "dve", [pk], ["ysb"], ysb[:NS, g0 * 128:(g0 + 4) * 128], pb[:NS, :512])
        I("sp", "dma_start", ["ysb"], [("out", "ys")], out=ys, in_=ysb[:NS, :], slot="oys")

    G.barrier()
    G.add("sp", lambda e: None)
    print('total ops', G.n, {e: len(v) for e, v in G.ops.items()})
    G.emit(nc, stack, getattr(cfg, 'limit', None))
    stack.close()
    return nc


OWN_CHUNKS = {0: (0, 3, 4, 7), 1: (1, 2, 5, 6)}


def _chunked_w(w, ncols_chunk=128):
    K, N = w.shape
    a = w.reshape(K // 128, 128, N // 128, 128)
    a = a.transpose(2, 1, 0, 3)
    return np.ascontiguousarray(a).reshape(N // 128 * 128, K // 128 * 128)


def _rope_tab(pos):
    half = 16
    inv = (10000.0 ** (-np.arange(half, dtype=np.float32) / half)).astype(np.float32)
    ang = pos.astype(np.float32)[None, :] * inv[:, None]
    cos = np.cos(ang).astype(np.float32)
    sin = np.sin(ang).astype(np.float32)
    tab = np.zeros((128, len(pos)), np.float32)
    tab[64:80] = cos
    tab[80:96] = cos
    tab[96:112] = -sin
    tab[112:128] = sin
    return tab


def prep_inputs(cfg, x_prompt, x_sample, cache_ckv, cache_krope, state_conv, page_table, meta_tokens,
                pre_norm_g, post_norm_g, w_in_conv, conv_w, w_out_conv, kv_norm_g, w_dkv,
                kv_lat_norm_g, w_uk, w_uv, w_in_mla, q_norm_g, w_uq, w_out_mla):
    f = np.float32
    shared = {}
    shared["meta"] = np.ascontiguousarray(meta_tokens, f)
    shared["ident"] = np.eye(128, dtype=f)
    shared["wic"] = np.concatenate([_chunked_w(np.asarray(w_in_conv[l])) for l in range(2)], 0)
    shared["woc"] = np.concatenate([_chunked_w(np.asarray(w_out_conv[l])) for l in range(2)], 0)
    wd = np.asarray(w_dkv)
    wkr = wd[:, 256:288]
    wd2 = np.concatenate([np.zeros((1024, 64), f), wkr, wkr[:, 16:32], wkr[:, 0:16]], 1)
    shared["wdkv"] = np.concatenate([_chunked_w(wd[:, 0:128]), _chunked_w(wd[:, 128:256]), _chunked_w(wd2)], 0)
    shared["wim"] = np.concatenate([_chunked_w(np.asarray(w_in_mla[j])) for j in range(2)], 0)
    uq = []
    for j in range(2):
        w = np.asarray(w_uq[j]).reshape(512, 16, 96)
        wh = np.concatenate([w[:, :, 0:64], w[:, :, 64:96], w[:, :, 80:96], w[:, :, 64:80]], 2)
        for h in range(16):
            uq.append(_chunked_w(np.ascontiguousarray(wh[:, h, :])))
    shared["wuq"] = np.concatenate(uq, 0)
    shared["wom"] = np.concatenate([_chunked_w(np.asarray(w_out_mla[j])) for j in range(2)], 0)
    shared["wuk"] = np.ascontiguousarray(np.asarray(w_uk).reshape(2, 128, 1024).transpose(1, 0, 2)).reshape(128, 2048)
    shared["wuv"] = np.ascontiguousarray(np.asarray(w_uv).reshape(2, 128, 1024).transpose(1, 0, 2)).reshape(128, 2048)
    shared["wukT"] = np.ascontiguousarray(np.asarray(w_uk).transpose(2, 1, 0)).reshape(64, 16 * 256)
    kpos = np.concatenate([np.arange(16, T), np.arange(0, 16)])
    shared["ktab"] = _rope_tab(kpos)
    npg = cfg.n_pages
    past = npg * PAGE
    spos = np.tile(past + np.arange(LS), NB)
    shared["mtab"] = np.concatenate([_rope_tab(np.arange(16)), _rope_tab(spos)], 1)
    shared["cckv"] = np.asarray(cache_ckv).reshape(cfg.n_pool * 16, 2048)
    shared["ckr"] = np.asarray(cache_krope).reshape(cfg.n_pool * 16, 256)
    sm = np.full((64, NB, NB, LS), NEG, f)
    for b in range(NB):
        for t in range(LS):
            sm[np.arange(16) * 4 + t, b, b, 0:t + 1] = 0.0
    shared["smask"] = sm.reshape(64, NB * 64)

    def fm(v):
        return np.asarray(v, f).reshape(-1, 128).T

    vec = np.zeros((128, NV), f)
    for l in range(4):
        vec[:, V_PRE + l * 8:V_PRE + l * 8 + 8] = fm(pre_norm_g[l])
        vec[:, V_POST + l * 8:V_POST + l * 8 + 8] = fm(post_norm_g[l])
    vec[:, V_KVN:V_KVN + 8] = fm(kv_norm_g)
    vec[:, V_LAT:V_LAT + 2] = fm(kv_lat_norm_g)
    for j in range(2):
        vec[:, V_QN + j * 4:V_QN + j * 4 + 4] = fm(q_norm_g[j])
    vec[:, V_R16] = np.arange(128) % 16
    vec[:, V_EPS] = EPS
    for l in range(2):
        for k in range(3):
            vec[:, V_CW + l * 24 + k * 8:V_CW + l * 24 + k * 8 + 8] = fm(conv_w[l, k])

    ki = np.arange(128)[:, None]
    qi = np.arange(512)[None, :]
    pat = np.zeros((2, 128, 8, 512), f)
    for d in range(4):
        tri = ((d * 128 + ki) <= qi).astype(f)
        pat[0, :, d, :] = tri
        pat[1, :, d, :] = 1.0
        pat[1, :, 4 + d, :] = tri

    in_maps = []
    xp_all = np.asarray(x_prompt)
    xs_all = np.asarray(x_sample)
    sc_all = np.asarray(state_conv)
    pt_all = np.asarray(page_table)
    for c in range(NCORES):
        b, r = c // 2, c % 2
        m = dict(shared)
        m["xp"] = np.ascontiguousarray(xp_all[b])
        m["xs"] = np.ascontiguousarray(xs_all[c * NB:(c + 1) * NB].reshape(NS, D))
        m["sconv"] = np.ascontiguousarray(sc_all[:, c * NB:(c + 1) * NB].reshape(64, D))
        v = vec.copy()
        own = OWN_CHUNKS[r]
        for s in range(4):
            v[:, V_BLEND + 2 * s] = 1.0 if own[s] == 2 * s else 0.0
            v[:, V_BLEND + 2 * s + 1] = 1.0 if own[s] == 2 * s + 1 else 0.0
        m["vecs"] = v
        qpos = np.concatenate([16 + ch * 512 + np.arange(512) for ch in own])
        m["qtab"] = _rope_tab(qpos)
        mk = np.stack([pat[0 if own[par] == 2 * par else 1] for par in range(2)], 0)
        m["masks"] = np.ascontiguousarray(mk).reshape(2 * 128, 4096)
        pt = pt_all[c * NB:(c + 1) * NB]
        e = pt.reshape(NB, npg // 8, 8)
        e = np.repeat(e[:, :, :, None], 16, 3)
        m["ptx"] = np.ascontiguousarray(e.transpose(2, 3, 0, 1)).reshape(128, NB * (npg // 8)).astype(np.int32)
        in_maps.append(m)
    return in_maps


def assemble(res, cfg):
    y_prompt = np.zeros((4, SEQ, D), np.float32)
    y_sample = np.zeros((128, LS, D), np.float32)
    ckv_p = np.zeros((4, T, 256), np.float32)
    kr_p = np.zeros((4, T, 32), np.float32)
    conv_p = np.zeros((2, 4, 2, D), np.float32)
    ckv_s = np.zeros((128, LS, 256), np.float32)
    kr_s = np.zeros((128, LS, 32), np.float32)
    conv_s = np.zeros((2, 128, 2, D), np.float32)
    for c in range(NCORES):
        b, r = c // 2, c % 2
        o = res[c]
        for s, ch in enumerate(OWN_CHUNKS[r]):
            y_prompt[b, ch * 512:(ch + 1) * 512] = o["y_own"][s * 512:(s + 1) * 512]
        y_sample[c * NB:(c + 1) * NB] = o["ys"].reshape(NB, LS, D)
        if r == 0:
            ckv_p[b] = o["ckv_p"]
            kr_p[b] = o["kr_p"]
            conv_p[:, b] = o["conv_p"].reshape(2, 2, D)
        ckv_s[c * NB:(c + 1) * NB] = o["ckv_s"].reshape(NB, LS, 256)
        kr_s[c * NB:(c + 1) * NB] = o["kr_s"].reshape(NB, LS, 32)
        conv_s[:, c * NB:(c + 1) * NB] = o["conv_s"].reshape(2, NB, 2, D)
    return (y_prompt, y_sample, ckv_p, kr_p, conv_p, ckv_s, kr_s, conv_s)


_NC_CACHE = {}


def kernel(**inputs):
    cfg = Cfg(n_pool=inputs["cache_ckv"].shape[0], n_pages=inputs["page_table"].shape[1])
    key = (cfg.n_pool, cfg.n_pages)
    if key not in _NC_CACHE:
        _NC_CACHE[key] = build(cfg)
    nc = _NC_CACHE[key]
    in_maps = prep_inputs(cfg, **inputs)
    res = run_bass_kernel_spmd(nc, in_maps, core_ids=list(range(NCORES)))
    return assemble(res.results, cfg)
```
